# Optimizing a Trainium2 kernel written in Bass

```python
import jax
import jax.numpy as jnp
from jax import lax
import numpy as np

D_MODEL = 1024
BATCH = 2
SEQ = 8192
DEPTH = 1
DEC_BATCH = 128
DEC_SEQ = 1
PAST_LEN = 16384
PAGE_SIZE = 128

CONV_DIM = D_MODEL // 2
CONV_WIDTH = 3
N_HEADS = 8
QK_NOPE_DIM = D_MODEL // 16
QK_ROPE_DIM = D_MODEL // 32
QK_HEAD_DIM = QK_NOPE_DIM + QK_ROPE_DIM
V_HEAD_DIM = D_MODEL // 16
Q_LORA_RANK = 3 * D_MODEL // 8
KV_LORA_RANK = D_MODEL // 4
ROPE_BASE = 10000.0
N_MEM = 256
CA_HEADS = 4
CA_HEAD_DIM = D_MODEL // CA_HEADS
D_FF = 4 * D_MODEL
Q_BLOCK = 128
RMS_EPS = 1e-6
NEG_INF = -1e30
IN_SIZES = (CONV_DIM, CONV_DIM, CONV_DIM, Q_LORA_RANK, KV_LORA_RANK, QK_ROPE_DIM, D_MODEL, D_MODEL)
IN_COLS = sum(IN_SIZES)

kernel_name = 'hybrid_conv_mla_mem_decoder_step'


def rms_norm(x, g):
    xf = x.astype(jnp.float32)
    y = xf * lax.rsqrt(jnp.mean(xf * xf, axis=-1, keepdims=True) + RMS_EPS)
    return (y * g.astype(jnp.float32)).astype(x.dtype)


def rope_tables(pos):
    inv = 1.0 / (ROPE_BASE ** (jnp.arange(0, QK_ROPE_DIM, 2, dtype=jnp.float32) / QK_ROPE_DIM))
    ang = pos.astype(jnp.float32)[:, None] * inv[None, :]
    return jnp.cos(ang), jnp.sin(ang)


def apply_rope(x, cos, sin):
    x1, x2 = jnp.split(x, 2, axis=-1)
    c = cos.astype(x.dtype)
    s = sin.astype(x.dtype)
    return jnp.concatenate([x1 * c - x2 * s, x1 * s + x2 * c], axis=-1)


def short_conv(u_ext, w):
    t = u_ext.shape[1] - (CONV_WIDTH - 1)
    y = u_ext[:, 0:t] * w[0]
    for k in range(1, CONV_WIDTH):
        y = y + u_ext[:, k:k + t] * w[k]
    return y


def mla_prompt_attention(q_nope, q_rope, ckv, kr, w_uk, w_uv):
    b, s_len = ckv.shape[0], ckv.shape[1]
    k_nope = jnp.einsum('bsl,lhd->bshd', ckv, w_uk)
    v = jnp.einsum('bsl,lhd->bshd', ckv, w_uv)
    scale = QK_HEAD_DIM ** -0.5
    key_pos = jnp.arange(s_len)

    def block(i):
        start = i * Q_BLOCK
        qn = lax.dynamic_slice_in_dim(q_nope, start, Q_BLOCK, axis=1)
        qr = lax.dynamic_slice_in_dim(q_rope, start, Q_BLOCK, axis=1)
        s = jnp.einsum('bqhd,bkhd->bhqk', qn, k_nope) + jnp.einsum('bqhr,bkr->bhqk', qr, kr)
        s = s.astype(jnp.float32) * scale
        q_pos = start + jnp.arange(Q_BLOCK)
        s = jnp.where(key_pos[None, :] <= q_pos[:, None], s, NEG_INF)
        p = jax.nn.softmax(s, axis=-1).astype(v.dtype)
        return jnp.einsum('bhqk,bkhd->bqhd', p, v)

    out = lax.map(block, jnp.arange(s_len // Q_BLOCK))
    return out.transpose(1, 0, 2, 3, 4).reshape(b, s_len, N_HEADS, V_HEAD_DIM)


def mla_sample_attention(q_nope, q_rope, ckv_new, kr_new, ckv_past, kr_past, w_uk, w_uv):
    t = ckv_new.shape[1]
    p_len = ckv_past.shape[1]
    scale = QK_HEAD_DIM ** -0.5
    q_lat = jnp.einsum('bthd,lhd->bthl', q_nope, w_uk)
    s_past = jnp.einsum('bthl,bpl->bhtp', q_lat, ckv_past) + jnp.einsum('bthr,bpr->bhtp', q_rope, kr_past)
    s_new = jnp.einsum('bthl,bul->bhtu', q_lat, ckv_new) + jnp.einsum('bthr,bur->bhtu', q_rope, kr_new)
    s_past = s_past.astype(jnp.float32) * scale
    causal = jnp.arange(t)[:, None] >= jnp.arange(t)[None, :]
    s_new = jnp.where(causal, s_new.astype(jnp.float32) * scale, NEG_INF)
    p = jax.nn.softmax(jnp.concatenate([s_past, s_new], axis=-1), axis=-1).astype(ckv_new.dtype)
    o_lat = jnp.einsum('bhtp,bpl->bthl', p[..., :p_len], ckv_past) + jnp.einsum('bhtu,bul->bthl', p[..., p_len:], ckv_new)
    return jnp.einsum('bthl,lhd->bthd', o_lat, w_uv)


def memory_kv(mem, mem_norm_g, w_ca_k, w_ca_v):
    mn = rms_norm(mem, mem_norm_g)
    k = jnp.einsum('bmd,dhe->bmhe', mn, w_ca_k)
    v = jnp.einsum('bmd,dhe->bmhe', mn, w_ca_v)
    return k, v


def cross_attend(xn, mem_k, mem_v, w_ca_q, w_ca_o):
    q = jnp.einsum('btd,dhe->bthe', xn, w_ca_q)
    s = jnp.einsum('bthe,bmhe->bhtm', q, mem_k).astype(jnp.float32) * (CA_HEAD_DIM ** -0.5)
    p = jax.nn.softmax(s, axis=-1).astype(mem_v.dtype)
    o = jnp.einsum('bhtm,bmhe->bthe', p, mem_v)
    return jnp.einsum('bthe,hed->btd', o, w_ca_o)


def mixer_block(xn, u_prev, past, cos, sin, w_in, conv_w, w_conv_out, q_norm_g, w_uq,
                kv_norm_g, w_uk, w_uv, w_mla_out, w_mix_out):
    b, t = xn.shape[0], xn.shape[1]
    split_pts = np.cumsum(np.array(IN_SIZES))[:-1].tolist()
    h, gate_b, gate_c, cq, ckv, kr, g_conv, g_mla = jnp.split(xn @ w_in, split_pts, axis=-1)
    u = gate_c * h
    u_ext = jnp.concatenate([u_prev, u], axis=1)
    y_conv = (gate_b * short_conv(u_ext, conv_w)) @ w_conv_out
    cq = rms_norm(cq, q_norm_g)
    q = jnp.einsum('btr,rhd->bthd', cq, w_uq)
    q_nope = q[..., :QK_NOPE_DIM]
    q_rope = apply_rope(q[..., QK_NOPE_DIM:], cos[:, None, :], sin[:, None, :])
    ckv = rms_norm(ckv, kv_norm_g)
    kr = apply_rope(kr, cos, sin)
    if past is None:
        o = mla_prompt_attention(q_nope, q_rope, ckv, kr, w_uk, w_uv)
    else:
        o = mla_sample_attention(q_nope, q_rope, ckv, kr, past[0], past[1], w_uk, w_uv)
    y_mla = o.reshape(b, t, N_HEADS * V_HEAD_DIM) @ w_mla_out
    merged = jax.nn.sigmoid(g_conv) * y_conv + jax.nn.sigmoid(g_mla) * y_mla
    return merged @ w_mix_out, ckv, kr, u_ext[:, -(CONV_WIDTH - 1):]


def decoder_layer(x, u_prev, past, mem_k, mem_v, cos, sin,
                  norm_mix_pre_g, w_in, conv_w, w_conv_out, q_norm_g, w_uq, kv_norm_g, w_uk, w_uv,
                  w_mla_out, w_mix_out, norm_mix_post_g, norm_ca_pre_g, w_ca_q, w_ca_o, norm_ca_post_g,
                  norm_mlp_pre_g, w_ff_up, w_ff_down, norm_mlp_post_g):
    y, ckv, kr, u_last = mixer_block(rms_norm(x, norm_mix_pre_g), u_prev, past, cos, sin, w_in, conv_w,
                                     w_conv_out, q_norm_g, w_uq, kv_norm_g, w_uk, w_uv, w_mla_out, w_mix_out)
    x = x + rms_norm(y, norm_mix_post_g)
    x = x + rms_norm(cross_attend(rms_norm(x, norm_ca_pre_g), mem_k, mem_v, w_ca_q, w_ca_o), norm_ca_post_g)
    hid = jnp.square(jax.nn.relu(rms_norm(x, norm_mlp_pre_g) @ w_ff_up))
    x = x + rms_norm(hid @ w_ff_down, norm_mlp_post_g)
    return x, ckv, kr, u_last


def setup_inputs(seed: int = 0) -> dict:
    key = jax.random.key(seed)
    ks = iter(jax.random.split(key, 48))
    f32 = jnp.float32

    def nrm(shape, scale=1.0):
        return scale * jax.random.normal(next(ks), shape, f32)

    def gain(n):
        return 1.0 + 0.02 * nrm((DEPTH, n))

    n_pages = PAST_LEN // PAGE_SIZE
    in_use = DEC_BATCH * n_pages
    n_pool = in_use + max(1, in_use // 4)
    page_table = jax.random.permutation(next(ks), n_pool)[:in_use].reshape(DEC_BATCH, n_pages).astype(jnp.int32)
    return {
        'x_prompt': nrm((BATCH, SEQ, D_MODEL)),
        'x_sample': nrm((DEC_BATCH, DEC_SEQ, D_MODEL)),
        'mem_prompt': nrm((BATCH, N_MEM, D_MODEL)),
        'cache_ckv': nrm((DEPTH, n_pool, PAGE_SIZE, KV_LORA_RANK)),
        'cache_krope': nrm((DEPTH, n_pool, PAGE_SIZE, QK_ROPE_DIM)),
        'state_conv': nrm((DEPTH, DEC_BATCH, CONV_WIDTH - 1, CONV_DIM)),
        'cache_mem_k': nrm((DEPTH, DEC_BATCH, N_MEM, CA_HEADS, CA_HEAD_DIM)),
        'cache_mem_v': nrm((DEPTH, DEC_BATCH, N_MEM, CA_HEADS, CA_HEAD_DIM)),
        'page_table': page_table,
        'norm_mix_pre_g': gain(D_MODEL),
        'w_in': nrm((DEPTH, D_MODEL, IN_COLS), D_MODEL ** -0.5),
        'conv_w': nrm((DEPTH, CONV_WIDTH, CONV_DIM), CONV_WIDTH ** -0.5),
        'w_conv_out': nrm((DEPTH, CONV_DIM, D_MODEL), CONV_DIM ** -0.5),
        'q_norm_g': gain(Q_LORA_RANK),
        'w_uq': nrm((DEPTH, Q_LORA_RANK, N_HEADS, QK_HEAD_DIM), Q_LORA_RANK ** -0.5),
        'kv_norm_g': gain(KV_LORA_RANK),
        'w_uk': nrm((DEPTH, KV_LORA_RANK, N_HEADS, QK_NOPE_DIM), KV_LORA_RANK ** -0.5),
        'w_uv': nrm((DEPTH, KV_LORA_RANK, N_HEADS, V_HEAD_DIM), KV_LORA_RANK ** -0.5),
        'w_mla_out': nrm((DEPTH, N_HEADS * V_HEAD_DIM, D_MODEL), (N_HEADS * V_HEAD_DIM) ** -0.5),
        'w_mix_out': nrm((DEPTH, D_MODEL, D_MODEL), D_MODEL ** -0.5),
        'norm_mix_post_g': gain(D_MODEL),
        'norm_ca_pre_g': gain(D_MODEL),
        'mem_norm_g': gain(D_MODEL),
        'w_ca_q': nrm((DEPTH, D_MODEL, CA_HEADS, CA_HEAD_DIM), D_MODEL ** -0.5),
        'w_ca_k': nrm((DEPTH, D_MODEL, CA_HEADS, CA_HEAD_DIM), D_MODEL ** -0.5),
        'w_ca_v': nrm((DEPTH, D_MODEL, CA_HEADS, CA_HEAD_DIM), D_MODEL ** -0.5),
        'w_ca_o': nrm((DEPTH, CA_HEADS, CA_HEAD_DIM, D_MODEL), (CA_HEADS * CA_HEAD_DIM) ** -0.5),
        'norm_ca_post_g': gain(D_MODEL),
        'norm_mlp_pre_g': gain(D_MODEL),
        'w_ff_up': nrm((DEPTH, D_MODEL, D_FF), D_MODEL ** -0.5),
        'w_ff_down': nrm((DEPTH, D_FF, D_MODEL), D_FF ** -0.5),
        'norm_mlp_post_g': gain(D_MODEL),
    }


def reference(x_prompt, x_sample, mem_prompt, cache_ckv, cache_krope, state_conv, cache_mem_k, cache_mem_v,
              page_table, norm_mix_pre_g, w_in, conv_w, w_conv_out, q_norm_g, w_uq, kv_norm_g, w_uk, w_uv,
              w_mla_out, w_mix_out, norm_mix_post_g, norm_ca_pre_g, mem_norm_g, w_ca_q, w_ca_k, w_ca_v, w_ca_o,
              norm_ca_post_g, norm_mlp_pre_g, w_ff_up, w_ff_down, norm_mlp_post_g):
    seq = x_prompt.shape[1]
    dec_b, t_s = x_sample.shape[0], x_sample.shape[1]
    past_len = page_table.shape[1] * cache_ckv.shape[2]
    cos_p, sin_p = rope_tables(jnp.arange(seq))
    cos_s, sin_s = rope_tables(past_len + jnp.arange(t_s))
    xp, xs = x_prompt, x_sample
    ckv_p_l, kr_p_l, conv_p_l, mk_p_l, mv_p_l = [], [], [], [], []
    ckv_s_l, kr_s_l, conv_s_l = [], [], []
    for l in range(DEPTH):
        lw = (norm_mix_pre_g[l], w_in[l], conv_w[l], w_conv_out[l], q_norm_g[l], w_uq[l], kv_norm_g[l],
              w_uk[l], w_uv[l], w_mla_out[l], w_mix_out[l], norm_mix_post_g[l], norm_ca_pre_g[l], w_ca_q[l],
              w_ca_o[l], norm_ca_post_g[l], norm_mlp_pre_g[l], w_ff_up[l], w_ff_down[l], norm_mlp_post_g[l])
        mk_p, mv_p = memory_kv(mem_prompt, mem_norm_g[l], w_ca_k[l], w_ca_v[l])
        u0 = jnp.zeros((xp.shape[0], CONV_WIDTH - 1, CONV_DIM), xp.dtype)
        xp, ckv_p, kr_p, u_p = decoder_layer(xp, u0, None, mk_p, mv_p, cos_p, sin_p, *lw)
        ckv_past = cache_ckv[l][page_table].reshape(dec_b, past_len, KV_LORA_RANK)
        kr_past = cache_krope[l][page_table].reshape(dec_b, past_len, QK_ROPE_DIM)
        xs, ckv_s, kr_s, u_s = decoder_layer(xs, state_conv[l], (ckv_past, kr_past), cache_mem_k[l],
                                             cache_mem_v[l], cos_s, sin_s, *lw)
        ckv_p_l.append(ckv_p)
        kr_p_l.append(kr_p)
        conv_p_l.append(u_p)
        mk_p_l.append(mk_p)
        mv_p_l.append(mv_p)
        ckv_s_l.append(ckv_s)
        kr_s_l.append(kr_s)
        conv_s_l.append(u_s)
    return (xp, xs, jnp.stack(ckv_p_l), jnp.stack(kr_p_l), jnp.stack(conv_p_l), jnp.stack(mk_p_l),
            jnp.stack(mv_p_l), jnp.stack(ckv_s_l), jnp.stack(kr_s_l), jnp.stack(conv_s_l))
```

```python
import numpy as np
from contextlib import ExitStack
import concourse.bass as bass
import concourse.mybir as mybir
from concourse.bass_utils import run_bass_kernel_spmd

F32 = mybir.dt.float32
BF16 = mybir.dt.bfloat16
I32 = mybir.dt.int32
AF = mybir.ActivationFunctionType
ALU = mybir.AluOpType
AX = mybir.AxisListType

D = 1024
KC = 8
OFF_H, OFF_GB, OFF_GC, OFF_CQ, OFF_CKV, OFF_KR, OFF_GCONV, OFF_GMLA, IN_COLS = (
    0, 512, 1024, 1536, 1920, 2176, 2208, 3232, 4256)
EPS = 1e-6
NS = 8
SCALE = 96 ** -0.5
NMEM = 256


import os
_PH = os.environ.get("K_PHASES", "2,1,3,3s,4a,4b").split(",")


class _Skip(Exception):
    pass


class PStack(ExitStack):
    def __init__(self, name):
        super().__init__()
        self.pname = name

    def chk(self):
        if self.pname not in _PH:
            raise _Skip()

    def __exit__(self, et, ev, tb):
        r = super().__exit__(None if et is _Skip else et, None if et is _Skip else ev, None if et is _Skip else tb)
        return True if et is _Skip else r


class Tok:
    __slots__ = ("sem", "val", "eng")

    def __init__(self, sem, val, eng):
        self.sem, self.val, self.eng = sem, val, eng


class Buf:
    def __init__(self, t, psum=False):
        self.t = t
        self.lw = None
        self.rd = {}
        self.wd = {}
        self.psum = psum

    def __getitem__(self, k):
        return self.t[k]


class Eng:
    def __init__(self, name, e, sem):
        self.name, self.e, self.sem = name, e, sem
        self.count = 0
        self.waited = {}


class B:
    def __init__(self, nc, es):
        self.nc, self.es = nc, es

        def mk(n, e):
            return Eng(n, e, es.enter_context(nc.semaphore("sem_" + n)))
        self.PE = mk("pe", nc.tensor)
        self.ACT = mk("act", nc.scalar)
        self.DVE = mk("dve", nc.vector)
        self.POOL = mk("pool", nc.gpsimd)
        self.SP = mk("sp", nc.sync)
        self.engs = [self.PE, self.ACT, self.DVE, self.POOL, self.SP]
        self.dq = {}
        for q in (self.SP, self.POOL, self.ACT):
            self.dq[q.name] = ([es.enter_context(nc.semaphore("d_%s%d" % (q.name, i))) for i in range(NS)], [0])
        self.uid = 0

    def sb(self, stack, shape, dt, name=None):
        self.uid += 1
        return Buf(stack.enter_context(self.nc.sbuf_tensor("%s_%d" % (name or "t", self.uid), list(shape), dt)))

    def psum(self, stack, shape, dt, name=None):
        self.uid += 1
        return Buf(stack.enter_context(self.nc.psum_tensor("%s_%d" % (name or "p", self.uid), list(shape), dt)), psum=True)

    def _wait(self, E, tok):
        if tok is None:
            return
        if tok.eng is E and E is self.PE:
            return
        k = tok.sem.num
        if E.waited.get(k, -1) >= tok.val:
            return
        E.e.wait_ge(tok.sem, tok.val)
        E.waited[k] = tok.val

    def _deps(self, E, r, w, wd=()):
        for b in r:
            self._wait(E, b.lw)
            for t in list(b.wd.values()):
                self._wait(E, t)
        for b in w:
            self._wait(E, b.lw)
            for t in list(b.wd.values()):
                self._wait(E, t)
            for t in list(b.rd.values()):
                self._wait(E, t)
        for b in wd:
            self._wait(E, b.lw)
            for t in list(b.rd.values()):
                self._wait(E, t)

    def _upd(self, tok, r, w, wd=()):
        for b in r:
            b.rd[tok.sem.num] = tok
        for b in w:
            b.lw = tok
            b.rd = {}
            b.wd = {}
        for b in wd:
            b.wd[tok.sem.num] = tok

    def op(self, E, fn, r=(), w=(), wd=()):
        w = list(w) + [b for b in r if b.psum and b not in w]
        r = [b for b in r if b not in w]
        self._deps(E, r, w, wd)
        inst = fn(E.e)
        E.count += 1
        inst.then_inc(E.sem, 1)
        tok = Tok(E.sem, E.count, E)
        self._upd(tok, r, w, wd)
        return tok

    def dma(self, Q, out_ap, in_ap, r=(), w=(), idx=None, slow=False):
        sems, ctr = self.dq[Q.name]
        i = ctr[0]
        ctr[0] += 1
        sem = sems[i % NS]
        prev = 16 * (i // NS)
        if prev > 0 and Q.waited.get(sem.num, -1) < prev:
            Q.e.wait_ge(sem, prev)
            Q.waited[sem.num] = prev
        self._deps(Q, r, w)
        if idx is None:
            if slow:
                inst = Q.e.dma_start(out=out_ap, in_=in_ap, allow_slow_non_contiguous=True)
            else:
                inst = Q.e.dma_start(out=out_ap, in_=in_ap)
        else:
            inst = Q.e.indirect_dma_start(out=out_ap, out_offset=None, in_=in_ap,
                                          in_offset=bass.IndirectOffsetOnAxis(ap=idx, axis=0))
        inst.then_inc(sem, 16)
        tok = Tok(sem, prev + 16, None)
        self._upd(tok, r, w)
        return tok

    def barrier(self):
        toks = [Tok(E.sem, E.count, None) for E in self.engs if E.count > 0]
        for name, (sems, ctr) in self.dq.items():
            n = ctr[0]
            for k, sem in enumerate(sems):
                cnt = (n - k + NS - 1) // NS if n > k else 0
                if cnt > 0:
                    toks.append(Tok(sem, 16 * cnt, None))
        for E in self.engs:
            for t in toks:
                self._wait(E, t)


def build(T, NB, SPC, PS, NPOOL):
    assert T % 4 == 0 and PS % 16 == 0
    CH = PS // 16
    NG = T // 4
    S = NB * 128
    nc = bass.Bass("TRN2", target_bir_lowering=False)

    def din(name, shape, dt=F32):
        return nc.dram_tensor(name, list(shape), dt, kind="ExternalInput").ap()

    def dout(name, shape):
        return nc.dram_tensor(name, list(shape), F32, kind="ExternalOutput").ap()

    def dscr(name, shape, dt=F32):
        return Buf(nc.dram_tensor(name, list(shape), dt, kind="Internal").ap())

    xq = din("xq", [T, 128, D]); xh = din("xh", [2 * T, D]); xs = din("xs", [NB, 128, D])
    xsm = din("xsm", [SPC, D]); memp = din("memp", [NMEM, D])
    pool_ckv = din("pool_ckv", [NPOOL * CH, 4096]); pool_kr = din("pool_kr", [NPOOL * CH, 512])
    stc = din("stc", [SPC, 2, 512]); cmk = din("cmk", [SPC, NMEM, D]); cmv = din("cmv", [SPC, NMEM, D])
    ptT = din("ptT", [128, CH * SPC], I32)
    masks = din("masks", [16, 128, 512])
    ropeq = din("ropeq", [T, 128, 256])
    ropek = din("ropek", [128, NB * 32])
    ropes = din("ropes", [SPC, 288])
    bdmask = din("bdmask", [SPC, SPC * 8])
    hmask = din("hmask", [4, 1024])
    w_in = din("w_in", [D, IN_COLS]); conv_wp = din("conv_wp", [128, 12])
    w_conv_out = din("w_conv_out", [512, D]); w_uq = din("w_uq", [384, 768])
    w_uk = din("w_uk", [256, 512]); w_uv = din("w_uv", [256, 512])
    w_mla_out = din("w_mla_out", [512, D]); w_mix_out = din("w_mix_out", [D, D])
    w_ca_q = din("w_ca_q", [D, D]); w_ca_k = din("w_ca_k", [D, D]); w_ca_v = din("w_ca_v", [D, D])
    w_ca_o = din("w_ca_o", [D, D]); w_ff_up = din("w_ff_up", [D, 4096]); w_ff_down = din("w_ff_down", [4096, D])
    gpart = din("gpart", [128, 5 * 8])
    gbc = din("gbc", [128, 3 * D + 256])

    y_p = dout("y_p", [T, 128, D]); y_s = dout("y_s", [SPC, D])
    ckv_p = dout("ckv_p", [T, 128, 256]); kr_p = dout("kr_p", [T, 128, 32])
    conv_p = dout("conv_p", [2, 512]); mk_p = dout("mk_p", [NMEM, D]); mv_p = dout("mv_p", [NMEM, D])
    ckv_s = dout("ckv_s", [SPC, 256]); kr_s = dout("kr_s", [SPC, 32]); conv_s = dout("conv_s", [SPC, 2, 512])

    NT = T + 1
    T1 = dscr("scr_t1", [NT, 128, D]); SG = dscr("scr_sg", [NT, 128, D]); X2 = dscr("scr_x2", [NT, 128, D])
    OA = dscr("scr_oa", [NT, 128, 512], BF16)

    with ExitStack() as es:
        k = B(nc, es)
        PE, ACT, DVE, POOL, SP = k.PE, k.ACT, k.DVE, k.POOL, k.SP
        out_toks = []
        rr = [0]

        def alt():
            rr[0] += 1
            return DVE if rr[0] % 2 else POOL

        ident = k.sb(es, [128, 128], BF16, "ident")
        identf = k.sb(es, [128, 128], F32, "identf")
        gp = k.sb(es, [128, 40], F32, "gp")
        gkv = k.sb(es, [128, 256], F32, "gkv")
        cw = k.sb(es, [128, 12], F32, "cw")
        psf = [k.psum(es, [128, 512], F32, "psf") for _ in range(6)]
        psb = [k.psum(es, [128, 1024], BF16, "psb") for _ in range(2)]
        pfi = [0]
        pbi = [0]

        pinned = set()

        def bankf(pin=False):
            while True:
                pfi[0] += 1
                b = psf[pfi[0] % 6]
                if id(b) not in pinned:
                    break
            if pin:
                pinned.add(id(b))
            return b

        def unpin(b):
            pinned.discard(id(b))

        def bankb():
            pbi[0] += 1
            return psb[pbi[0] % 2]

        k.op(POOL, lambda e: e.memset(ident[:], 1.0), w=[ident])
        k.op(POOL, lambda e: e.affine_select(out=ident[:], in_=ident[:], pattern=[[-1, 128]], compare_op=ALU.is_equal,
                                             fill=0.0, base=0, channel_multiplier=1), r=[ident], w=[ident])
        k.op(POOL, lambda e: e.memset(identf[:], 1.0), w=[identf])
        k.op(POOL, lambda e: e.affine_select(out=identf[:], in_=identf[:], pattern=[[-1, 128]], compare_op=ALU.is_equal,
                                             fill=0.0, base=0, channel_multiplier=1), r=[identf], w=[identf])
        k.dma(SP, gp[:], gpart[:, :], w=[gp])
        k.dma(SP, gkv[:], gbc[:, 3 * D:3 * D + 256], w=[gkv])
        k.dma(SP, cw[:], conv_wp[:, :], w=[cw])

        stg = [k.sb(es, [128, 1024], F32, "stg") for _ in range(4)]
        stq = [(stg[i], 0) for i in range(4)]
        sti = [0]

        def load_w(dst, src, kcw, ncols, gain_col=None, col0=0, dcol0=0):
            for kc_ in range(kcw):
                for c0 in range(0, ncols, 1024):
                    n = min(1024, ncols - c0)
                    sti[0] += 1
                    i = sti[0]
                    st, so = stq[i % 4]
                    sv = st[:, so:so + n]
                    k.dma(SP, sv, src[kc_ * 128:(kc_ + 1) * 128, col0 + c0:col0 + c0 + n], w=[st])
                    o = dst[:, kc_, dcol0 + c0:dcol0 + c0 + n]
                    if gain_col is None:
                        E = (DVE, POOL, ACT)[i % 3]
                        if E is ACT:
                            k.op(ACT, lambda e, o=o, sv=sv: e.copy(out=o, in_=sv), r=[st], wd=[dst])
                        else:
                            k.op(E, lambda e, o=o, sv=sv: e.tensor_copy(out=o, in_=sv), r=[st], wd=[dst])
                    else:
                        g = gp[:, gain_col + kc_:gain_col + kc_ + 1]
                        if i % 2:
                            k.op(DVE, lambda e, o=o, sv=sv, g=g: e.tensor_scalar(out=o, in0=sv, scalar1=g, scalar2=None, op0=ALU.mult),
                                 r=[st, gp], wd=[dst])
                        else:
                            k.op(ACT, lambda e, o=o, sv=sv, g=g: e.activation(out=o, in_=sv, func=AF.Copy, scale=g), r=[st, gp], wd=[dst])

        def rstd_from_ss(ss_ap, out_ap, ssb, outb, n, R):
            k.op(ACT, lambda e: e.activation(out=out_ap, in_=ss_ap, func=AF.Sqrt, bias=EPS, scale=1.0 / n), r=[ssb], w=[outb])
            k.op(DVE, lambda e: e.reciprocal(out=out_ap, in_=out_ap), r=[outb], w=[outb])

        def normT(st, xt, R, dstT, c0=0):
            junk, ss, rs, xb = st["junk"], st["ss"], st["rs"], st["xb"]
            k.op(ACT, lambda e: e.activation(out=junk[0:R, :], in_=xt[0:R, :], func=AF.Square, accum_out=ss[0:R, 0:1]),
                 r=[xt], w=[junk, ss])
            rstd_from_ss(ss[0:R, 0:1], rs[0:R, 0:1], ss, rs, D, R)
            k.op(DVE, lambda e: e.tensor_scalar(out=xb[0:R, :], in0=xt[0:R, :], scalar1=rs[0:R, 0:1], scalar2=None, op0=ALU.mult),
                 r=[xt, rs], w=[xb])
            pb = bankb()
            pv = pb[:].rearrange("p (a b) -> p a b", b=128)
            for c in range(8):
                k.op(PE, lambda e, c=c: e.transpose(pv[:, c, 0:R], xb[0:R, c * 128:(c + 1) * 128], ident[0:R, 0:R]),
                     r=[xb, ident], w=[pb])
            k.op(ACT, lambda e: e.copy(out=dstT[:, :, c0:c0 + R], in_=pv[:, :, 0:R]), r=[pb], w=[dstT])

        def transposeT(src, R, ncol, dstT, E=None):
            nchunk = ncol // 128
            pb = bankb()
            pv = pb[:].rearrange("p (a b) -> p a b", b=128)
            for c in range(nchunk):
                k.op(PE, lambda e, c=c: e.transpose(pv[:, c, 0:R], src[0:R, c * 128:(c + 1) * 128], ident[0:R, 0:R]),
                     r=[src, ident], w=[pb])
            k.op(E or DVE, lambda e: e.tensor_copy(out=dstT[:, 0:nchunk, 0:R], in_=pv[:, 0:nchunk, 0:R]), r=[pb], w=[dstT])

        def mm_tm(xT, R, wt, kcw, col0, ncols, bank, bcol0=0):
            for c in range(kcw):
                k.op(PE, lambda e, c=c: e.matmul(bank[0:R, bcol0:bcol0 + ncols], lhsT=xT[:, c, 0:R],
                                                 rhs=wt[:, c, col0:col0 + ncols], start=(c == 0), stop=(c == kcw - 1)),
                     r=[xT, wt], w=[bank])

        def ckv_post(st, bank, R, cos_ap, sin_ap, ropeb, out_ck, out_kr, outb):
            junk, ss, rs = st["junk"], st["ss2"], st["rs2"]
            k.op(ACT, lambda e: e.activation(out=junk[0:R, 0:256], in_=bank[0:R, 0:256], func=AF.Square, accum_out=ss[0:R, 0:1]),
                 r=[bank], w=[junk, ss])
            rstd_from_ss(ss[0:R, 0:1], rs[0:R, 0:1], ss, rs, 256, R)
            k.op(DVE, lambda e: e.scalar_tensor_tensor(out=out_ck, in0=bank[0:R, 0:256], scalar=rs[0:R, 0:1],
                                                       in1=gkv[0:R, 0:256], op0=ALU.mult, op1=ALU.mult),
                 r=[bank, rs, gkv], w=[outb])
            kr = st["kr"]
            tm = st["tm"]
            k.op(ACT, lambda e: e.copy(out=kr[0:R, 0:32], in_=bank[0:R, 256:288]), r=[bank], w=[kr])
            k.op(DVE, lambda e: e.tensor_tensor(out=tm[0:R, 0:16], in0=kr[0:R, 0:16], in1=cos_ap, op=ALU.mult), r=[kr, ropeb], w=[tm])
            k.op(DVE, lambda e: e.tensor_tensor(out=tm[0:R, 16:32], in0=kr[0:R, 16:32], in1=sin_ap, op=ALU.mult), r=[kr, ropeb], w=[tm])
            k.op(DVE, lambda e: e.tensor_tensor(out=out_kr[:, 0:16], in0=tm[0:R, 0:16], in1=tm[0:R, 16:32], op=ALU.subtract),
                 r=[tm], w=[outb])
            k.op(DVE, lambda e: e.tensor_tensor(out=tm[0:R, 32:48], in0=kr[0:R, 0:16], in1=sin_ap, op=ALU.mult), r=[kr, ropeb], w=[tm])
            k.op(DVE, lambda e: e.tensor_tensor(out=tm[0:R, 48:64], in0=kr[0:R, 16:32], in1=cos_ap, op=ALU.mult), r=[kr, ropeb], w=[tm])
            k.op(DVE, lambda e: e.tensor_tensor(out=out_kr[:, 16:32], in0=tm[0:R, 32:48], in1=tm[0:R, 48:64], op=ALU.add),
                 r=[tm], w=[outb])

        def postnorm_res(st, banks, R, gcol, xres, xout):
            gb = st["gpost"]
            junk, ss, rs = st["junk"], st["ss3"], st["rs3"]
            for hh in range(2):
                k.op(ACT, lambda e, hh=hh: e.activation(out=junk[0:R, hh * 512:(hh + 1) * 512], in_=banks[hh][0:R, :], func=AF.Square,
                                                       accum_out=ss[0:R, hh:hh + 1]), r=[banks[hh]], w=[junk, ss])
            k.op(DVE, lambda e: e.tensor_tensor(out=ss[0:R, 2:3], in0=ss[0:R, 0:1], in1=ss[0:R, 1:2], op=ALU.add), r=[ss], w=[ss])
            rstd_from_ss(ss[0:R, 2:3], rs[0:R, 0:1], ss, rs, D, R)
            for hh in range(2):
                sl = slice(hh * 512, (hh + 1) * 512)
                k.op(DVE, lambda e, hh=hh, sl=sl: e.scalar_tensor_tensor(out=junk[0:R, sl], in0=banks[hh][0:R, :], scalar=rs[0:R, 0:1],
                                                                        in1=gb[0:R, gcol + hh * 512:gcol + (hh + 1) * 512],
                                                                        op0=ALU.mult, op1=ALU.mult), r=[banks[hh], rs, gb], w=[junk])
                k.op(POOL, lambda e, sl=sl: e.tensor_tensor(out=xout[0:R, sl], in0=junk[0:R, sl], in1=xres[0:R, sl], op=ALU.add),
                     r=[junk, xres], w=[xout])

        def mk_small(stack):
            return {
                "junk": k.sb(stack, [128, D], F32, "junk"), "xb": k.sb(stack, [128, D], BF16, "xb"),
                "ss": k.sb(stack, [128, 1], F32, "ss"), "rs": k.sb(stack, [128, 1], F32, "rs"),
                "ss2": k.sb(stack, [128, 1], F32, "ss2"), "rs2": k.sb(stack, [128, 1], F32, "rs2"),
                "ss3": k.sb(stack, [128, 4], F32, "ss3"), "rs3": k.sb(stack, [128, 1], F32, "rs3"),
                "kr": k.sb(stack, [128, 32], F32, "kr"), "tm": k.sb(stack, [128, 64], F32, "tm"),
            }

        def mk_gpost(stack, st, lo=0, hi=3 * D):
            g = k.sb(stack, [128, hi - lo], F32, "gpost")
            k.dma(SP, g[:], gbc[:, lo:hi], w=[g])
            st["gpost"] = g

        with ExitStack() as e13:
            qT = k.sb(e13, [128, 8, T * 128], BF16, "qT")
            qlT = k.sb(e13, [128, 2, SPC, 8], BF16, "qlT")
            qrT = k.sb(e13, [32, SPC, 8], BF16, "qrT")
            cknb = k.sb(e13, [SPC, 257], BF16, "cknb")
            cknT = k.sb(e13, [128, 2, SPC], BF16, "cknT")
            krnT = k.sb(e13, [32, SPC], BF16, "krnT")
            wuk = k.sb(e13, [128, 2, 512], BF16, "wuk")
            wuv = k.sb(e13, [128, 2, 512], BF16, "wuv")
            load_w(wuk, w_uk, 2, 512)
            load_w(wuv, w_uv, 2, 512)

            with PStack("2") as e2:
                e2.chk()
                st = mk_small(e2)
                win = k.sb(e2, [128, KC, IN_COLS], BF16, "win")
                wuq = k.sb(e2, [128, 3, 768], BF16, "wuq")
                wco = k.sb(e2, [128, 4, D], BF16, "wco")
                load_w(win, w_in, KC, IN_COLS, gain_col=0)
                load_w(wuq, w_uq, 3, 768, gain_col=32)
                load_w(wco, w_conv_out, 4, D)
                xt = [k.sb(e2, [128, D], F32, "xt") for _ in range(2)]
                xnT = [k.sb(e2, [128, 8, 128], BF16, "xnT") for _ in range(2)]
                uTh = k.sb(e2, [128, 4, 2 * T], F32, "uTh")
                uT = k.sb(e2, [128, 4, 130], F32, "uT")
                hs = k.sb(e2, [128, 128], F32, "hs")
                acc = k.sb(e2, [128, 128], F32, "acc")
                ycT = k.sb(e2, [128, 4, 128], BF16, "ycT")
                sgc = k.sb(e2, [128, 512], F32, "sgc")
                t1 = [k.sb(e2, [128, D], F32, "t1")] * 2
                sgm = [k.sb(e2, [128, D], F32, "sgm")] * 2
                cko = [k.sb(e2, [128, 288], F32, "cko") for _ in range(2)]
                cqn = k.sb(e2, [128, 384], BF16, "cqn")
                cqnT = k.sb(e2, [128, 3, 128], BF16, "cqnT")
                qf = k.sb(e2, [128, 8, 96], F32, "qf")
                qtm = k.sb(e2, [128, 4, 128], F32, "qtm")
                qb = k.sb(e2, [128, 8, 96], BF16, "qb")
                rq = [k.sb(e2, [128, 256], F32, "rq") for _ in range(2)]
                stt = k.sb(e2, [SPC, 2, 512], F32, "stt")
                stT = k.sb(e2, [128, 2, 4, SPC], F32, "stT")
                utm = st["junk"]
                rsm = k.sb(e2, [SPC, 288], F32, "rsm")
                wukT = k.sb(e2, [64, 8, 256], BF16, "wukT")

                k.dma(SP, xt[0][0:2 * T, :], xh[:, :], w=[xt[0]])
                normT(st, xt[0], 2 * T, xnT[0])
                for fc in range(4):
                    bk = bankf()
                    for part, off in ((0, OFF_H), (1, OFF_GC)):
                        for c in range(KC):
                            k.op(PE, lambda e, c=c, off=off, part=part, fc=fc, bk=bk: e.matmul(
                                bk[:, part * 128:part * 128 + 2 * T], lhsT=win[:, c, off + fc * 128:off + (fc + 1) * 128],
                                rhs=xnT[0][:, c, 0:2 * T], start=(c == 0), stop=(c == KC - 1)), r=[win, xnT[0]], w=[bk])
                    k.op(ACT, lambda e, bk=bk: e.copy(out=hs[:, 0:2 * T], in_=bk[:, 0:2 * T]), r=[bk], w=[hs])
                    k.op(DVE, lambda e, bk=bk, fc=fc: e.tensor_tensor(out=uTh[:, fc, :], in0=hs[:, 0:2 * T], in1=bk[:, 128:128 + 2 * T],
                                                                       op=ALU.mult), r=[hs, bk], w=[uTh])

                k.dma(SP, stt[:], stc[:, :, :], w=[stt])
                for kk in range(2):
                    bk = bankf()
                    for fc in range(4):
                        k.op(PE, lambda e, kk=kk, fc=fc, bk=bk: e.transpose(bk[:, fc * SPC:(fc + 1) * SPC], stt[0:SPC, kk, fc * 128:(fc + 1) * 128],
                                                                           identf[0:SPC, 0:SPC]), r=[stt, identf], w=[bk])
                    k.op(DVE, lambda e, kk=kk, bk=bk: e.tensor_copy(out=stT[:, kk, :, :], in_=bk[:, 0:4 * SPC].rearrange("p (a b) -> p a b", b=SPC)),
                         r=[bk], w=[stT])
                for h in range(8):
                    pb = bankb()
                    for lc in range(2):
                        k.op(PE, lambda e, h=h, lc=lc, pb=pb: e.transpose(pb[0:64, lc * 128:(lc + 1) * 128], wuk[:, lc, h * 64:(h + 1) * 64],
                                                                         ident[:, :]), r=[wuk, ident], w=[pb])
                    k.op(DVE, lambda e, h=h, pb=pb: e.tensor_copy(out=wukT[:, h, :], in_=pb[0:64, 0:256]), r=[pb], w=[wukT])

                for t in range(NT):
                    smp = (t == T)
                    R = SPC if smp else 128
                    x_t = xt[t % 2]
                    xn = xnT[t % 2]
                    if smp:
                        k.dma(SP, x_t[0:R, :], xsm[:, :], w=[x_t])
                        k.dma(SP, rsm[:], ropes[:, :], w=[rsm])
                    else:
                        k.dma(SP, x_t[:, :], xq[t, :, :], w=[x_t])
                        k.dma(SP, rq[t % 2][:], ropeq[t, :, :], w=[rq[t % 2]])
                    normT(st, x_t, R, xn)
                    if not smp:
                        k.op(POOL, lambda e, t=t: e.tensor_copy(out=uT[:, :, 0:2], in_=uTh[:, :, 2 * t:2 * t + 2]), r=[uTh], w=[uT])
                    for fc in range(4):
                        bk = bankf()
                        for part, off in ((0, OFF_H), (1, OFF_GB), (2, OFF_GC)):
                            for c in range(KC):
                                k.op(PE, lambda e, c=c, off=off, part=part, fc=fc, bk=bk, xn=xn, R=R: e.matmul(
                                    bk[:, part * 128:part * 128 + R], lhsT=win[:, c, off + fc * 128:off + (fc + 1) * 128],
                                    rhs=xn[:, c, 0:R], start=(c == 0), stop=(c == KC - 1)), r=[win, xn], w=[bk])
                        k.op(ACT, lambda e, bk=bk, R=R: e.copy(out=hs[:, 0:R], in_=bk[:, 0:R]), r=[bk], w=[hs])
                        k.op(DVE, lambda e, bk=bk, fc=fc, R=R: e.tensor_tensor(out=uT[:, fc, 2:2 + R], in0=hs[:, 0:R], in1=bk[:, 256:256 + R],
                                                                                op=ALU.mult), r=[hs, bk], w=[uT])
                        if smp:
                            a0, a1, a2 = stT[:, 0, fc, :], stT[:, 1, fc, :], uT[:, fc, 2:2 + R]
                            rd = [stT, uT, cw]
                        else:
                            a0, a1, a2 = uT[:, fc, 0:R], uT[:, fc, 1:1 + R], uT[:, fc, 2:2 + R]
                            rd = [uT, cw]
                        k.op(DVE, lambda e, a0=a0, fc=fc, R=R: e.tensor_scalar(out=acc[:, 0:R], in0=a0, scalar1=cw[:, fc * 3:fc * 3 + 1], scalar2=None,
                                                                               op0=ALU.mult), r=rd, w=[acc])
                        k.op(DVE, lambda e, a1=a1, fc=fc, R=R: e.scalar_tensor_tensor(out=acc[:, 0:R], in0=a1, scalar=cw[:, fc * 3 + 1:fc * 3 + 2],
                                                                                      in1=acc[:, 0:R], op0=ALU.mult, op1=ALU.add), r=rd + [acc], w=[acc])
                        k.op(DVE, lambda e, a2=a2, fc=fc, R=R: e.scalar_tensor_tensor(out=acc[:, 0:R], in0=a2, scalar=cw[:, fc * 3 + 2:fc * 3 + 3],
                                                                                      in1=acc[:, 0:R], op0=ALU.mult, op1=ALU.add), r=rd + [acc], w=[acc])
                        k.op(DVE, lambda e, bk=bk, fc=fc, R=R: e.tensor_tensor(out=ycT[:, fc, 0:R], in0=acc[:, 0:R], in1=bk[:, 128:128 + R], op=ALU.mult),
                             r=[acc, bk], w=[ycT])
                    if smp or t == T - 1:
                        bk = bankf()
                        ncol = R if smp else 2
                        c0 = 2 if smp else 128
                        for fc in range(4):
                            k.op(PE, lambda e, fc=fc, bk=bk, ncol=ncol, c0=c0: e.transpose(bk[0:ncol, fc * 128:(fc + 1) * 128], uT[:, fc, c0:c0 + ncol],
                                                                                          identf[:, :]), r=[uT, identf], w=[bk])
                        k.op(DVE, lambda e, bk=bk, ncol=ncol: e.tensor_copy(out=utm[0:ncol, 0:512], in_=bk[0:ncol, :]), r=[bk], w=[utm])
                        if smp:
                            out_toks.append(k.dma(POOL, conv_s[:, 1, :], utm[0:R, 0:512], r=[utm]))
                            out_toks.append(k.dma(POOL, conv_s[:, 0, :], stt[0:R, 1, :], r=[stt]))
                        else:
                            out_toks.append(k.dma(POOL, conv_p[:, :], utm[0:2, 0:512], r=[utm]))
                    ycb = [bankf(), bankf()]
                    for hh in range(2):
                        mm_tm(ycT, R, wco, 4, hh * 512, 512, ycb[hh])
                    t1b = t1[t % 2]
                    for hh in range(2):
                        bk = bankf()
                        mm_tm(xn, R, win, KC, OFF_GCONV + hh * 512, 512, bk)
                        k.op(ACT, lambda e, bk=bk, R=R: e.activation(out=sgc[0:R, :], in_=bk[0:R, :], func=AF.Sigmoid), r=[bk], w=[sgc])
                        k.op(DVE, lambda e, hh=hh, R=R, t1b=t1b: e.tensor_tensor(out=t1b[0:R, hh * 512:(hh + 1) * 512], in0=sgc[0:R, :],
                                                                                  in1=ycb[hh][0:R, :], op=ALU.mult), r=[sgc, ycb[hh]], w=[t1b])
                    k.dma(POOL, T1[t, 0:R, :], t1b[0:R, :], r=[t1b], w=[T1])
                    sgb = sgm[t % 2]
                    for hh in range(2):
                        bk = bankf()
                        mm_tm(xn, R, win, KC, OFF_GMLA + hh * 512, 512, bk)
                        k.op(ACT, lambda e, bk=bk, hh=hh, R=R, sgb=sgb: e.activation(out=sgb[0:R, hh * 512:(hh + 1) * 512], in_=bk[0:R, :],
                                                                                      func=AF.Sigmoid), r=[bk], w=[sgb])
                    k.dma(POOL, SG[t, 0:R, :], sgb[0:R, :], r=[sgb], w=[SG])
                    bk = bankf()
                    mm_tm(xn, R, win, KC, OFF_CKV, 288, bk)
                    ckb = cko[t % 2]
                    if smp:
                        cos_ap, sin_ap, ropeb = rsm[0:R, 256:272], rsm[0:R, 272:288], rsm
                    else:
                        cos_ap, sin_ap, ropeb = rq[t % 2][:, 0:16], rq[t % 2][:, 128:144], rq[t % 2]
                    ckv_post(st, bk, R, cos_ap, sin_ap, ropeb, ckb[0:R, 0:256], ckb[0:R, 256:288], ckb)
                    if smp:
                        out_toks.append(k.dma(POOL, ckv_s[:, :], ckb[0:R, 0:256], r=[ckb]))
                        out_toks.append(k.dma(POOL, kr_s[:, :], ckb[0:R, 256:288], r=[ckb]))
                        k.op(DVE, lambda e: e.tensor_copy(out=cknb[:, 0:256], in_=ckb[0:R, 0:256]), r=[ckb], w=[cknb])
                        k.op(POOL, lambda e: e.memset(cknb[:, 256:257], 1.0), w=[cknb])
                        k.op(DVE, lambda e: e.tensor_copy(out=st["xb"][0:R, 0:32], in_=ckb[0:R, 256:288]), r=[ckb], w=[st["xb"]])
                        pb = bankb()
                        for lc in range(2):
                            k.op(PE, lambda e, lc=lc, pb=pb: e.transpose(pb[:, lc * 128:lc * 128 + R], cknb[0:R, lc * 128:(lc + 1) * 128], ident[0:R, 0:R]),
                                 r=[cknb, ident], w=[pb])
                        k.op(PE, lambda e, pb=pb: e.transpose(pb[0:32, 256:256 + R], st["xb"][0:R, 0:32], ident[0:R, 0:R]), r=[st["xb"], ident], w=[pb])
                        k.op(DVE, lambda e, pb=pb: e.tensor_copy(out=cknT[:, :, :], in_=pb[:, 0:256].rearrange("p (a b) -> p a b", b=128)[:, :, 0:R]),
                             r=[pb], w=[cknT])
                        k.op(DVE, lambda e, pb=pb: e.tensor_copy(out=krnT[:, :], in_=pb[0:32, 256:256 + R]), r=[pb], w=[krnT])
                    else:
                        out_toks.append(k.dma(POOL, ckv_p[t, :, :], ckb[:, 0:256], r=[ckb]))
                        out_toks.append(k.dma(POOL, kr_p[t, :, :], ckb[:, 256:288], r=[ckb]))
                    bk = bankf()
                    mm_tm(xn, R, win, KC, OFF_CQ, 384, bk)
                    k.op(ACT, lambda e, bk=bk, R=R: e.activation(out=st["junk"][0:R, 0:384], in_=bk[0:R, 0:384], func=AF.Square,
                                                                 accum_out=st["ss"][0:R, 0:1]), r=[bk], w=[st["junk"], st["ss"]])
                    rstd_from_ss(st["ss"][0:R, 0:1], st["rs"][0:R, 0:1], st["ss"], st["rs"], 384, R)
                    k.op(DVE, lambda e, bk=bk, R=R: e.tensor_scalar(out=cqn[0:R, :], in0=bk[0:R, 0:384], scalar1=st["rs"][0:R, 0:1], scalar2=None,
                                                                    op0=ALU.mult), r=[bk, st["rs"]], w=[cqn])
                    transposeT(cqn, R, 384, cqnT)
                    qb0, qb1 = bankf(), bankf()
                    mm_tm(cqnT, R, wuq, 3, 0, 512, qb0)
                    mm_tm(cqnT, R, wuq, 3, 512, 256, qb1)
                    qfl = qf[:].rearrange("p a b -> p (a b)")
                    k.op(ACT, lambda e, R=R: e.copy(out=qfl[0:R, 0:512], in_=qb0[0:R, :]), r=[qb0], w=[qf])
                    k.op(ACT, lambda e, R=R: e.copy(out=qfl[0:R, 512:768], in_=qb1[0:R, 0:256]), r=[qb1], w=[qf])
                    if smp:
                        cq8 = rsm[0:R, 0:128].rearrange("p (a b) -> p a b", b=16)
                        sq8 = rsm[0:R, 128:256].rearrange("p (a b) -> p a b", b=16)
                        rb = rsm
                    else:
                        cq8 = rq[t % 2][:, 0:128].rearrange("p (a b) -> p a b", b=16)
                        sq8 = rq[t % 2][:, 128:256].rearrange("p (a b) -> p a b", b=16)
                        rb = rq[t % 2]
                    x1, x2 = qf[0:R, :, 64:80], qf[0:R, :, 80:96]
                    tq = [qtm[0:R, i, :].rearrange("p (a b) -> p a b", b=16) for i in range(4)]
                    k.op(POOL, lambda e: e.tensor_tensor(out=tq[0], in0=x1, in1=cq8, op=ALU.mult), r=[qf, rb], w=[qtm])
                    k.op(POOL, lambda e: e.tensor_tensor(out=tq[1], in0=x2, in1=sq8, op=ALU.mult), r=[qf, rb], w=[qtm])
                    k.op(POOL, lambda e: e.tensor_tensor(out=tq[2], in0=x1, in1=sq8, op=ALU.mult), r=[qf, rb], w=[qtm])
                    k.op(POOL, lambda e: e.tensor_tensor(out=tq[3], in0=x2, in1=cq8, op=ALU.mult), r=[qf, rb], w=[qtm])
                    k.op(DVE, lambda e, R=R: e.tensor_copy(out=qb[0:R, :, 32:96], in_=qf[0:R, :, 0:64]), r=[qf], w=[qb])
                    k.op(DVE, lambda e, R=R: e.tensor_tensor(out=qb[0:R, :, 0:16], in0=tq[0], in1=tq[1], op=ALU.subtract), r=[qtm], w=[qb])
                    k.op(DVE, lambda e, R=R: e.tensor_tensor(out=qb[0:R, :, 16:32], in0=tq[2], in1=tq[3], op=ALU.add), r=[qtm], w=[qb])
                    pb = bankb()
                    pv = pb[:].rearrange("p (a b) -> p a b", b=128)
                    if not smp:
                        for h in range(8):
                            k.op(PE, lambda e, h=h, pb=pb, pv=pv, R=R: e.transpose(pv[0:96, h, 0:R], qb[0:R, h, :], ident[0:R, 0:R]), r=[qb, ident], w=[pb])
                        k.op(DVE, lambda e, pv=pv, t=t: e.tensor_copy(out=qT[0:96, :, t * 128:(t + 1) * 128], in_=pv[0:96, :, :]), r=[pb], w=[qT])
                    else:
                        for h in range(8):
                            k.op(PE, lambda e, h=h, pb=pb, pv=pv, R=R: e.transpose(pv[0:64, h, 0:R], qb[0:R, h, 32:96], ident[0:R, 0:R]), r=[qb, ident], w=[pb])
                        qsT = k.sb(e2, [64, 8, SPC], BF16, "qsT")
                        k.op(DVE, lambda e, pv=pv, R=R: e.tensor_copy(out=qsT[0:64, :, :], in_=pv[0:64, :, 0:R]), r=[pb], w=[qsT])
                        pb2 = bankb()
                        pv2 = pb2[:].rearrange("p (a b) -> p a b", b=128)
                        for h in range(8):
                            k.op(PE, lambda e, h=h, pv2=pv2, pb2=pb2, R=R: e.transpose(pv2[0:32, h, 0:R], qb[0:R, h, 0:32], ident[0:R, 0:R]),
                                 r=[qb, ident], w=[pb2])
                        k.op(DVE, lambda e, pv2=pv2, R=R: e.tensor_copy(out=qrT[:, :, :].rearrange("p s h -> p h s"), in_=pv2[0:32, :, 0:R]),
                             r=[pb2], w=[qrT])
                        for lc in range(2):
                            bk = bankf()
                            for h in range(8):
                                k.op(PE, lambda e, h=h, lc=lc, bk=bk, R=R: e.matmul(bk[:, h * SPC:(h + 1) * SPC], lhsT=wukT[:, h, lc * 128:(lc + 1) * 128],
                                                                                   rhs=qsT[0:64, h, 0:R], start=True, stop=True), r=[wukT, qsT], w=[bk])
                            k.op(DVE, lambda e, lc=lc, bk=bk: e.tensor_copy(out=qlT[:, lc, :, :].rearrange("p s h -> p h s"),
                                                                             in_=bk[:, 0:8 * SPC].rearrange("p (h s) -> p h s", s=SPC)), r=[bk], w=[qlT])
            k.barrier()
            o_all = k.sb(e13, [128, NT, 512], BF16, "o_all")
            eK = ExitStack()
            eK.__enter__()
            ckvnT = k.sb(eK, [128, 2, S], BF16, "ckvnT")
            KT = k.sb(eK, [128, S], BF16, "KT")
            KRB = k.sb(eK, [128, S], BF16, "KRB")
            wukp = k.sb(eK, [128, 2, 8, 96], BF16, "wukp")
            k.op(DVE, lambda e: e.memset(wukp[:].rearrange("p a h d -> p (a h d)"), 0.0), w=[wukp])
            for lc_ in range(2):
                k.op(DVE, lambda e, lc_=lc_: e.tensor_copy(out=wukp[:, lc_, :, 32:96], in_=wuk[:, lc_, :].rearrange("p (h d) -> p h d", d=64)), r=[wuk], w=[wukp])

            with PStack("1") as e1:
                e1.chk()
                stX = mk_small(e1)
                stY = mk_small(e1)
                wkv = k.sb(e1, [128, KC, 288], BF16, "wkv")
                load_w(wkv, w_in, KC, 288, gain_col=0, col0=OFF_CKV)
                xt = [k.sb(e1, [128, D], F32, "xt") for _ in range(3)]
                xnT = [k.sb(e1, [128, 8, 128], BF16, "xnT") for _ in range(2)]
                rk = k.sb(e1, [128, NB, 32], F32, "rk")
                ckb = [k.sb(e1, [128, 352], BF16, "ckb") for _ in range(2)]
                for cb_ in ckb:
                    k.op(DVE, lambda e, cb_=cb_: e.memset(cb_[:, 288:352], 0.0), w=[cb_])
                k.dma(SP, rk[:].rearrange("p n c -> p (n c)"), ropek[:, :], w=[rk])

                def stage_x(nb):
                    x_t = xt[nb % 3]
                    k.dma(SP if nb % 2 else ACT, x_t[:, :], xs[nb, :, :], w=[x_t])
                    normT(stX, x_t, 128, xnT[nb % 2])

                def stage_y(nb):
                    xn = xnT[nb % 2]
                    bk = bankf()
                    mm_tm(xn, 128, wkv, KC, 0, 288, bk)
                    cb = ckb[nb % 2]
                    ckv_post(stY, bk, 128, rk[:, nb, 0:16], rk[:, nb, 16:32], rk, cb[:, 0:256], cb[:, 256:288], cb)
                    pb = bankb()
                    pv = pb[:].rearrange("p (a b) -> p a b", b=128)
                    k.op(PE, lambda e: e.transpose(pv[:, 0, :], cb[:, 0:128], ident[:, :]), r=[cb, ident], w=[pb])
                    k.op(PE, lambda e: e.transpose(pv[:, 1, :], cb[:, 128:256], ident[:, :]), r=[cb, ident], w=[pb])
                    k.op(PE, lambda e: e.transpose(pv[0:96, 2, :], cb[:, 256:352], ident[:, :]), r=[cb, ident], w=[pb])
                    k.op(ACT, lambda e: e.copy(out=ckvnT[:, :, nb * 128:(nb + 1) * 128], in_=pv[:, 0:2, :]), r=[pb], w=[ckvnT])
                    k.op(ACT, lambda e: e.copy(out=KRB[0:96, nb * 128:(nb + 1) * 128], in_=pv[0:96, 2, :]), r=[pb], w=[KRB])

                stage_x(0)
                for nb in range(NB):
                    if nb + 1 < NB:
                        stage_x(nb + 1)
                    stage_y(nb)
            k.barrier()

            with PStack("3") as e3:
                e3.chk()
                mk16 = k.sb(e3, [128, 16, 512], BF16, "mk16")
                for d in range(16):
                    sti[0] += 1
                    s_ = stg[sti[0] % 4]
                    k.dma(SP, s_[:, 0:512], masks[d, :, :], w=[s_])
                    k.op(alt(), lambda e, d=d, s_=s_: e.tensor_copy(out=mk16[:, d, :], in_=s_[:, 0:512]), r=[s_], w=[mk16])
                V1 = k.sb(e3, [128, NB, 65], BF16, "V1")
                k.op(POOL, lambda e: e.memset(V1[:, :, 64:65], 1.0), w=[V1])
                pT = [k.sb(e3, [128, 512], BF16, "pT") for _ in range(3)]
                pM = [k.sb(e3, [128, 512], BF16, "pM") for _ in range(2)]
                rec = k.sb(e3, [128, 4], F32, "rec")
                oTs = k.sb(e3, [128, 512], F32, "oTs")
                pti = 0
                for h in range(8):
                    for c0 in range(0, S, 512):
                        bk = bankf()
                        for lc in range(2):
                            k.op(PE, lambda e, lc=lc, bk=bk, c0=c0, h=h: e.matmul(bk[0:96, :], lhsT=wukp[:, lc, h, :],
                                                                                 rhs=ckvnT[:, lc, c0:c0 + 512], start=(lc == 0), stop=(lc == 1)),
                                 r=[wukp, ckvnT], w=[bk])
                        k.op(DVE, lambda e, bk=bk, c0=c0: e.tensor_tensor(out=KT[0:96, c0:c0 + 512], in0=bk[0:96, :], in1=KRB[0:96, c0:c0 + 512], op=ALU.add),
                             r=[bk, KRB], w=[KT])
                    for nb0 in range(0, NB, 8):
                        bk = bankf()
                        nn = min(8, NB - nb0)
                        for i in range(nn):
                            nb = nb0 + i
                            for lc in range(2):
                                k.op(PE, lambda e, lc=lc, bk=bk, nb=nb, i=i, h=h: e.matmul(bk[:, i * 64:(i + 1) * 64], lhsT=ckvnT[:, lc, nb * 128:(nb + 1) * 128],
                                                                                          rhs=wuv[:, lc, h * 64:(h + 1) * 64], start=(lc == 0), stop=(lc == 1)),
                                     r=[wuv, ckvnT], w=[bk])
                        k.op(DVE, lambda e, bk=bk, nb0=nb0, nn=nn: e.tensor_copy(out=V1[:, nb0:nb0 + nn, 0:64],
                                                                                 in_=bk[:, 0:nn * 64].rearrange("p (a b) -> p a b", b=64)), r=[bk], w=[V1])
                    for u in range(NG):
                        nkb = 16 * u + 16
                        oT_ = bankf(pin=True)

                        def emit_st(kb, h=h, u=u):
                            sb_ = bankf()
                            k.op(PE, lambda e: e.matmul(sb_[:, :], lhsT=KT[0:96, kb * 128:(kb + 1) * 128],
                                                        rhs=qT[0:96, h, u * 512:(u + 1) * 512], start=True, stop=True),
                                 r=[KT, qT], w=[sb_])
                            return sb_
                        pend = [emit_st(0)]
                        if nkb > 1:
                            pend.append(emit_st(1))
                        for kb in range(nkb):
                            sb_ = pend.pop(0)
                            if kb + 2 < nkb:
                                pend.append(emit_st(kb + 2))
                            pti += 1
                            p_ = pT[pti % 3]
                            k.op(ACT, lambda e, sb_=sb_, p_=p_: e.activation(out=p_[:, :], in_=sb_[:, :], func=AF.Exp, scale=SCALE), r=[sb_], w=[p_])
                            if kb >= 16 * u:
                                d = kb - 16 * u
                                pm_ = pM[pti % 2]
                                k.op(DVE if pti % 2 else POOL, lambda e, p_=p_, d=d, pm_=pm_: e.tensor_tensor(out=pm_[:, :], in0=p_[:, :], in1=mk16[:, d, :], op=ALU.mult),
                                     r=[p_, mk16], w=[pm_])
                                p_ = pm_
                            k.op(PE, lambda e, p_=p_, kb=kb, nkb=nkb: e.matmul(oT_[0:65, :], lhsT=V1[:, kb, :], rhs=p_[:, :],
                                                                              start=(kb == 0), stop=(kb == nkb - 1)), r=[p_, V1], w=[oT_])
                        k.op(ACT, lambda e: e.copy(out=oTs[0:65, :], in_=oT_[0:65, :]), r=[oT_], w=[oTs])
                        unpin(oT_)
                        tb = bankf()
                        for qi in range(4):
                            k.op(PE, lambda e, qi=qi, tb=tb: e.transpose(tb[:, qi * 65:(qi + 1) * 65], oTs[0:65, qi * 128:(qi + 1) * 128], identf[0:65, 0:65]),
                                 r=[oTs, identf], w=[tb])
                        tv = tb[:, 0:260].rearrange("p (a b) -> p a b", b=65)
                        k.op(DVE, lambda e, tv=tv: e.reciprocal(out=rec[:, :], in_=tv[:, :, 64]), r=[tb], w=[rec])
                        for qi in range(4):
                            k.op(DVE, lambda e, qi=qi, tv=tv, u=u, h=h: e.tensor_scalar(out=o_all[:, 4 * u + qi, h * 64:(h + 1) * 64], in0=tv[:, qi, 0:64],
                                                                                       scalar1=rec[:, qi:qi + 1], scalar2=None, op0=ALU.mult),
                                 r=[tb, rec], w=[o_all])
            k.barrier()
            eK.close()

            with PStack("3s") as e3s:
                e3s.chk()
                pti_ = k.sb(e3s, [128, CH * SPC], I32, "pti")
                k.dma(SP, pti_[:], ptT[:, :], w=[pti_])
                NBUF = 3
                G = [k.sb(e3s, [128, 16, 256], F32, "G") for _ in range(NBUF)]
                Gk = [k.sb(e3s, [128, 16, 32], F32, "Gk") for _ in range(NBUF)]
                Gb = [k.sb(e3s, [128, 16, 256], BF16, "Gb") for _ in range(NBUF)]
                Gkb = [k.sb(e3s, [128, 16, 32], BF16, "Gkb") for _ in range(NBUF)]
                cTc = [k.sb(e3s, [128, 4, 2, 512], BF16, "cTc") for _ in range(2)]
                cTk = [k.sb(e3s, [32, 2, 1024], BF16, "cTk") for _ in range(2)]
                pch = [k.sb(e3s, [8, 2048], BF16, "pch") for _ in range(2)]
                ptp = [k.sb(e3s, [128, 16, 8], BF16, "ptp") for _ in range(2)]
                rsx = k.sb(e3s, [8, SPC * CH * 4], F32, "rsx")
                den = k.sb(e3s, [8, 2], F32, "den")
                olb = k.sb(e3s, [8, 256], BF16, "olb")
                OT = k.sb(e3s, [128, 2, 8, SPC], BF16, "OT")
                bdm = k.sb(e3s, [SPC, SPC * 8], F32, "bdm")
                pnw = k.sb(e3s, [SPC, SPC * 8], F32, "pnw")
                PnT = k.sb(e3s, [SPC, SPC, 8], BF16, "PnT")
                k.dma(SP, bdm[:], bdmask[:, :], w=[bdm])
                bk = bankf()
                qlf = [qlT[:, lc, :, :].rearrange("p s h -> p (s h)") for lc in range(2)]
                for lc in range(2):
                    k.op(PE, lambda e, lc=lc, bk=bk: e.matmul(bk[0:SPC, 0:SPC * 8], lhsT=cknT[:, lc, :], rhs=qlf[lc], start=(lc == 0), stop=False),
                         r=[cknT, qlT], w=[bk])
                k.op(PE, lambda e, bk=bk: e.matmul(bk[0:SPC, 0:SPC * 8], lhsT=krnT[:, :], rhs=qrT[:, :, :].rearrange("p s h -> p (s h)"),
                                                   start=False, stop=True), r=[krnT, qrT], w=[bk])
                k.op(ACT, lambda e, bk=bk: e.activation(out=pnw[:, :], in_=bk[0:SPC, 0:SPC * 8], func=AF.Exp, scale=SCALE), r=[bk], w=[pnw])
                k.op(DVE, lambda e: e.tensor_tensor(out=PnT[:, :, :].rearrange("p s h -> p (s h)"), in0=pnw[:, :], in1=bdm[:, :], op=ALU.mult),
                     r=[pnw, bdm], w=[PnT])

                chunks = [(s_, ch_) for s_ in range(SPC) for ch_ in range(CH)]
                NCK = len(chunks)

                def load(c):
                    s_, ch_ = chunks[c]
                    g_, gk_, gb_, gkb_ = G[c % NBUF], Gk[c % NBUF], Gb[c % NBUF], Gkb[c % NBUF]
                    k.dma(POOL, g_[:].rearrange("p a b -> p (a b)"), pool_ckv[:, :], r=[pti_], w=[g_], idx=pti_[:, ch_ * SPC + s_:ch_ * SPC + s_ + 1])
                    k.dma(POOL, gk_[:].rearrange("p a b -> p (a b)"), pool_kr[:, :], r=[pti_], w=[gk_], idx=pti_[:, ch_ * SPC + s_:ch_ * SPC + s_ + 1])
                    Ec = (DVE, ACT, POOL)[c % 3]
                    if Ec is ACT:
                        k.op(ACT, lambda e: e.copy(out=gb_[:], in_=g_[:]), r=[g_], w=[gb_])
                    else:
                        k.op(Ec, lambda e: e.tensor_copy(out=gb_[:], in_=g_[:]), r=[g_], w=[gb_])
                    k.op(POOL if c % 2 else DVE, lambda e: e.tensor_copy(out=gkb_[:], in_=gk_[:]), r=[gk_], w=[gkb_])

                tbs = [psb[0], psb[1], psf[4], psf[5]]
                pinned.add(id(psf[4]))
                pinned.add(id(psf[5]))
                tbi = [0]

                def bankb4():
                    tbi[0] += 1
                    b_ = tbs[tbi[0] % 4]
                    full = b_[:] if (b_ is psb[0] or b_ is psb[1]) else b_.t.bitcast(BF16)[:]
                    return b_, full

                def stageA(c):
                    gb_, gkb_, cc, ck = Gb[c % NBUF], Gkb[c % NBUF], cTc[c % 2], cTk[c % 2]
                    for grp in range(4):
                        pb, pf = bankb4()
                        pv = pf.rearrange("p (a b) -> p a b", b=128)
                        for lc in range(2):
                            for i in range(4):
                                tt = grp * 4 + i
                                k.op(PE, lambda e, lc=lc, i=i, tt=tt, pv=pv: e.transpose(pv[:, lc * 4 + i, :], gb_[:, tt, lc * 128:(lc + 1) * 128], ident[:, :]),
                                     r=[gb_, ident], w=[pb])
                        if grp % 2:
                            k.op(ACT, lambda e, pf=pf, grp=grp: e.copy(out=cc[:, grp, :, :].rearrange("p a b -> p (a b)"), in_=pf[:, :]), r=[pb], w=[cc])
                        else:
                            k.op(DVE, lambda e, pf=pf, grp=grp: e.tensor_copy(out=cc[:, grp, :, :].rearrange("p a b -> p (a b)"), in_=pf[:, :]), r=[pb], w=[cc])
                    for half in range(2):
                        pb, pf = bankb4()
                        pv = pf.rearrange("p (a b) -> p a b", b=128)
                        for i in range(8):
                            tt = half * 8 + i
                            k.op(PE, lambda e, i=i, tt=tt, pv=pv: e.transpose(pv[0:32, i, :], gkb_[:, tt, :], ident[:, :]), r=[gkb_, ident], w=[pb])
                        if half:
                            k.op(ACT, lambda e, pf=pf, half=half: e.copy(out=ck[:, half, :], in_=pf[0:32, :]), r=[pb], w=[ck])
                        else:
                            k.op(DVE, lambda e, pf=pf, half=half: e.tensor_copy(out=ck[:, half, :], in_=pf[0:32, :]), r=[pb], w=[ck])

                def stageB(c):
                    s_, ch_ = chunks[c]
                    cc, ck, pc = cTc[c % 2], cTk[c % 2], pch[c % 2]
                    for grp in range(4):
                        sb_ = bankf()
                        for lc in range(2):
                            k.op(PE, lambda e, lc=lc, sb_=sb_, grp=grp: e.matmul(sb_[0:8, :], lhsT=qlT[:, lc, s_, :], rhs=cc[:, grp, lc, :],
                                                                                start=(lc == 0), stop=False), r=[qlT, cc], w=[sb_])
                        k.op(PE, lambda e, sb_=sb_, grp=grp: e.matmul(sb_[0:8, :], lhsT=qrT[:, s_, :], rhs=ck[:, grp // 2, (grp % 2) * 512:(grp % 2 + 1) * 512],
                                                                     start=False, stop=True), r=[qrT, ck], w=[sb_])
                        col = (s_ * CH + ch_) * 4 + grp
                        k.op(ACT, lambda e, sb_=sb_, grp=grp, col=col: e.activation(out=pc[0:8, grp * 512:(grp + 1) * 512], in_=sb_[0:8, :], func=AF.Exp,
                                                                                   scale=SCALE, accum_out=rsx[0:8, col:col + 1]), r=[sb_], w=[pc, rsx])

                def stageC(c):
                    pc, pp = pch[c % 2], ptp[c % 2]
                    pb, pf = bankb4()
                    pv8 = pf[:, 0:128].rearrange("p (a b) -> p a b", b=8)
                    for tt in range(16):
                        k.op(PE, lambda e, tt=tt, pv8=pv8: e.transpose(pv8[:, tt, :], pc[0:8, tt * 128:(tt + 1) * 128], ident[0:8, 0:8]),
                             r=[pc, ident], w=[pb])
                    k.op(DVE, lambda e, pv8=pv8: e.tensor_copy(out=pp[:, :, :], in_=pv8[:, :, :]), r=[pb], w=[pp])

                def stageD(c, oacc):
                    s_, ch_ = chunks[c]
                    pp, gb_ = ptp[c % 2], Gb[c % NBUF]
                    for tt in range(16):
                        last = (ch_ == CH - 1 and tt == 15)
                        k.op(PE, lambda e, tt=tt, last=last: e.matmul(oacc[0:8, 0:256], lhsT=pp[:, tt, :], rhs=gb_[:, tt, :],
                                                                     start=False, stop=last), r=[pp, gb_], w=[oacc])

                load(0)
                if NCK > 1:
                    load(1)
                stageA(0)
                oacc = None
                for c in range(NCK):
                    s_, ch_ = chunks[c]
                    if ch_ == 0:
                        oacc = bankf(pin=True)
                        k.op(PE, lambda e, s_=s_, oacc=oacc: e.matmul(oacc[0:8, 0:257], lhsT=PnT[:, s_, :], rhs=cknb[:, 0:257], start=True, stop=False),
                             r=[PnT, cknb], w=[oacc])
                    if c + 2 < NCK:
                        load(c + 2)
                    stageB(c)
                    if c + 1 < NCK:
                        stageA(c + 1)
                    stageC(c)
                    stageD(c, oacc)
                    if ch_ == CH - 1:
                        s = s_
                        k.op(DVE, lambda e, s=s: e.reduce_sum(out=den[0:8, 1:2], in_=rsx[0:8, s * CH * 4:(s + 1) * CH * 4], axis=AX.X), r=[rsx], w=[den])
                        k.op(DVE, lambda e, oacc=oacc: e.tensor_tensor(out=den[0:8, 1:2], in0=den[0:8, 1:2], in1=oacc[0:8, 256:257], op=ALU.add), r=[den, oacc], w=[den])
                        k.op(DVE, lambda e: e.reciprocal(out=den[0:8, 1:2], in_=den[0:8, 1:2]), r=[den], w=[den])
                        k.op(DVE, lambda e, oacc=oacc: e.tensor_scalar(out=olb[:, :], in0=oacc[0:8, 0:256], scalar1=den[0:8, 1:2], scalar2=None, op0=ALU.mult),
                             r=[oacc, den], w=[olb])
                        pb, pf = bankb4()
                        for lc in range(2):
                            k.op(PE, lambda e, lc=lc, pf=pf: e.transpose(pf[:, lc * 8:(lc + 1) * 8], olb[0:8, lc * 128:(lc + 1) * 128], ident[0:8, 0:8]),
                                 r=[olb, ident], w=[pb])
                        k.op(DVE, lambda e, pf=pf, s=s: e.tensor_copy(out=OT[:, :, :, s], in_=pf[:, 0:16].rearrange("p (a b) -> p a b", b=8)), r=[pb], w=[OT])
                        unpin(oacc)
                bk = bankf()
                for h in range(8):
                    for lc in range(2):
                        k.op(PE, lambda e, h=h, lc=lc, bk=bk: e.matmul(bk[0:SPC, h * 64:(h + 1) * 64], lhsT=OT[:, lc, h, :], rhs=wuv[:, lc, h * 64:(h + 1) * 64],
                                                                       start=(lc == 0), stop=(lc == 1)), r=[OT, wuv], w=[bk])
                k.op(DVE, lambda e, bk=bk: e.tensor_copy(out=o_all[0:SPC, T, :], in_=bk[0:SPC, :]), r=[bk], w=[o_all])
                unpin(psf[4])
                unpin(psf[5])
            k.dma(POOL, OA[:, :, :].rearrange("t p c -> p t c"), o_all[:, :, :], r=[o_all], w=[OA])
            k.barrier()

            e13.close()
            with PStack("4a") as e4:
                e4.chk()
                st = mk_small(e4)
                mk_gpost(e4, st)
                wmo = k.sb(e4, [128, 4, D], BF16, "wmo")
                wmx = k.sb(e4, [128, KC, D], BF16, "wmx")
                wcq = k.sb(e4, [128, KC, D], BF16, "wcq")
                wcoo = k.sb(e4, [128, KC, D], BF16, "wcoo")
                load_w(wmo, w_mla_out, 4, D)
                load_w(wmx, w_mix_out, KC, D)
                load_w(wcq, w_ca_q, KC, D, gain_col=8)
                load_w(wcoo, w_ca_o, KC, D)
                mkT = k.sb(e4, [128, 8, NMEM], BF16, "mkT")
                mv1 = k.sb(e4, [128, 2, 4, 257], BF16, "mv1")
                xt = [k.sb(e4, [128, D], F32, "xt") for _ in range(2)]
                x1 = k.sb(e4, [128, D], F32, "x1")
                x2 = [k.sb(e4, [128, D], F32, "x2") for _ in range(2)]
                t1 = [k.sb(e4, [128, D], F32, "t1") for _ in range(2)]
                sgm = [k.sb(e4, [128, D], F32, "sgm") for _ in range(2)]
                oT = k.sb(e4, [128, 4, 128], BF16, "oT")
                mg = k.sb(e4, [128, D], BF16, "mg")
                mgT = k.sb(e4, [128, 8, 128], BF16, "mgT")
                xn2T = k.sb(e4, [128, 8, 128], BF16, "xn2T")
                qcT = k.sb(e4, [128, 8, 128], BF16, "qcT")
                pcT = k.sb(e4, [128, 8, 128], BF16, "pcT")
                ocb = k.sb(e4, [128, D], BF16, "ocb")
                ocT = k.sb(e4, [128, 8, 128], BF16, "ocT")
                rec4 = k.sb(e4, [128, 4], F32, "rec4")
                with ExitStack() as em:
                    wck = k.sb(em, [128, KC, D], BF16, "wck")
                    wcv = k.sb(em, [128, KC, D], BF16, "wcv")
                    load_w(wck, w_ca_k, KC, D, gain_col=16)
                    load_w(wcv, w_ca_v, KC, D, gain_col=16)
                    mnT = k.sb(em, [128, 8, NMEM], BF16, "mnT")
                    mtmp = k.sb(em, [128, 8, 128], BF16, "mtmp")
                    mo = [k.sb(em, [128, D], F32, "mo") for _ in range(2)]
                    k.op(POOL, lambda e: e.memset(mv1[:, :, :, 256:257], 1.0), w=[mv1])
                    for mb in range(2):
                        k.dma(SP, xt[mb][:, :], memp[mb * 128:(mb + 1) * 128, :], w=[xt[mb]])
                        normT(st, xt[mb], 128, mtmp)
                        k.op(POOL, lambda e, mb=mb: e.tensor_copy(out=mnT[:, :, mb * 128:(mb + 1) * 128], in_=mtmp[:, :, :]), r=[mtmp], w=[mnT])
                        for wi, (wt, dst) in enumerate(((wck, mk_p), (wcv, mv_p))):
                            mob = mo[wi]
                            for hh in range(2):
                                bk = bankf()
                                mm_tm(mtmp, 128, wt, KC, hh * 512, 512, bk)
                                k.op(ACT, lambda e, bk=bk, hh=hh, mob=mob: e.copy(out=mob[:, hh * 512:(hh + 1) * 512], in_=bk[:, :]), r=[bk], w=[mob])
                                if wi == 1:
                                    k.op(DVE, lambda e, bk=bk, hh=hh, mb=mb: e.tensor_copy(out=mv1[:, mb, 2 * hh:2 * hh + 2, 0:256],
                                                                                          in_=bk[:, :].rearrange("p (a b) -> p a b", b=256)), r=[bk], w=[mv1])
                            out_toks.append(k.dma(POOL, dst[mb * 128:(mb + 1) * 128, :], mob[:, :], r=[mob]))
                    for c8 in range(8):
                        bk = bankf()
                        for c in range(KC):
                            k.op(PE, lambda e, c=c, c8=c8, bk=bk: e.matmul(bk[:, 0:NMEM], lhsT=wck[:, c, c8 * 128:(c8 + 1) * 128], rhs=mnT[:, c, :],
                                                                           start=(c == 0), stop=(c == KC - 1)), r=[wck, mnT], w=[bk])
                        k.op(ACT, lambda e, c8=c8, bk=bk: e.copy(out=mkT[:, c8, :], in_=bk[:, 0:NMEM]), r=[bk], w=[mkT])
                k.barrier()
                ksf = [k.sb(e4, [128, 2, D], F32, "ksf") for _ in range(2)]
                ksb = k.sb(e4, [128, 2, D], BF16, "ksb")
                vsb = k.sb(e4, [128, 2, D], BF16, "vsb")
                ksT = k.sb(e4, [128, 2, 8, 128], BF16, "ksT")
                pcs = k.sb(e4, [128, 2, 4], BF16, "pcs")
                ones = k.sb(e4, [128, 1], BF16, "ones")
                hm = k.sb(e4, [4, D], F32, "hm")
                ovs = k.sb(e4, [4, D], F32, "ovs")
                ocs = k.sb(e4, [4, 256], F32, "ocs")
                ocsb = k.sb(e4, [4, 256], BF16, "ocsb")
                rs4 = k.sb(e4, [4, 1], F32, "rs4")
                k.op(POOL, lambda e: e.memset(ones[:], 1.0), w=[ones])
                k.dma(SP, hm[:], hmask[:, :], w=[hm])

                for t in range(NT):
                    smp = (t == T)
                    R = SPC if smp else 128
                    x_t = xt[t % 2]
                    if smp:
                        k.dma(SP, x_t[0:R, :], xsm[:, :], w=[x_t])
                    else:
                        k.dma(SP, x_t[:, :], xq[t, :, :], w=[x_t])
                    t1b, sgb = t1[t % 2], sgm[t % 2]
                    k.dma(SP, t1b[0:R, :], T1[t, 0:R, :], r=[T1], w=[t1b])
                    k.dma(SP, sgb[0:R, :], SG[t, 0:R, :], r=[SG], w=[sgb])
                    osrc = k.sb(e4, [128, 512], BF16, "osrc") if t == 0 else osrc
                    k.dma(SP, osrc[0:R, :], OA[t, 0:R, :], r=[OA], w=[osrc])
                    transposeT(osrc, R, 512, oT)
                    ymb = [bankf(), bankf()]
                    for hh in range(2):
                        mm_tm(oT, R, wmo, 4, hh * 512, 512, ymb[hh])
                    for hh in range(2):
                        sl = slice(hh * 512, (hh + 1) * 512)
                        k.op(DVE, lambda e, hh=hh, sl=sl, R=R, sgb=sgb: e.tensor_tensor(out=sgb[0:R, sl], in0=sgb[0:R, sl], in1=ymb[hh][0:R, :], op=ALU.mult),
                             r=[sgb, ymb[hh]], w=[sgb])
                        k.op(POOL, lambda e, sl=sl, R=R, sgb=sgb, t1b=t1b: e.tensor_tensor(out=mg[0:R, sl], in0=sgb[0:R, sl], in1=t1b[0:R, sl], op=ALU.add),
                             r=[sgb, t1b], w=[mg])
                    transposeT(mg, R, D, mgT)
                    mxb = [bankf(), bankf()]
                    for hh in range(2):
                        mm_tm(mgT, R, wmx, KC, hh * 512, 512, mxb[hh])
                    postnorm_res(st, mxb, R, 0, x_t, x1)
                    normT(st, x1, R, xn2T)
                    for half in range(2):
                        bk2 = [bankf(), bankf()]
                        for i in range(4):
                            c8 = half * 4 + i
                            bk = bk2[i // 2] if R == 128 else bk2[0]
                            oc = (i % 2) * 128 if R == 128 else i * R
                            for c in range(KC):
                                k.op(PE, lambda e, c=c, c8=c8, bk=bk, oc=oc, R=R: e.matmul(bk[:, oc:oc + R], lhsT=wcq[:, c, c8 * 128:(c8 + 1) * 128], rhs=xn2T[:, c, 0:R],
                                                                                          start=(c == 0), stop=(c == KC - 1)), r=[wcq, xn2T], w=[bk])
                        if R == 128:
                            for i2 in range(2):
                                k.op(ACT, lambda e, i2=i2, half=half, bk2=bk2: e.copy(out=qcT[:, half * 4 + 2 * i2:half * 4 + 2 * i2 + 2, :],
                                                                                      in_=bk2[i2][:, 0:256].rearrange("p (a b) -> p a b", b=128)), r=[bk2[i2]], w=[qcT])
                        else:
                            k.op(ACT, lambda e, half=half, bk2=bk2, R=R: e.copy(out=qcT[:, half * 4:half * 4 + 4, 0:R],
                                                                                in_=bk2[0][:, 0:4 * R].rearrange("p (a b) -> p a b", b=R)), r=[bk2[0]], w=[qcT])
                    if not smp:
                        for hp in range(2):
                            sb_ = bankf()
                            for i in range(4):
                                h, mb = hp * 2 + i // 2, i % 2
                                for ec in range(2):
                                    k.op(PE, lambda e, h=h, mb=mb, ec=ec, i=i, sb_=sb_: e.matmul(sb_[:, i * 128:(i + 1) * 128], lhsT=mkT[:, h * 2 + ec, mb * 128:(mb + 1) * 128],
                                                                                                 rhs=qcT[:, h * 2 + ec, :], start=(ec == 0), stop=(ec == 1)),
                                         r=[mkT, qcT], w=[sb_])
                            k.op(ACT, lambda e, hp=hp, sb_=sb_: e.activation(out=pcT[:, hp * 4:(hp + 1) * 4, :], in_=sb_[:, :].rearrange("p (a b) -> p a b", b=128),
                                                                             func=AF.Exp, scale=1.0 / 16.0), r=[sb_], w=[pcT])
                        for h in range(4):
                            ob = bankf()
                            for mb in range(2):
                                k.op(PE, lambda e, h=h, mb=mb, ob=ob: e.matmul(ob[:, 0:257], lhsT=pcT[:, h * 2 + mb, :], rhs=mv1[:, mb, h, :],
                                                                               start=(mb == 0), stop=(mb == 1)), r=[pcT, mv1], w=[ob])
                            k.op(DVE, lambda e, h=h, ob=ob: e.reciprocal(out=rec4[:, h:h + 1], in_=ob[:, 256:257]), r=[ob], w=[rec4])
                            k.op(DVE, lambda e, h=h, ob=ob: e.tensor_scalar(out=ocb[:, h * 256:(h + 1) * 256], in0=ob[:, 0:256], scalar1=rec4[:, h:h + 1], scalar2=None,
                                                                            op0=ALU.mult), r=[ob, rec4], w=[ocb])
                        transposeT(ocb, 128, D, ocT)
                    else:
                        for s in range(SPC):
                            kf, vf = ksf[0], ksf[1]
                            k.dma(SP, kf[:], cmk[s, :, :].rearrange("(a p) d -> p a d", p=128), w=[kf])
                            k.dma(SP, vf[:], cmv[s, :, :].rearrange("(a p) d -> p a d", p=128), w=[vf])
                            k.op(POOL, lambda e, kf=kf: e.tensor_copy(out=ksb[:], in_=kf[:]), r=[kf], w=[ksb])
                            k.op(DVE, lambda e, vf=vf: e.tensor_copy(out=vsb[:], in_=vf[:]), r=[vf], w=[vsb])
                            for mb in range(2):
                                pb = bankb()
                                pv = pb[:].rearrange("p (a b) -> p a b", b=128)
                                for c8 in range(8):
                                    k.op(PE, lambda e, c8=c8, mb=mb, pv=pv, pb=pb: e.transpose(pv[:, c8, :], ksb[:, mb, c8 * 128:(c8 + 1) * 128], ident[:, :]),
                                         r=[ksb, ident], w=[pb])
                                k.op(ACT, lambda e, mb=mb, pv=pv: e.copy(out=ksT[:, mb, :, :], in_=pv[:, :, :]), r=[pb], w=[ksT])
                            sb_ = bankf()
                            for mb in range(2):
                                for h in range(4):
                                    for ec in range(2):
                                        k.op(PE, lambda e, mb=mb, h=h, ec=ec, s=s, sb_=sb_: e.matmul(sb_[:, mb * 4 + h:mb * 4 + h + 1], lhsT=ksT[:, mb, h * 2 + ec, :],
                                                                                                     rhs=qcT[:, h * 2 + ec, s:s + 1], start=(ec == 0), stop=(ec == 1)),
                                             r=[ksT, qcT], w=[sb_])
                            k.op(ACT, lambda e, sb_=sb_: e.activation(out=pcs[:, :, :].rearrange("p a b -> p (a b)"), in_=sb_[:, 0:8], func=AF.Exp, scale=1.0 / 16.0),
                                 r=[sb_], w=[pcs])
                            ovb = [bankf(), bankf()]
                            for hh in range(2):
                                for mb in range(2):
                                    k.op(PE, lambda e, hh=hh, mb=mb, ovb=ovb: e.matmul(ovb[hh][0:4, :], lhsT=pcs[:, mb, :], rhs=vsb[:, mb, hh * 512:(hh + 1) * 512],
                                                                                       start=(mb == 0), stop=(mb == 1)), r=[pcs, vsb], w=[ovb[hh]])
                            rb_ = bankf()
                            for mb in range(2):
                                k.op(PE, lambda e, mb=mb, rb_=rb_: e.matmul(rb_[0:4, 0:1], lhsT=pcs[:, mb, :], rhs=ones[:, 0:1], start=(mb == 0), stop=(mb == 1)),
                                     r=[pcs, ones], w=[rb_])
                            for hh in range(2):
                                k.op(DVE, lambda e, hh=hh, ovb=ovb: e.tensor_tensor(out=ovs[:, hh * 512:(hh + 1) * 512], in0=ovb[hh][0:4, :], in1=hm[:, hh * 512:(hh + 1) * 512],
                                                                                   op=ALU.mult), r=[ovb[hh], hm], w=[ovs])
                            k.op(DVE, lambda e: e.tensor_reduce(out=ocs[:, :], in_=ovs[:, :].rearrange("p (h e) -> p e h", h=4), axis=AX.X, op=ALU.add), r=[ovs], w=[ocs])
                            k.op(DVE, lambda e, rb_=rb_: e.reciprocal(out=rs4[:, :], in_=rb_[0:4, 0:1]), r=[rb_], w=[rs4])
                            k.op(DVE, lambda e: e.tensor_scalar(out=ocsb[:, :], in0=ocs[:, :], scalar1=rs4[:, 0:1], scalar2=None, op0=ALU.mult), r=[ocs, rs4], w=[ocsb])
                            pb = bankb()
                            for ec in range(2):
                                k.op(PE, lambda e, ec=ec, pb=pb: e.transpose(pb[:, ec * 4:(ec + 1) * 4], ocsb[0:4, ec * 128:(ec + 1) * 128], ident[0:4, 0:4]),
                                     r=[ocsb, ident], w=[pb])
                            k.op(DVE, lambda e, pb=pb, s=s: e.tensor_copy(out=ocT[:, :, s].rearrange("p (h e) -> p e h", e=2),
                                                                          in_=pb[:, 0:8].rearrange("p (e h) -> p e h", h=4)), r=[pb], w=[ocT])
                    cab = [bankf(), bankf()]
                    for hh in range(2):
                        mm_tm(ocT, R, wcoo, KC, hh * 512, 512, cab[hh])
                    x2b = x2[t % 2]
                    postnorm_res(st, cab, R, D, x1, x2b)
                    if os.environ.get("K_DBG", "") == "x1":
                        k.dma(POOL, X2[t, 0:R, :], x1[0:R, :], r=[x1], w=[X2])
                    else:
                        k.dma(POOL, X2[t, 0:R, :], x2b[0:R, :], r=[x2b], w=[X2])
            k.barrier()

        with PStack("4b") as e5:
            e5.chk()
            st = mk_small(e5)
            mk_gpost(e5, st, 2 * D, 3 * D)
            wup = k.sb(e5, [128, KC, 4096], BF16, "wup")
            wdn = k.sb(e5, [128, 32, D], BF16, "wdn")
            load_w(wup, w_ff_up, KC, 4096, gain_col=24)
            load_w(wdn, w_ff_down, 32, D)
            x2 = [k.sb(e5, [128, D], F32, "x2") for _ in range(2)]
            yo = k.sb(e5, [128, D], F32, "yo")
            xn3T = k.sb(e5, [128, 8, 256], BF16, "xn3T")
            hr = [k.sb(e5, [128, 512], F32, "hr") for _ in range(2)]
            hT = k.sb(e5, [128, 32, 256], BF16, "hT")
            groups = [[(t_, 128) for t_ in range(g_, g_ + 2)] for g_ in range(0, T, 2)] + [[(T, SPC)]]
            hri = 0
            for grp in groups:
                W = sum(R_ for _, R_ in grp)
                offs = []
                o_ = 0
                for gi, (t, R) in enumerate(grp):
                    offs.append(o_)
                    k.dma(SP, x2[gi][0:R, :], X2[t, 0:R, :], r=[X2], w=[x2[gi]])
                    normT(st, x2[gi], R, xn3T, c0=o_)
                    o_ += R
                per = min(4, 512 // W)
                for f0 in range(0, 32, per):
                    bk = bankf()
                    for i in range(per):
                        fc = f0 + i
                        for c in range(KC):
                            k.op(PE, lambda e, c=c, fc=fc, i=i, bk=bk, W=W: e.matmul(bk[:, i * W:(i + 1) * W], lhsT=wup[:, c, fc * 128:(fc + 1) * 128], rhs=xn3T[:, c, 0:W],
                                                                                    start=(c == 0), stop=(c == KC - 1)), r=[wup, xn3T], w=[bk])
                    hri += 1
                    hrb = hr[hri % 2]
                    k.op(ACT, lambda e, bk=bk, hrb=hrb, W=W, per=per: e.activation(out=hrb[:, 0:per * W], in_=bk[:, 0:per * W], func=AF.Relu), r=[bk], w=[hrb])
                    k.op(POOL if hri % 2 else DVE, lambda e, hrb=hrb, f0=f0, W=W, per=per: e.tensor_tensor(
                        out=hT[:, f0:f0 + per, 0:W], in0=hrb[:, 0:per * W].rearrange("p (a b) -> p a b", b=W),
                        in1=hrb[:, 0:per * W].rearrange("p (a b) -> p a b", b=W), op=ALU.mult), r=[hrb], w=[hT])
                for gi, (t, R) in enumerate(grp):
                    o0 = offs[gi]
                    fb = [bankf(), bankf()]
                    for hh in range(2):
                        for c in range(32):
                            k.op(PE, lambda e, c=c, hh=hh, o0=o0, R=R, fb=fb: e.matmul(fb[hh][0:R, 0:512], lhsT=hT[:, c, o0:o0 + R], rhs=wdn[:, c, hh * 512:(hh + 1) * 512],
                                                                                      start=(c == 0), stop=(c == 31)), r=[hT, wdn], w=[fb[hh]])
                    postnorm_res(st, fb, R, 0, x2[gi], yo)
                    if t == T:
                        out_toks.append(k.dma(POOL, y_s[:, :], yo[0:R, :], r=[yo]))
                    else:
                        out_toks.append(k.dma(POOL, y_p[t, :, :], yo[:, :], r=[yo]))
        for tk in out_toks:
            k._wait(SP, tk)
        k.barrier()
    return nc


def _rope_tables(pos):
    inv = 1.0 / (10000.0 ** (np.arange(0, 32, 2, dtype=np.float32) / 32.0))
    ang = pos.astype(np.float32)[:, None] * inv[None, :].astype(np.float32)
    return np.cos(ang).astype(np.float32), np.sin(ang).astype(np.float32)


_CACHE = {}


def kernel(**inp):
    x_prompt = np.asarray(inp["x_prompt"]); x_sample = np.asarray(inp["x_sample"])
    Bp, SEQ, _ = x_prompt.shape
    DEC = x_sample.shape[0]
    NPOOL, PS = inp["cache_ckv"].shape[1], inp["cache_ckv"].shape[2]
    NPG = inp["page_table"].shape[1]
    assert Bp == 2 and NPG == 128 and DEC % 8 == 0
    NB = SEQ // 128
    T = NB // 4
    SPC = DEC // 8
    past_len = NPG * PS
    key = (T, NB, SPC, PS, NPOOL)
    if key not in _CACHE:
        _CACHE[key] = build(*key)
    nc = _CACHE[key]

    f32 = lambda a: np.ascontiguousarray(np.asarray(a, dtype=np.float32))
    cos_all, sin_all = _rope_tables(np.arange(SEQ))
    cos_s, sin_s = _rope_tables(np.array([past_len]))
    pool_ckv = f32(inp["cache_ckv"][0]).reshape(NPOOL * (PS // 16), 4096)
    pool_kr = f32(inp["cache_krope"][0]).reshape(NPOOL * (PS // 16), 512)
    gp = np.zeros((128, 40), np.float32)
    for i, nm in enumerate(("norm_mix_pre_g", "norm_ca_pre_g", "mem_norm_g", "norm_mlp_pre_g")):
        gp[:, i * 8:(i + 1) * 8] = f32(inp[nm][0]).reshape(8, 128).T
    gp[:, 32:35] = f32(inp["q_norm_g"][0]).reshape(3, 128).T
    gbc = np.concatenate([f32(inp["norm_mix_post_g"][0]), f32(inp["norm_ca_post_g"][0]), f32(inp["norm_mlp_post_g"][0]),
                          f32(inp["kv_norm_g"][0])])[None, :].repeat(128, 0)
    conv_wp = np.ascontiguousarray(f32(inp["conv_w"][0]).reshape(3, 4, 128).transpose(2, 1, 0).reshape(128, 12))
    bd = np.zeros((SPC, SPC, 8), np.float32)
    for s in range(SPC):
        bd[s, s, :] = 1.0
    hm = np.zeros((4, 4, 256), np.float32)
    for h in range(4):
        hm[h, h, :] = 1.0
    shared = {
        "pool_ckv": pool_ckv, "pool_kr": pool_kr,
        "ropek": np.ascontiguousarray(np.concatenate([cos_all, sin_all], 1).reshape(NB, 128, 32).transpose(1, 0, 2).reshape(128, NB * 32)),
        "bdmask": bd.reshape(SPC, SPC * 8), "hmask": hm.reshape(4, 1024),
        "w_in": f32(inp["w_in"][0]), "conv_wp": conv_wp, "w_conv_out": f32(inp["w_conv_out"][0]),
        "w_uq": f32(inp["w_uq"][0]).reshape(384, 768), "w_uk": f32(inp["w_uk"][0]).reshape(256, 512),
        "w_uv": f32(inp["w_uv"][0]).reshape(256, 512), "w_mla_out": f32(inp["w_mla_out"][0]),
        "w_mix_out": f32(inp["w_mix_out"][0]), "w_ca_q": f32(inp["w_ca_q"][0]).reshape(D, D),
        "w_ca_k": f32(inp["w_ca_k"][0]).reshape(D, D), "w_ca_v": f32(inp["w_ca_v"][0]).reshape(D, D),
        "w_ca_o": f32(inp["w_ca_o"][0]).reshape(D, D), "w_ff_up": f32(inp["w_ff_up"][0]),
        "w_ff_down": f32(inp["w_ff_down"][0]), "gpart": gp, "gbc": np.ascontiguousarray(gbc),
    }
    ropes = np.concatenate([np.tile(cos_s, (1, 8)), np.tile(sin_s, (1, 8)), cos_s, sin_s], 1).repeat(SPC, 0)
    in_maps = []
    kk = np.arange(128)[:, None]
    qq = np.arange(128)[None, :]
    tri = (kk <= qq).astype(np.float32)
    for c in range(8):
        b, j = c // 4, c % 4
        xb = f32(x_prompt[b]).reshape(NB, 128, D)
        blocks = [4 * t + j for t in range(T)]
        xq = np.ascontiguousarray(xb[blocks])
        xh = np.zeros((2 * T, D), np.float32)
        for t, g in enumerate(blocks):
            if g > 0:
                xh[2 * t:2 * t + 2] = xb[g - 1, 126:128]
        masks = np.zeros((16, 128, 512), np.float32)
        for d in range(16):
            for qi in range(4):
                lim = 4 * qi + j
                if d < lim:
                    masks[d, :, qi * 128:(qi + 1) * 128] = 1.0
                elif d == lim:
                    masks[d, :, qi * 128:(qi + 1) * 128] = tri
        cq = cos_all.reshape(NB, 128, 16)[blocks]
        sq = sin_all.reshape(NB, 128, 16)[blocks]
        ropeq = np.concatenate([np.tile(cq, (1, 1, 8)), np.tile(sq, (1, 1, 8))], 2)
        sl = slice(c * SPC, (c + 1) * SPC)
        m = dict(shared)
        m.update({
            "xq": xq, "xh": xh, "xs": np.ascontiguousarray(xb), "xsm": f32(x_sample[sl, 0, :]),
            "memp": f32(inp["mem_prompt"][b]), "stc": f32(inp["state_conv"][0, sl]),
            "cmk": f32(inp["cache_mem_k"][0, sl]).reshape(SPC, NMEM, D), "cmv": f32(inp["cache_mem_v"][0, sl]).reshape(SPC, NMEM, D),
            "ptT": np.ascontiguousarray(np.concatenate([np.asarray(inp["page_table"])[sl].T.astype(np.int32) * (PS // 16) + ch_ for ch_ in range(PS // 16)], 1)),
            "masks": masks, "ropeq": np.ascontiguousarray(ropeq.astype(np.float32)), "ropes": np.ascontiguousarray(ropes.astype(np.float32)),
        })
        in_maps.append(m)
    res = run_bass_kernel_spmd(nc, in_maps, core_ids=list(range(8))).results

    y_prompt = np.zeros((2, SEQ, D), np.float32)
    ckv_prompt = np.zeros((1, 2, SEQ, 256), np.float32)
    kr_prompt = np.zeros((1, 2, SEQ, 32), np.float32)
    for c in range(8):
        b, j = c // 4, c % 4
        for t in range(T):
            g = 4 * t + j
            y_prompt[b, g * 128:(g + 1) * 128] = res[c]["y_p"][t]
            ckv_prompt[0, b, g * 128:(g + 1) * 128] = res[c]["ckv_p"][t]
            kr_prompt[0, b, g * 128:(g + 1) * 128] = res[c]["kr_p"][t]
    y_sample = np.concatenate([res[c]["y_s"] for c in range(8)], 0).reshape(DEC, 1, D)
    conv_prompt = np.stack([res[3]["conv_p"], res[7]["conv_p"]], 0)[None]
    mem_k = np.stack([res[0]["mk_p"], res[4]["mk_p"]], 0).reshape(1, 2, NMEM, 4, 256)
    mem_v = np.stack([res[0]["mv_p"], res[4]["mv_p"]], 0).reshape(1, 2, NMEM, 4, 256)
    ckv_sample = np.concatenate([res[c]["ckv_s"] for c in range(8)], 0).reshape(1, DEC, 1, 256)
    kr_sample = np.concatenate([res[c]["kr_s"] for c in range(8)], 0).reshape(1, DEC, 1, 32)
    conv_sample = np.concatenate([res[c]["conv_s"] for c in range(8)], 0).reshape(1, DEC, 2, 512)
    return (y_prompt, y_sample, ckv_prompt, kr_prompt, conv_prompt.astype(np.float32), mem_k, mem_v,
            ckv_sample, kr_sample, conv_sample)
```

```python
import numpy as np
from contextlib import ExitStack
import concourse.bass as bass
import concourse.mybir as mybir
from concourse.bass_utils import run_bass_kernel_spmd

F32 = mybir.dt.float32
BF16 = mybir.dt.bfloat16
I32 = mybir.dt.int32
AF = mybir.ActivationFunctionType
ALU = mybir.AluOpType
AX = mybir.AxisListType

D = 1024
KC = 8
OFF_H, OFF_GB, OFF_GC, OFF_CQ, OFF_CKV, OFF_KR, OFF_GCONV, OFF_GMLA, IN_COLS = (
    0, 512, 1024, 1536, 1920, 2176, 2208, 3232, 4256)
EPS = 1e-6
NS = 8
SCALE = 96 ** -0.5
NMEM = 256


import os
_PH = os.environ.get("K_PHASES", "2,1,3,3s,4a,4b").split(",")


class _Skip(Exception):
    pass


class PStack(ExitStack):
    def __init__(self, name):
        super().__init__()
        self.pname = name

    def chk(self):
        if self.pname not in _PH:
            raise _Skip()

    def __exit__(self, et, ev, tb):
        r = super().__exit__(None if et is _Skip else et, None if et is _Skip else ev, None if et is _Skip else tb)
        return True if et is _Skip else r


class Tok:
    __slots__ = ("sem", "val", "eng")

    def __init__(self, sem, val, eng):
        self.sem, self.val, self.eng = sem, val, eng


class Buf:
    def __init__(self, t, psum=False):
        self.t = t
        self.lw = None
        self.rd = {}
        self.wd = {}
        self.psum = psum

    def __getitem__(self, k):
        return self.t[k]


class Eng:
    def __init__(self, name, e, sem):
        self.name, self.e, self.sem = name, e, sem
        self.count = 0
        self.waited = {}


class B:
    def __init__(self, nc, es):
        self.nc, self.es = nc, es

        def mk(n, e):
            return Eng(n, e, es.enter_context(nc.semaphore("sem_" + n)))
        self.PE = mk("pe", nc.tensor)
        self.ACT = mk("act", nc.scalar)
        self.DVE = mk("dve", nc.vector)
        self.POOL = mk("pool", nc.gpsimd)
        self.SP = mk("sp", nc.sync)
        self.engs = [self.PE, self.ACT, self.DVE, self.POOL, self.SP]
        self.dq = {}
        for q in (self.SP, self.POOL, self.ACT):
            self.dq[q.name] = ([es.enter_context(nc.semaphore("d_%s%d" % (q.name, i))) for i in range(NS)], [0])
        self.uid = 0

    def sb(self, stack, shape, dt, name=None):
        self.uid += 1
        return Buf(stack.enter_context(self.nc.sbuf_tensor("%s_%d" % (name or "t", self.uid), list(shape), dt)))

    def psum(self, stack, shape, dt, name=None):
        self.uid += 1
        return Buf(stack.enter_context(self.nc.psum_tensor("%s_%d" % (name or "p", self.uid), list(shape), dt)), psum=True)

    def _wait(self, E, tok):
        if tok is None:
            return
        if tok.eng is E and E is self.PE:
            return
        k = tok.sem.num
        if E.waited.get(k, -1) >= tok.val:
            return
        E.e.wait_ge(tok.sem, tok.val)
        E.waited[k] = tok.val

    def _deps(self, E, r, w, wd=()):
        for b in r:
            self._wait(E, b.lw)
            for t in list(b.wd.values()):
                self._wait(E, t)
        for b in w:
            self._wait(E, b.lw)
            for t in list(b.wd.values()):
                self._wait(E, t)
            for t in list(b.rd.values()):
                self._wait(E, t)
        for b in wd:
            self._wait(E, b.lw)
            for t in list(b.rd.values()):
                self._wait(E, t)

    def _upd(self, tok, r, w, wd=()):
        for b in r:
            b.rd[tok.sem.num] = tok
        for b in w:
            b.lw = tok
            b.rd = {}
            b.wd = {}
        for b in wd:
            b.wd[tok.sem.num] = tok

    def op(self, E, fn, r=(), w=(), wd=()):
        w = list(w) + [b for b in r if b.psum and b not in w]
        r = [b for b in r if b not in w]
        self._deps(E, r, w, wd)
        inst = fn(E.e)
        E.count += 1
        inst.then_inc(E.sem, 1)
        tok = Tok(E.sem, E.count, E)
        self._upd(tok, r, w, wd)
        return tok

    def dma(self, Q, out_ap, in_ap, r=(), w=(), idx=None, slow=False):
        sems, ctr = self.dq[Q.name]
        i = ctr[0]
        ctr[0] += 1
        sem = sems[i % NS]
        prev = 16 * (i // NS)
        if prev > 0 and Q.waited.get(sem.num, -1) < prev:
            Q.e.wait_ge(sem, prev)
            Q.waited[sem.num] = prev
        self._deps(Q, r, w)
        if idx is None:
            if slow:
                inst = Q.e.dma_start(out=out_ap, in_=in_ap, allow_slow_non_contiguous=True)
            else:
                inst = Q.e.dma_start(out=out_ap, in_=in_ap)
        else:
            inst = Q.e.indirect_dma_start(out=out_ap, out_offset=None, in_=in_ap,
                                          in_offset=bass.IndirectOffsetOnAxis(ap=idx, axis=0))
        inst.then_inc(sem, 16)
        tok = Tok(sem, prev + 16, None)
        self._upd(tok, r, w)
        return tok

    def barrier(self):
        toks = [Tok(E.sem, E.count, None) for E in self.engs if E.count > 0]
        for name, (sems, ctr) in self.dq.items():
            n = ctr[0]
            for k, sem in enumerate(sems):
                cnt = (n - k + NS - 1) // NS if n > k else 0
                if cnt > 0:
                    toks.append(Tok(sem, 16 * cnt, None))
        for E in self.engs:
            for t in toks:
                self._wait(E, t)


def build(T, NB, SPC, PS, NPOOL):
    assert T % 4 == 0 and PS % 16 == 0
    CH = PS // 16
    NG = T // 4
    S = NB * 128
    nc = bass.Bass("TRN2", target_bir_lowering=False)

    def din(name, shape, dt=F32):
        return nc.dram_tensor(name, list(shape), dt, kind="ExternalInput").ap()

    def dout(name, shape):
        return nc.dram_tensor(name, list(shape), F32, kind="ExternalOutput").ap()

    def dscr(name, shape, dt=F32):
        return Buf(nc.dram_tensor(name, list(shape), dt, kind="Internal").ap())

    xq = din("xq", [T, 128, D]); xh = din("xh", [2 * T, D]); xs = din("xs", [NB, 128, D])
    xsm = din("xsm", [SPC, D]); memp = din("memp", [NMEM, D])
    pool_ckv = din("pool_ckv", [NPOOL * CH, 4096]); pool_kr = din("pool_kr", [NPOOL * CH, 512])
    stc = din("stc", [SPC, 2, 512]); cmk = din("cmk", [SPC, NMEM, D]); cmv = din("cmv", [SPC, NMEM, D])
    ptT = din("ptT", [128, CH * SPC], I32)
    masks = din("masks", [16, 128, 512])
    ropeq = din("ropeq", [T, 128, 256])
    ropek = din("ropek", [128, NB * 32])
    ropes = din("ropes", [SPC, 288])
    bdmask = din("bdmask", [SPC, SPC * 8])
    hmask = din("hmask", [4, 1024])
    w_in = din("w_in", [D, IN_COLS]); conv_wp = din("conv_wp", [128, 12])
    w_conv_out = din("w_conv_out", [512, D]); w_uq = din("w_uq", [384, 768])
    w_uk = din("w_uk", [256, 512]); w_uv = din("w_uv", [256, 512])
    w_mla_out = din("w_mla_out", [512, D]); w_mix_out = din("w_mix_out", [D, D])
    w_ca_q = din("w_ca_q", [D, D]); w_ca_k = din("w_ca_k", [D, D]); w_ca_v = din("w_ca_v", [D, D])
    w_ca_o = din("w_ca_o", [D, D]); w_ff_up = din("w_ff_up", [D, 4096]); w_ff_down = din("w_ff_down", [4096, D])
    gpart = din("gpart", [128, 5 * 8])
    gbc = din("gbc", [128, 3 * D + 256])

    y_p = dout("y_p", [T, 128, D]); y_s = dout("y_s", [SPC, D])
    ckv_p = dout("ckv_p", [T, 128, 256]); kr_p = dout("kr_p", [T, 128, 32])
    conv_p = dout("conv_p", [2, 512]); mk_p = dout("mk_p", [NMEM, D]); mv_p = dout("mv_p", [NMEM, D])
    ckv_s = dout("ckv_s", [SPC, 256]); kr_s = dout("kr_s", [SPC, 32]); conv_s = dout("conv_s", [SPC, 2, 512])

    NT = T + 1
    T1 = dscr("scr_t1", [NT, 128, D]); SG = dscr("scr_sg", [NT, 128, D]); X2 = dscr("scr_x2", [NT, 128, D])
    OA = dscr("scr_oa", [NT, 128, 512], BF16)

    with ExitStack() as es:
        k = B(nc, es)
        PE, ACT, DVE, POOL, SP = k.PE, k.ACT, k.DVE, k.POOL, k.SP
        out_toks = []
        rr = [0]

        def alt():
            rr[0] += 1
            return DVE if rr[0] % 2 else POOL

        ident = k.sb(es, [128, 128], BF16, "ident")
        identf = k.sb(es, [128, 128], F32, "identf")
        gp = k.sb(es, [128, 40], F32, "gp")
        gkv = k.sb(es, [128, 256], F32, "gkv")
        cw = k.sb(es, [128, 12], F32, "cw")
        psf = [k.psum(es, [128, 512], F32, "psf") for _ in range(6)]
        psb = [k.psum(es, [128, 1024], BF16, "psb") for _ in range(2)]
        pfi = [0]
        pbi = [0]

        pinned = set()

        def bankf(pin=False):
            while True:
                pfi[0] += 1
                b = psf[pfi[0] % 6]
                if id(b) not in pinned:
                    break
            if pin:
                pinned.add(id(b))
            return b

        def unpin(b):
            pinned.discard(id(b))

        def bankb():
            pbi[0] += 1
            return psb[pbi[0] % 2]

        k.op(POOL, lambda e: e.memset(ident[:], 1.0), w=[ident])
        k.op(POOL, lambda e: e.affine_select(out=ident[:], in_=ident[:], pattern=[[-1, 128]], compare_op=ALU.is_equal,
                                             fill=0.0, base=0, channel_multiplier=1), r=[ident], w=[ident])
        k.op(POOL, lambda e: e.memset(identf[:], 1.0), w=[identf])
        k.op(POOL, lambda e: e.affine_select(out=identf[:], in_=identf[:], pattern=[[-1, 128]], compare_op=ALU.is_equal,
                                             fill=0.0, base=0, channel_multiplier=1), r=[identf], w=[identf])
        k.dma(SP, gp[:], gpart[:, :], w=[gp])
        k.dma(SP, gkv[:], gbc[:, 3 * D:3 * D + 256], w=[gkv])
        k.dma(SP, cw[:], conv_wp[:, :], w=[cw])

        stg = [k.sb(es, [128, 1024], F32, "stg") for _ in range(4)]
        stq = [(stg[i], 0) for i in range(4)]
        sti = [0]

        def load_w(dst, src, kcw, ncols, gain_col=None, col0=0, dcol0=0):
            for kc_ in range(kcw):
                for c0 in range(0, ncols, 1024):
                    n = min(1024, ncols - c0)
                    sti[0] += 1
                    i = sti[0]
                    st, so = stq[i % 4]
                    sv = st[:, so:so + n]
                    k.dma(SP, sv, src[kc_ * 128:(kc_ + 1) * 128, col0 + c0:col0 + c0 + n], w=[st])
                    o = dst[:, kc_, dcol0 + c0:dcol0 + c0 + n]
                    if gain_col is None:
                        E = (DVE, POOL, ACT)[i % 3]
                        if E is ACT:
                            k.op(ACT, lambda e, o=o, sv=sv: e.copy(out=o, in_=sv), r=[st], wd=[dst])
                        else:
                            k.op(E, lambda e, o=o, sv=sv: e.tensor_copy(out=o, in_=sv), r=[st], wd=[dst])
                    else:
                        g = gp[:, gain_col + kc_:gain_col + kc_ + 1]
                        if i % 2:
                            k.op(DVE, lambda e, o=o, sv=sv, g=g: e.tensor_scalar(out=o, in0=sv, scalar1=g, scalar2=None, op0=ALU.mult),
                                 r=[st, gp], wd=[dst])
                        else:
                            k.op(ACT, lambda e, o=o, sv=sv, g=g: e.activation(out=o, in_=sv, func=AF.Copy, scale=g), r=[st, gp], wd=[dst])

        def rstd_from_ss(ss_ap, out_ap, ssb, outb, n, R):
            k.op(ACT, lambda e: e.activation(out=out_ap, in_=ss_ap, func=AF.Sqrt, bias=EPS, scale=1.0 / n), r=[ssb], w=[outb])
            k.op(DVE, lambda e: e.reciprocal(out=out_ap, in_=out_ap), r=[outb], w=[outb])

        def normT(st, xt, R, dstT, c0=0):
            junk, ss, rs, xb = st["junk"], st["ss"], st["rs"], st["xb"]
            k.op(ACT, lambda e: e.activation(out=junk[0:R, :], in_=xt[0:R, :], func=AF.Square, accum_out=ss[0:R, 0:1]),
                 r=[xt], w=[junk, ss])
            rstd_from_ss(ss[0:R, 0:1], rs[0:R, 0:1], ss, rs, D, R)
            k.op(DVE, lambda e: e.tensor_scalar(out=xb[0:R, :], in0=xt[0:R, :], scalar1=rs[0:R, 0:1], scalar2=None, op0=ALU.mult),
                 r=[xt, rs], w=[xb])
            pb = bankb()
            pv = pb[:].rearrange("p (a b) -> p a b", b=128)
            for c in range(8):
                k.op(PE, lambda e, c=c: e.transpose(pv[:, c, 0:R], xb[0:R, c * 128:(c + 1) * 128], ident[0:R, 0:R]),
                     r=[xb, ident], w=[pb])
            k.op(ACT, lambda e: e.copy(out=dstT[:, :, c0:c0 + R], in_=pv[:, :, 0:R]), r=[pb], w=[dstT])

        def transposeT(src, R, ncol, dstT, E=None):
            nchunk = ncol // 128
            pb = bankb()
            pv = pb[:].rearrange("p (a b) -> p a b", b=128)
            for c in range(nchunk):
                k.op(PE, lambda e, c=c: e.transpose(pv[:, c, 0:R], src[0:R, c * 128:(c + 1) * 128], ident[0:R, 0:R]),
                     r=[src, ident], w=[pb])
            k.op(E or DVE, lambda e: e.tensor_copy(out=dstT[:, 0:nchunk, 0:R], in_=pv[:, 0:nchunk, 0:R]), r=[pb], w=[dstT])

        def mm_tm(xT, R, wt, kcw, col0, ncols, bank, bcol0=0):
            for c in range(kcw):
                k.op(PE, lambda e, c=c: e.matmul(bank[0:R, bcol0:bcol0 + ncols], lhsT=xT[:, c, 0:R],
                                                 rhs=wt[:, c, col0:col0 + ncols], start=(c == 0), stop=(c == kcw - 1)),
                     r=[xT, wt], w=[bank])

        def ckv_post(st, bank, R, cos_ap, sin_ap, ropeb, out_ck, out_kr, outb):
            junk, ss, rs = st["junk"], st["ss2"], st["rs2"]
            k.op(ACT, lambda e: e.activation(out=junk[0:R, 0:256], in_=bank[0:R, 0:256], func=AF.Square, accum_out=ss[0:R, 0:1]),
                 r=[bank], w=[junk, ss])
            rstd_from_ss(ss[0:R, 0:1], rs[0:R, 0:1], ss, rs, 256, R)
            k.op(DVE, lambda e: e.scalar_tensor_tensor(out=out_ck, in0=bank[0:R, 0:256], scalar=rs[0:R, 0:1],
                                                       in1=gkv[0:R, 0:256], op0=ALU.mult, op1=ALU.mult),
                 r=[bank, rs, gkv], w=[outb])
            kr = st["kr"]
            tm = st["tm"]
            k.op(ACT, lambda e: e.copy(out=kr[0:R, 0:32], in_=bank[0:R, 256:288]), r=[bank], w=[kr])
            k.op(DVE, lambda e: e.tensor_tensor(out=tm[0:R, 0:16], in0=kr[0:R, 0:16], in1=cos_ap, op=ALU.mult), r=[kr, ropeb], w=[tm])
            k.op(DVE, lambda e: e.tensor_tensor(out=tm[0:R, 16:32], in0=kr[0:R, 16:32], in1=sin_ap, op=ALU.mult), r=[kr, ropeb], w=[tm])
            k.op(DVE, lambda e: e.tensor_tensor(out=out_kr[:, 0:16], in0=tm[0:R, 0:16], in1=tm[0:R, 16:32], op=ALU.subtract),
                 r=[tm], w=[outb])
            k.op(DVE, lambda e: e.tensor_tensor(out=tm[0:R, 32:48], in0=kr[0:R, 0:16], in1=sin_ap, op=ALU.mult), r=[kr, ropeb], w=[tm])
            k.op(DVE, lambda e: e.tensor_tensor(out=tm[0:R, 48:64], in0=kr[0:R, 16:32], in1=cos_ap, op=ALU.mult), r=[kr, ropeb], w=[tm])
            k.op(DVE, lambda e: e.tensor_tensor(out=out_kr[:, 16:32], in0=tm[0:R, 32:48], in1=tm[0:R, 48:64], op=ALU.add),
                 r=[tm], w=[outb])

        def postnorm_res(st, banks, R, gcol, xres, xout):
            gb = st["gpost"]
            junk, ss, rs = st["junk"], st["ss3"], st["rs3"]
            for hh in range(2):
                k.op(ACT, lambda e, hh=hh: e.activation(out=junk[0:R, hh * 512:(hh + 1) * 512], in_=banks[hh][0:R, :], func=AF.Square,
                                                       accum_out=ss[0:R, hh:hh + 1]), r=[banks[hh]], w=[junk, ss])
            k.op(DVE, lambda e: e.tensor_tensor(out=ss[0:R, 2:3], in0=ss[0:R, 0:1], in1=ss[0:R, 1:2], op=ALU.add), r=[ss], w=[ss])
            rstd_from_ss(ss[0:R, 2:3], rs[0:R, 0:1], ss, rs, D, R)
            for hh in range(2):
                sl = slice(hh * 512, (hh + 1) * 512)
                k.op(DVE, lambda e, hh=hh, sl=sl: e.scalar_tensor_tensor(out=junk[0:R, sl], in0=banks[hh][0:R, :], scalar=rs[0:R, 0:1],
                                                                        in1=gb[0:R, gcol + hh * 512:gcol + (hh + 1) * 512],
                                                                        op0=ALU.mult, op1=ALU.mult), r=[banks[hh], rs, gb], w=[junk])
                k.op(POOL, lambda e, sl=sl: e.tensor_tensor(out=xout[0:R, sl], in0=junk[0:R, sl], in1=xres[0:R, sl], op=ALU.add),
                     r=[junk, xres], w=[xout])

        def mk_small(stack):
            return {
                "junk": k.sb(stack, [128, D], F32, "junk"), "xb": k.sb(stack, [128, D], BF16, "xb"),
                "ss": k.sb(stack, [128, 1], F32, "ss"), "rs": k.sb(stack, [128, 1], F32, "rs"),
                "ss2": k.sb(stack, [128, 1], F32, "ss2"), "rs2": k.sb(stack, [128, 1], F32, "rs2"),
                "ss3": k.sb(stack, [128, 4], F32, "ss3"), "rs3": k.sb(stack, [128, 1], F32, "rs3"),
                "kr": k.sb(stack, [128, 32], F32, "kr"), "tm": k.sb(stack, [128, 64], F32, "tm"),
            }

        def mk_gpost(stack, st, lo=0, hi=3 * D):
            g = k.sb(stack, [128, hi - lo], F32, "gpost")
            k.dma(SP, g[:], gbc[:, lo:hi], w=[g])
            st["gpost"] = g

        with ExitStack() as e13:
            qT = k.sb(e13, [128, 8, T * 128], BF16, "qT")
            qlT = k.sb(e13, [128, 2, SPC, 8], BF16, "qlT")
            qrT = k.sb(e13, [32, SPC, 8], BF16, "qrT")
            cknb = k.sb(e13, [SPC, 257], BF16, "cknb")
            cknT = k.sb(e13, [128, 2, SPC], BF16, "cknT")
            krnT = k.sb(e13, [32, SPC], BF16, "krnT")
            wuk = k.sb(e13, [128, 2, 512], BF16, "wuk")
            wuv = k.sb(e13, [128, 2, 512], BF16, "wuv")
            load_w(wuk, w_uk, 2, 512)
            load_w(wuv, w_uv, 2, 512)

            with PStack("2") as e2:
                e2.chk()
                st = mk_small(e2)
                win = k.sb(e2, [128, KC, IN_COLS], BF16, "win")
                wuq = k.sb(e2, [128, 3, 768], BF16, "wuq")
                wco = k.sb(e2, [128, 4, D], BF16, "wco")
                load_w(win, w_in, KC, IN_COLS, gain_col=0)
                load_w(wuq, w_uq, 3, 768, gain_col=32)
                load_w(wco, w_conv_out, 4, D)
                xt = [k.sb(e2, [128, D], F32, "xt") for _ in range(2)]
                xnT = [k.sb(e2, [128, 8, 128], BF16, "xnT") for _ in range(2)]
                uTh = k.sb(e2, [128, 4, 2 * T], F32, "uTh")
                uT = k.sb(e2, [128, 4, 130], F32, "uT")
                hs = k.sb(e2, [128, 128], F32, "hs")
                acc = k.sb(e2, [128, 128], F32, "acc")
                ycT = k.sb(e2, [128, 4, 128], BF16, "ycT")
                sgc = k.sb(e2, [128, 512], F32, "sgc")
                t1 = [k.sb(e2, [128, D], F32, "t1")] * 2
                sgm = [k.sb(e2, [128, D], F32, "sgm")] * 2
                cko = [k.sb(e2, [128, 288], F32, "cko") for _ in range(2)]
                cqn = k.sb(e2, [128, 384], BF16, "cqn")
                cqnT = k.sb(e2, [128, 3, 128], BF16, "cqnT")
                qf = k.sb(e2, [128, 8, 96], F32, "qf")
                qtm = k.sb(e2, [128, 4, 128], F32, "qtm")
                qb = k.sb(e2, [128, 8, 96], BF16, "qb")
                rq = [k.sb(e2, [128, 256], F32, "rq") for _ in range(2)]
                stt = k.sb(e2, [SPC, 2, 512], F32, "stt")
                stT = k.sb(e2, [128, 2, 4, SPC], F32, "stT")
                utm = st["junk"]
                rsm = k.sb(e2, [SPC, 288], F32, "rsm")
                wukT = k.sb(e2, [64, 8, 256], BF16, "wukT")

                k.dma(SP, xt[0][0:2 * T, :], xh[:, :], w=[xt[0]])
                normT(st, xt[0], 2 * T, xnT[0])
                for fc in range(4):
                    bk = bankf()
                    for part, off in ((0, OFF_H), (1, OFF_GC)):
                        for c in range(KC):
                            k.op(PE, lambda e, c=c, off=off, part=part, fc=fc, bk=bk: e.matmul(
                                bk[:, part * 128:part * 128 + 2 * T], lhsT=win[:, c, off + fc * 128:off + (fc + 1) * 128],
                                rhs=xnT[0][:, c, 0:2 * T], start=(c == 0), stop=(c == KC - 1)), r=[win, xnT[0]], w=[bk])
                    k.op(ACT, lambda e, bk=bk: e.copy(out=hs[:, 0:2 * T], in_=bk[:, 0:2 * T]), r=[bk], w=[hs])
                    k.op(DVE, lambda e, bk=bk, fc=fc: e.tensor_tensor(out=uTh[:, fc, :], in0=hs[:, 0:2 * T], in1=bk[:, 128:128 + 2 * T],
                                                                       op=ALU.mult), r=[hs, bk], w=[uTh])

                k.dma(SP, stt[:], stc[:, :, :], w=[stt])
                for kk in range(2):
                    bk = bankf()
                    for fc in range(4):
                        k.op(PE, lambda e, kk=kk, fc=fc, bk=bk: e.transpose(bk[:, fc * SPC:(fc + 1) * SPC], stt[0:SPC, kk, fc * 128:(fc + 1) * 128],
                                                                           identf[0:SPC, 0:SPC]), r=[stt, identf], w=[bk])
                    k.op(DVE, lambda e, kk=kk, bk=bk: e.tensor_copy(out=stT[:, kk, :, :], in_=bk[:, 0:4 * SPC].rearrange("p (a b) -> p a b", b=SPC)),
                         r=[bk], w=[stT])
                for h in range(8):
                    pb = bankb()
                    for lc in range(2):
                        k.op(PE, lambda e, h=h, lc=lc, pb=pb: e.transpose(pb[0:64, lc * 128:(lc + 1) * 128], wuk[:, lc, h * 64:(h + 1) * 64],
                                                                         ident[:, :]), r=[wuk, ident], w=[pb])
                    k.op(DVE, lambda e, h=h, pb=pb: e.tensor_copy(out=wukT[:, h, :], in_=pb[0:64, 0:256]), r=[pb], w=[wukT])

                for t in range(NT):
                    smp = (t == T)
                    R = SPC if smp else 128
                    x_t = xt[t % 2]
                    xn = xnT[t % 2]
                    if smp:
                        k.dma(SP, x_t[0:R, :], xsm[:, :], w=[x_t])
                        k.dma(SP, rsm[:], ropes[:, :], w=[rsm])
                    else:
                        k.dma(SP, x_t[:, :], xq[t, :, :], w=[x_t])
                        k.dma(SP, rq[t % 2][:], ropeq[t, :, :], w=[rq[t % 2]])
                    normT(st, x_t, R, xn)
                    if not smp:
                        k.op(POOL, lambda e, t=t: e.tensor_copy(out=uT[:, :, 0:2], in_=uTh[:, :, 2 * t:2 * t + 2]), r=[uTh], w=[uT])
                    for fc in range(4):
                        bk = bankf()
                        for part, off in ((0, OFF_H), (1, OFF_GB), (2, OFF_GC)):
                            for c in range(KC):
                                k.op(PE, lambda e, c=c, off=off, part=part, fc=fc, bk=bk, xn=xn, R=R: e.matmul(
                                    bk[:, part * 128:part * 128 + R], lhsT=win[:, c, off + fc * 128:off + (fc + 1) * 128],
                                    rhs=xn[:, c, 0:R], start=(c == 0), stop=(c == KC - 1)), r=[win, xn], w=[bk])
                        k.op(ACT, lambda e, bk=bk, R=R: e.copy(out=hs[:, 0:R], in_=bk[:, 0:R]), r=[bk], w=[hs])
                        k.op(DVE, lambda e, bk=bk, fc=fc, R=R: e.tensor_tensor(out=uT[:, fc, 2:2 + R], in0=hs[:, 0:R], in1=bk[:, 256:256 + R],
                                                                                op=ALU.mult), r=[hs, bk], w=[uT])
                        if smp:
                            a0, a1, a2 = stT[:, 0, fc, :], stT[:, 1, fc, :], uT[:, fc, 2:2 + R]
                            rd = [stT, uT, cw]
                        else:
                            a0, a1, a2 = uT[:, fc, 0:R], uT[:, fc, 1:1 + R], uT[:, fc, 2:2 + R]
                            rd = [uT, cw]
                        k.op(DVE, lambda e, a0=a0, fc=fc, R=R: e.tensor_scalar(out=acc[:, 0:R], in0=a0, scalar1=cw[:, fc * 3:fc * 3 + 1], scalar2=None,
                                                                               op0=ALU.mult), r=rd, w=[acc])
                        k.op(DVE, lambda e, a1=a1, fc=fc, R=R: e.scalar_tensor_tensor(out=acc[:, 0:R], in0=a1, scalar=cw[:, fc * 3 + 1:fc * 3 + 2],
                                                                                      in1=acc[:, 0:R], op0=ALU.mult, op1=ALU.add), r=rd + [acc], w=[acc])
                        k.op(DVE, lambda e, a2=a2, fc=fc, R=R: e.scalar_tensor_tensor(out=acc[:, 0:R], in0=a2, scalar=cw[:, fc * 3 + 2:fc * 3 + 3],
                                                                                      in1=acc[:, 0:R], op0=ALU.mult, op1=ALU.add), r=rd + [acc], w=[acc])
                        k.op(DVE, lambda e, bk=bk, fc=fc, R=R: e.tensor_tensor(out=ycT[:, fc, 0:R], in0=acc[:, 0:R], in1=bk[:, 128:128 + R], op=ALU.mult),
                             r=[acc, bk], w=[ycT])
                    if smp or t == T - 1:
                        bk = bankf()
                        ncol = R if smp else 2
                        c0 = 2 if smp else 128
                        for fc in range(4):
                            k.op(PE, lambda e, fc=fc, bk=bk, ncol=ncol, c0=c0: e.transpose(bk[0:ncol, fc * 128:(fc + 1) * 128], uT[:, fc, c0:c0 + ncol],
                                                                                          identf[:, :]), r=[uT, identf], w=[bk])
                        k.op(DVE, lambda e, bk=bk, ncol=ncol: e.tensor_copy(out=utm[0:ncol, 0:512], in_=bk[0:ncol, :]), r=[bk], w=[utm])
                        if smp:
                            out_toks.append(k.dma(POOL, conv_s[:, 1, :], utm[0:R, 0:512], r=[utm]))
                            out_toks.append(k.dma(POOL, conv_s[:, 0, :], stt[0:R, 1, :], r=[stt]))
                        else:
                            out_toks.append(k.dma(POOL, conv_p[:, :], utm[0:2, 0:512], r=[utm]))
                    ycb = [bankf(), bankf()]
                    for hh in range(2):
                        mm_tm(ycT, R, wco, 4, hh * 512, 512, ycb[hh])
                    t1b = t1[t % 2]
                    for hh in range(2):
                        bk = bankf()
                        mm_tm(xn, R, win, KC, OFF_GCONV + hh * 512, 512, bk)
                        k.op(ACT, lambda e, bk=bk, R=R: e.activation(out=sgc[0:R, :], in_=bk[0:R, :], func=AF.Sigmoid), r=[bk], w=[sgc])
                        k.op(DVE, lambda e, hh=hh, R=R, t1b=t1b: e.tensor_tensor(out=t1b[0:R, hh * 512:(hh + 1) * 512], in0=sgc[0:R, :],
                                                                                  in1=ycb[hh][0:R, :], op=ALU.mult), r=[sgc, ycb[hh]], w=[t1b])
                    k.dma(POOL, T1[t, 0:R, :], t1b[0:R, :], r=[t1b], w=[T1])
                    sgb = sgm[t % 2]
                    for hh in range(2):
                        bk = bankf()
                        mm_tm(xn, R, win, KC, OFF_GMLA + hh * 512, 512, bk)
                        k.op(ACT, lambda e, bk=bk, hh=hh, R=R, sgb=sgb: e.activation(out=sgb[0:R, hh * 512:(hh + 1) * 512], in_=bk[0:R, :],
                                                                                      func=AF.Sigmoid), r=[bk], w=[sgb])
                    k.dma(POOL, SG[t, 0:R, :], sgb[0:R, :], r=[sgb], w=[SG])
                    bk = bankf()
                    mm_tm(xn, R, win, KC, OFF_CKV, 288, bk)
                    ckb = cko[t % 2]
                    if smp:
                        cos_ap, sin_ap, ropeb = rsm[0:R, 256:272], rsm[0:R, 272:288], rsm
                    else:
                        cos_ap, sin_ap, ropeb = rq[t % 2][:, 0:16], rq[t % 2][:, 128:144], rq[t % 2]
                    ckv_post(st, bk, R, cos_ap, sin_ap, ropeb, ckb[0:R, 0:256], ckb[0:R, 256:288], ckb)
                    if smp:
                        out_toks.append(k.dma(POOL, ckv_s[:, :], ckb[0:R, 0:256], r=[ckb]))
                        out_toks.append(k.dma(POOL, kr_s[:, :], ckb[0:R, 256:288], r=[ckb]))
                        k.op(DVE, lambda e: e.tensor_copy(out=cknb[:, 0:256], in_=ckb[0:R, 0:256]), r=[ckb], w=[cknb])
                        k.op(POOL, lambda e: e.memset(cknb[:, 256:257], 1.0), w=[cknb])
                        k.op(DVE, lambda e: e.tensor_copy(out=st["xb"][0:R, 0:32], in_=ckb[0:R, 256:288]), r=[ckb], w=[st["xb"]])
                        pb = bankb()
                        for lc in range(2):
                            k.op(PE, lambda e, lc=lc, pb=pb: e.transpose(pb[:, lc * 128:lc * 128 + R], cknb[0:R, lc * 128:(lc + 1) * 128], ident[0:R, 0:R]),
                                 r=[cknb, ident], w=[pb])
                        k.op(PE, lambda e, pb=pb: e.transpose(pb[0:32, 256:256 + R], st["xb"][0:R, 0:32], ident[0:R, 0:R]), r=[st["xb"], ident], w=[pb])
                        k.op(DVE, lambda e, pb=pb: e.tensor_copy(out=cknT[:, :, :], in_=pb[:, 0:256].rearrange("p (a b) -> p a b", b=128)[:, :, 0:R]),
                             r=[pb], w=[cknT])
                        k.op(DVE, lambda e, pb=pb: e.tensor_copy(out=krnT[:, :], in_=pb[0:32, 256:256 + R]), r=[pb], w=[krnT])
                    else:
                        out_toks.append(k.dma(POOL, ckv_p[t, :, :], ckb[:, 0:256], r=[ckb]))
                        out_toks.append(k.dma(POOL, kr_p[t, :, :], ckb[:, 256:288], r=[ckb]))
                    bk = bankf()
                    mm_tm(xn, R, win, KC, OFF_CQ, 384, bk)
                    k.op(ACT, lambda e, bk=bk, R=R: e.activation(out=st["junk"][0:R, 0:384], in_=bk[0:R, 0:384], func=AF.Square,
                                                                 accum_out=st["ss"][0:R, 0:1]), r=[bk], w=[st["junk"], st["ss"]])
                    rstd_from_ss(st["ss"][0:R, 0:1], st["rs"][0:R, 0:1], st["ss"], st["rs"], 384, R)
                    k.op(DVE, lambda e, bk=bk, R=R: e.tensor_scalar(out=cqn[0:R, :], in0=bk[0:R, 0:384], scalar1=st["rs"][0:R, 0:1], scalar2=None,
                                                                    op0=ALU.mult), r=[bk, st["rs"]], w=[cqn])
                    transposeT(cqn, R, 384, cqnT)
                    qb0, qb1 = bankf(), bankf()
                    mm_tm(cqnT, R, wuq, 3, 0, 512, qb0)
                    mm_tm(cqnT, R, wuq, 3, 512, 256, qb1)
                    qfl = qf[:].rearrange("p a b -> p (a b)")
                    k.op(ACT, lambda e, R=R: e.copy(out=qfl[0:R, 0:512], in_=qb0[0:R, :]), r=[qb0], w=[qf])
                    k.op(ACT, lambda e, R=R: e.copy(out=qfl[0:R, 512:768], in_=qb1[0:R, 0:256]), r=[qb1], w=[qf])
                    if smp:
                        cq8 = rsm[0:R, 0:128].rearrange("p (a b) -> p a b", b=16)
                        sq8 = rsm[0:R, 128:256].rearrange("p (a b) -> p a b", b=16)
                        rb = rsm
                    else:
                        cq8 = rq[t % 2][:, 0:128].rearrange("p (a b) -> p a b", b=16)
                        sq8 = rq[t % 2][:, 128:256].rearrange("p (a b) -> p a b", b=16)
                        rb = rq[t % 2]
                    x1, x2 = qf[0:R, :, 64:80], qf[0:R, :, 80:96]
                    tq = [qtm[0:R, i, :].rearrange("p (a b) -> p a b", b=16) for i in range(4)]
                    k.op(POOL, lambda e: e.tensor_tensor(out=tq[0], in0=x1, in1=cq8, op=ALU.mult), r=[qf, rb], w=[qtm])
                    k.op(POOL, lambda e: e.tensor_tensor(out=tq[1], in0=x2, in1=sq8, op=ALU.mult), r=[qf, rb], w=[qtm])
                    k.op(POOL, lambda e: e.tensor_tensor(out=tq[2], in0=x1, in1=sq8, op=ALU.mult), r=[qf, rb], w=[qtm])
                    k.op(POOL, lambda e: e.tensor_tensor(out=tq[3], in0=x2, in1=cq8, op=ALU.mult), r=[qf, rb], w=[qtm])
                    k.op(DVE, lambda e, R=R: e.tensor_copy(out=qb[0:R, :, 32:96], in_=qf[0:R, :, 0:64]), r=[qf], w=[qb])
                    k.op(DVE, lambda e, R=R: e.tensor_tensor(out=qb[0:R, :, 0:16], in0=tq[0], in1=tq[1], op=ALU.subtract), r=[qtm], w=[qb])
                    k.op(DVE, lambda e, R=R: e.tensor_tensor(out=qb[0:R, :, 16:32], in0=tq[2], in1=tq[3], op=ALU.add), r=[qtm], w=[qb])
                    pb = bankb()
                    pv = pb[:].rearrange("p (a b) -> p a b", b=128)
                    if not smp:
                        for h in range(8):
                            k.op(PE, lambda e, h=h, pb=pb, pv=pv, R=R: e.transpose(pv[0:96, h, 0:R], qb[0:R, h, :], ident[0:R, 0:R]), r=[qb, ident], w=[pb])
                        k.op(DVE, lambda e, pv=pv, t=t: e.tensor_copy(out=qT[0:96, :, t * 128:(t + 1) * 128], in_=pv[0:96, :, :]), r=[pb], w=[qT])
                    else:
                        for h in range(8):
                            k.op(PE, lambda e, h=h, pb=pb, pv=pv, R=R: e.transpose(pv[0:64, h, 0:R], qb[0:R, h, 32:96], ident[0:R, 0:R]), r=[qb, ident], w=[pb])
                        qsT = k.sb(e2, [64, 8, SPC], BF16, "qsT")
                        k.op(DVE, lambda e, pv=pv, R=R: e.tensor_copy(out=qsT[0:64, :, :], in_=pv[0:64, :, 0:R]), r=[pb], w=[qsT])
                        pb2 = bankb()
                        pv2 = pb2[:].rearrange("p (a b) -> p a b", b=128)
                        for h in range(8):
                            k.op(PE, lambda e, h=h, pv2=pv2, pb2=pb2, R=R: e.transpose(pv2[0:32, h, 0:R], qb[0:R, h, 0:32], ident[0:R, 0:R]),
                                 r=[qb, ident], w=[pb2])
                        k.op(DVE, lambda e, pv2=pv2, R=R: e.tensor_copy(out=qrT[:, :, :].rearrange("p s h -> p h s"), in_=pv2[0:32, :, 0:R]),
                             r=[pb2], w=[qrT])
                        for lc in range(2):
                            bk = bankf()
                            for h in range(8):
                                k.op(PE, lambda e, h=h, lc=lc, bk=bk, R=R: e.matmul(bk[:, h * SPC:(h + 1) * SPC], lhsT=wukT[:, h, lc * 128:(lc + 1) * 128],
                                                                                   rhs=qsT[0:64, h, 0:R], start=True, stop=True), r=[wukT, qsT], w=[bk])
                            k.op(DVE, lambda e, lc=lc, bk=bk: e.tensor_copy(out=qlT[:, lc, :, :].rearrange("p s h -> p h s"),
                                                                             in_=bk[:, 0:8 * SPC].rearrange("p (h s) -> p h s", s=SPC)), r=[bk], w=[qlT])
            k.barrier()
            o_all = k.sb(e13, [128, NT, 512], BF16, "o_all")
            eK = ExitStack()
            eK.__enter__()
            ckvnT = k.sb(eK, [128, 2, S], BF16, "ckvnT")
            KT = k.sb(eK, [128, S], BF16, "KT")
            KRB = k.sb(eK, [128, S], BF16, "KRB")
            wukp = k.sb(eK, [128, 2, 8, 96], BF16, "wukp")
            k.op(DVE, lambda e: e.memset(wukp[:].rearrange("p a h d -> p (a h d)"), 0.0), w=[wukp])
            for lc_ in range(2):
                k.op(DVE, lambda e, lc_=lc_: e.tensor_copy(out=wukp[:, lc_, :, 32:96], in_=wuk[:, lc_, :].rearrange("p (h d) -> p h d", d=64)), r=[wuk], w=[wukp])

            with PStack("1") as e1:
                e1.chk()
                stX = mk_small(e1)
                stY = mk_small(e1)
                wkv = k.sb(e1, [128, KC, 288], BF16, "wkv")
                load_w(wkv, w_in, KC, 288, gain_col=0, col0=OFF_CKV)
                xt = [k.sb(e1, [128, D], F32, "xt") for _ in range(3)]
                xnT = [k.sb(e1, [128, 8, 128], BF16, "xnT") for _ in range(2)]
                rk = k.sb(e1, [128, NB, 32], F32, "rk")
                ckb = [k.sb(e1, [128, 352], BF16, "ckb") for _ in range(2)]
                for cb_ in ckb:
                    k.op(DVE, lambda e, cb_=cb_: e.memset(cb_[:, 288:352], 0.0), w=[cb_])
                k.dma(SP, rk[:].rearrange("p n c -> p (n c)"), ropek[:, :], w=[rk])

                def stage_x(nb):
                    x_t = xt[nb % 3]
                    k.dma(SP if nb % 2 else ACT, x_t[:, :], xs[nb, :, :], w=[x_t])
                    normT(stX, x_t, 128, xnT[nb % 2])

                def stage_y(nb):
                    xn = xnT[nb % 2]
                    bk = bankf()
                    mm_tm(xn, 128, wkv, KC, 0, 288, bk)
                    cb = ckb[nb % 2]
                    ckv_post(stY, bk, 128, rk[:, nb, 0:16], rk[:, nb, 16:32], rk, cb[:, 0:256], cb[:, 256:288], cb)
                    pb = bankb()
                    pv = pb[:].rearrange("p (a b) -> p a b", b=128)
                    k.op(PE, lambda e: e.transpose(pv[:, 0, :], cb[:, 0:128], ident[:, :]), r=[cb, ident], w=[pb])
                    k.op(PE, lambda e: e.transpose(pv[:, 1, :], cb[:, 128:256], ident[:, :]), r=[cb, ident], w=[pb])
                    k.op(PE, lambda e: e.transpose(pv[0:96, 2, :], cb[:, 256:352], ident[:, :]), r=[cb, ident], w=[pb])
                    k.op(ACT, lambda e: e.copy(out=ckvnT[:, :, nb * 128:(nb + 1) * 128], in_=pv[:, 0:2, :]), r=[pb], w=[ckvnT])
                    k.op(ACT, lambda e: e.copy(out=KRB[0:96, nb * 128:(nb + 1) * 128], in_=pv[0:96, 2, :]), r=[pb], w=[KRB])

                stage_x(0)
                for nb in range(NB):
                    if nb + 1 < NB:
                        stage_x(nb + 1)
                    stage_y(nb)
            k.barrier()

            with PStack("3") as e3:
                e3.chk()
                mk16 = k.sb(e3, [128, 16, 512], BF16, "mk16")
                for d in range(16):
                    sti[0] += 1
                    s_ = stg[sti[0] % 4]
                    k.dma(SP, s_[:, 0:512], masks[d, :, :], w=[s_])
                    k.op(alt(), lambda e, d=d, s_=s_: e.tensor_copy(out=mk16[:, d, :], in_=s_[:, 0:512]), r=[s_], w=[mk16])
                V1 = k.sb(e3, [128, NB, 65], BF16, "V1")
                k.op(POOL, lambda e: e.memset(V1[:, :, 64:65], 1.0), w=[V1])
                pT = [k.sb(e3, [128, 512], BF16, "pT") for _ in range(3)]
                pM = [k.sb(e3, [128, 512], BF16, "pM") for _ in range(2)]
                rec = k.sb(e3, [128, 4], F32, "rec")
                oTs = k.sb(e3, [128, 512], F32, "oTs")
                pti = 0
                for h in range(8):
                    for c0 in range(0, S, 512):
                        bk = bankf()
                        for lc in range(2):
                            k.op(PE, lambda e, lc=lc, bk=bk, c0=c0, h=h: e.matmul(bk[0:96, :], lhsT=wukp[:, lc, h, :],
                                                                                 rhs=ckvnT[:, lc, c0:c0 + 512], start=(lc == 0), stop=(lc == 1)),
                                 r=[wukp, ckvnT], w=[bk])
                        k.op(DVE, lambda e, bk=bk, c0=c0: e.tensor_tensor(out=KT[0:96, c0:c0 + 512], in0=bk[0:96, :], in1=KRB[0:96, c0:c0 + 512], op=ALU.add),
                             r=[bk, KRB], w=[KT])
                    for nb0 in range(0, NB, 8):
                        bk = bankf()
                        nn = min(8, NB - nb0)
                        for i in range(nn):
                            nb = nb0 + i
                            for lc in range(2):
                                k.op(PE, lambda e, lc=lc, bk=bk, nb=nb, i=i, h=h: e.matmul(bk[:, i * 64:(i + 1) * 64], lhsT=ckvnT[:, lc, nb * 128:(nb + 1) * 128],
                                                                                          rhs=wuv[:, lc, h * 64:(h + 1) * 64], start=(lc == 0), stop=(lc == 1)),
                                     r=[wuv, ckvnT], w=[bk])
                        k.op(DVE, lambda e, bk=bk, nb0=nb0, nn=nn: e.tensor_copy(out=V1[:, nb0:nb0 + nn, 0:64],
                                                                                 in_=bk[:, 0:nn * 64].rearrange("p (a b) -> p a b", b=64)), r=[bk], w=[V1])
                    for u in range(NG):
                        nkb = 16 * u + 16
                        oT_ = bankf(pin=True)

                        def qstart(kb, u=u):
                            return 0 if kb < 16 * u else ((kb - 16 * u) // 4) * 128

                        def emit_st(kb, h=h, u=u):
                            sb_ = bankf()
                            q0 = qstart(kb)
                            k.op(PE, lambda e: e.matmul(sb_[:, q0:512], lhsT=KT[0:96, kb * 128:(kb + 1) * 128],
                                                        rhs=qT[0:96, h, u * 512 + q0:(u + 1) * 512], start=True, stop=True),
                                 r=[KT, qT], w=[sb_])
                            return sb_
                        pend = [emit_st(0)]
                        if nkb > 1:
                            pend.append(emit_st(1))
                        for kb in range(nkb):
                            sb_ = pend.pop(0)
                            if kb + 2 < nkb:
                                pend.append(emit_st(kb + 2))
                            pti += 1
                            p_ = pT[pti % 3]
                            q0 = qstart(kb)
                            k.op(ACT, lambda e, sb_=sb_, p_=p_, q0=q0: e.activation(out=p_[:, q0:512], in_=sb_[:, q0:512], func=AF.Exp, scale=SCALE), r=[sb_], w=[p_])
                            if kb >= 16 * u:
                                d = kb - 16 * u
                                pm_ = pM[pti % 2]
                                k.op(DVE if pti % 2 else POOL, lambda e, p_=p_, d=d, pm_=pm_, q0=q0: e.tensor_tensor(out=pm_[:, q0:512], in0=p_[:, q0:512], in1=mk16[:, d, q0:512], op=ALU.mult),
                                     r=[p_, mk16], w=[pm_])
                                p_ = pm_
                            k.op(PE, lambda e, p_=p_, kb=kb, nkb=nkb, q0=q0: e.matmul(oT_[0:65, q0:512], lhsT=V1[:, kb, :], rhs=p_[:, q0:512],
                                                                                     start=(kb == 0), stop=(kb == nkb - 1)), r=[p_, V1], w=[oT_])
                        k.op(ACT, lambda e: e.copy(out=oTs[0:65, :], in_=oT_[0:65, :]), r=[oT_], w=[oTs])
                        unpin(oT_)
                        tb = bankf()
                        for qi in range(4):
                            k.op(PE, lambda e, qi=qi, tb=tb: e.transpose(tb[:, qi * 65:(qi + 1) * 65], oTs[0:65, qi * 128:(qi + 1) * 128], identf[0:65, 0:65]),
                                 r=[oTs, identf], w=[tb])
                        tv = tb[:, 0:260].rearrange("p (a b) -> p a b", b=65)
                        k.op(DVE, lambda e, tv=tv: e.reciprocal(out=rec[:, :], in_=tv[:, :, 64]), r=[tb], w=[rec])
                        for qi in range(4):
                            k.op(DVE, lambda e, qi=qi, tv=tv, u=u, h=h: e.tensor_scalar(out=o_all[:, 4 * u + qi, h * 64:(h + 1) * 64], in0=tv[:, qi, 0:64],
                                                                                       scalar1=rec[:, qi:qi + 1], scalar2=None, op0=ALU.mult),
                                 r=[tb, rec], w=[o_all])
            k.barrier()
            eK.close()

            with PStack("3s") as e3s:
                e3s.chk()
                pti_ = k.sb(e3s, [128, CH * SPC], I32, "pti")
                k.dma(SP, pti_[:], ptT[:, :], w=[pti_])
                NBUF = 3
                G = [k.sb(e3s, [128, 16, 256], F32, "G") for _ in range(NBUF)]
                Gk = [k.sb(e3s, [128, 16, 32], F32, "Gk") for _ in range(NBUF)]
                Gb = [k.sb(e3s, [128, 16, 256], BF16, "Gb") for _ in range(NBUF)]
                Gkb = [k.sb(e3s, [128, 16, 32], BF16, "Gkb") for _ in range(NBUF)]
                cTc = [k.sb(e3s, [128, 4, 2, 512], BF16, "cTc") for _ in range(2)]
                cTk = [k.sb(e3s, [32, 2, 1024], BF16, "cTk") for _ in range(2)]
                pch = [k.sb(e3s, [8, 2048], BF16, "pch") for _ in range(2)]
                ptp = [k.sb(e3s, [128, 16, 8], BF16, "ptp") for _ in range(2)]
                rsx = k.sb(e3s, [8, SPC * CH * 4], F32, "rsx")
                den = k.sb(e3s, [8, 2], F32, "den")
                olb = k.sb(e3s, [8, 256], BF16, "olb")
                OT = k.sb(e3s, [128, 2, 8, SPC], BF16, "OT")
                bdm = k.sb(e3s, [SPC, SPC * 8], F32, "bdm")
                pnw = k.sb(e3s, [SPC, SPC * 8], F32, "pnw")
                PnT = k.sb(e3s, [SPC, SPC, 8], BF16, "PnT")
                k.dma(SP, bdm[:], bdmask[:, :], w=[bdm])
                bk = bankf()
                qlf = [qlT[:, lc, :, :].rearrange("p s h -> p (s h)") for lc in range(2)]
                for lc in range(2):
                    k.op(PE, lambda e, lc=lc, bk=bk: e.matmul(bk[0:SPC, 0:SPC * 8], lhsT=cknT[:, lc, :], rhs=qlf[lc], start=(lc == 0), stop=False),
                         r=[cknT, qlT], w=[bk])
                k.op(PE, lambda e, bk=bk: e.matmul(bk[0:SPC, 0:SPC * 8], lhsT=krnT[:, :], rhs=qrT[:, :, :].rearrange("p s h -> p (s h)"),
                                                   start=False, stop=True), r=[krnT, qrT], w=[bk])
                k.op(ACT, lambda e, bk=bk: e.activation(out=pnw[:, :], in_=bk[0:SPC, 0:SPC * 8], func=AF.Exp, scale=SCALE), r=[bk], w=[pnw])
                k.op(DVE, lambda e: e.tensor_tensor(out=PnT[:, :, :].rearrange("p s h -> p (s h)"), in0=pnw[:, :], in1=bdm[:, :], op=ALU.mult),
                     r=[pnw, bdm], w=[PnT])

                chunks = [(s_, ch_) for s_ in range(SPC) for ch_ in range(CH)]
                NCK = len(chunks)

                def load(c):
                    s_, ch_ = chunks[c]
                    g_, gk_, gb_, gkb_ = G[c % NBUF], Gk[c % NBUF], Gb[c % NBUF], Gkb[c % NBUF]
                    k.dma(POOL, g_[:].rearrange("p a b -> p (a b)"), pool_ckv[:, :], r=[pti_], w=[g_], idx=pti_[:, ch_ * SPC + s_:ch_ * SPC + s_ + 1])
                    k.dma(POOL, gk_[:].rearrange("p a b -> p (a b)"), pool_kr[:, :], r=[pti_], w=[gk_], idx=pti_[:, ch_ * SPC + s_:ch_ * SPC + s_ + 1])
                    Ec = (DVE, ACT, POOL)[c % 3]
                    if Ec is ACT:
                        k.op(ACT, lambda e: e.copy(out=gb_[:], in_=g_[:]), r=[g_], w=[gb_])
                    else:
                        k.op(Ec, lambda e: e.tensor_copy(out=gb_[:], in_=g_[:]), r=[g_], w=[gb_])
                    k.op(POOL if c % 2 else DVE, lambda e: e.tensor_copy(out=gkb_[:], in_=gk_[:]), r=[gk_], w=[gkb_])

                tbs = [psb[0], psb[1], psf[4], psf[5]]
                pinned.add(id(psf[4]))
                pinned.add(id(psf[5]))
                tbi = [0]

                def bankb4():
                    tbi[0] += 1
                    b_ = tbs[tbi[0] % 4]
                    full = b_[:] if (b_ is psb[0] or b_ is psb[1]) else b_.t.bitcast(BF16)[:]
                    return b_, full

                def stageA(c):
                    gb_, gkb_, cc, ck = Gb[c % NBUF], Gkb[c % NBUF], cTc[c % 2], cTk[c % 2]
                    for grp in range(4):
                        pb, pf = bankb4()
                        pv = pf.rearrange("p (a b) -> p a b", b=128)
                        for lc in range(2):
                            for i in range(4):
                                tt = grp * 4 + i
                                k.op(PE, lambda e, lc=lc, i=i, tt=tt, pv=pv: e.transpose(pv[:, lc * 4 + i, :], gb_[:, tt, lc * 128:(lc + 1) * 128], ident[:, :]),
                                     r=[gb_, ident], w=[pb])
                        if grp % 2:
                            k.op(ACT, lambda e, pf=pf, grp=grp: e.copy(out=cc[:, grp, :, :].rearrange("p a b -> p (a b)"), in_=pf[:, :]), r=[pb], w=[cc])
                        else:
                            k.op(DVE, lambda e, pf=pf, grp=grp: e.tensor_copy(out=cc[:, grp, :, :].rearrange("p a b -> p (a b)"), in_=pf[:, :]), r=[pb], w=[cc])
                    for half in range(2):
                        pb, pf = bankb4()
                        pv = pf.rearrange("p (a b) -> p a b", b=128)
                        for i in range(8):
                            tt = half * 8 + i
                            k.op(PE, lambda e, i=i, tt=tt, pv=pv: e.transpose(pv[0:32, i, :], gkb_[:, tt, :], ident[:, :]), r=[gkb_, ident], w=[pb])
                        if half:
                            k.op(ACT, lambda e, pf=pf, half=half: e.copy(out=ck[:, half, :], in_=pf[0:32, :]), r=[pb], w=[ck])
                        else:
                            k.op(DVE, lambda e, pf=pf, half=half: e.tensor_copy(out=ck[:, half, :], in_=pf[0:32, :]), r=[pb], w=[ck])

                def stageB(c):
                    s_, ch_ = chunks[c]
                    cc, ck, pc = cTc[c % 2], cTk[c % 2], pch[c % 2]
                    for grp in range(4):
                        sb_ = bankf()
                        for lc in range(2):
                            k.op(PE, lambda e, lc=lc, sb_=sb_, grp=grp: e.matmul(sb_[0:8, :], lhsT=qlT[:, lc, s_, :], rhs=cc[:, grp, lc, :],
                                                                                start=(lc == 0), stop=False), r=[qlT, cc], w=[sb_])
                        k.op(PE, lambda e, sb_=sb_, grp=grp: e.matmul(sb_[0:8, :], lhsT=qrT[:, s_, :], rhs=ck[:, grp // 2, (grp % 2) * 512:(grp % 2 + 1) * 512],
                                                                     start=False, stop=True), r=[qrT, ck], w=[sb_])
                        col = (s_ * CH + ch_) * 4 + grp
                        k.op(ACT, lambda e, sb_=sb_, grp=grp, col=col: e.activation(out=pc[0:8, grp * 512:(grp + 1) * 512], in_=sb_[0:8, :], func=AF.Exp,
                                                                                   scale=SCALE, accum_out=rsx[0:8, col:col + 1]), r=[sb_], w=[pc, rsx])

                def stageC(c):
                    pc, pp = pch[c % 2], ptp[c % 2]
                    pb, pf = bankb4()
                    pv8 = pf[:, 0:128].rearrange("p (a b) -> p a b", b=8)
                    for tt in range(16):
                        k.op(PE, lambda e, tt=tt, pv8=pv8: e.transpose(pv8[:, tt, :], pc[0:8, tt * 128:(tt + 1) * 128], ident[0:8, 0:8]),
                             r=[pc, ident], w=[pb])
                    k.op(DVE, lambda e, pv8=pv8: e.tensor_copy(out=pp[:, :, :], in_=pv8[:, :, :]), r=[pb], w=[pp])

                def stageD(c, oacc):
                    s_, ch_ = chunks[c]
                    pp, gb_ = ptp[c % 2], Gb[c % NBUF]
                    for tt in range(16):
                        last = (ch_ == CH - 1 and tt == 15)
                        k.op(PE, lambda e, tt=tt, last=last: e.matmul(oacc[0:8, 0:256], lhsT=pp[:, tt, :], rhs=gb_[:, tt, :],
                                                                     start=False, stop=last), r=[pp, gb_], w=[oacc])

                load(0)
                if NCK > 1:
                    load(1)
                stageA(0)
                oacc = None
                for c in range(NCK):
                    s_, ch_ = chunks[c]
                    if ch_ == 0:
                        oacc = bankf(pin=True)
                        k.op(PE, lambda e, s_=s_, oacc=oacc: e.matmul(oacc[0:8, 0:257], lhsT=PnT[:, s_, :], rhs=cknb[:, 0:257], start=True, stop=False),
                             r=[PnT, cknb], w=[oacc])
                    if c + 2 < NCK:
                        load(c + 2)
                    stageB(c)
                    if c + 1 < NCK:
                        stageA(c + 1)
                    stageC(c)
                    stageD(c, oacc)
                    if ch_ == CH - 1:
                        s = s_
                        k.op(DVE, lambda e, s=s: e.reduce_sum(out=den[0:8, 1:2], in_=rsx[0:8, s * CH * 4:(s + 1) * CH * 4], axis=AX.X), r=[rsx], w=[den])
                        k.op(DVE, lambda e, oacc=oacc: e.tensor_tensor(out=den[0:8, 1:2], in0=den[0:8, 1:2], in1=oacc[0:8, 256:257], op=ALU.add), r=[den, oacc], w=[den])
                        k.op(DVE, lambda e: e.reciprocal(out=den[0:8, 1:2], in_=den[0:8, 1:2]), r=[den], w=[den])
                        k.op(DVE, lambda e, oacc=oacc: e.tensor_scalar(out=olb[:, :], in0=oacc[0:8, 0:256], scalar1=den[0:8, 1:2], scalar2=None, op0=ALU.mult),
                             r=[oacc, den], w=[olb])
                        pb, pf = bankb4()
                        for lc in range(2):
                            k.op(PE, lambda e, lc=lc, pf=pf: e.transpose(pf[:, lc * 8:(lc + 1) * 8], olb[0:8, lc * 128:(lc + 1) * 128], ident[0:8, 0:8]),
                                 r=[olb, ident], w=[pb])
                        k.op(DVE, lambda e, pf=pf, s=s: e.tensor_copy(out=OT[:, :, :, s], in_=pf[:, 0:16].rearrange("p (a b) -> p a b", b=8)), r=[pb], w=[OT])
                        unpin(oacc)
                bk = bankf()
                for h in range(8):
                    for lc in range(2):
                        k.op(PE, lambda e, h=h, lc=lc, bk=bk: e.matmul(bk[0:SPC, h * 64:(h + 1) * 64], lhsT=OT[:, lc, h, :], rhs=wuv[:, lc, h * 64:(h + 1) * 64],
                                                                       start=(lc == 0), stop=(lc == 1)), r=[OT, wuv], w=[bk])
                k.op(DVE, lambda e, bk=bk: e.tensor_copy(out=o_all[0:SPC, T, :], in_=bk[0:SPC, :]), r=[bk], w=[o_all])
                unpin(psf[4])
                unpin(psf[5])
            k.dma(POOL, OA[:, :, :].rearrange("t p c -> p t c"), o_all[:, :, :], r=[o_all], w=[OA])
            k.barrier()

            e13.close()
            with PStack("4a") as e4:
                e4.chk()
                st = mk_small(e4)
                mk_gpost(e4, st)
                wmo = k.sb(e4, [128, 4, D], BF16, "wmo")
                wmx = k.sb(e4, [128, KC, D], BF16, "wmx")
                wcq = k.sb(e4, [128, KC, D], BF16, "wcq")
                wcoo = k.sb(e4, [128, KC, D], BF16, "wcoo")
                load_w(wmo, w_mla_out, 4, D)
                load_w(wmx, w_mix_out, KC, D)
                load_w(wcq, w_ca_q, KC, D, gain_col=8)
                load_w(wcoo, w_ca_o, KC, D)
                mkT = k.sb(e4, [128, 8, NMEM], BF16, "mkT")
                mv1 = k.sb(e4, [128, 2, 4, 257], BF16, "mv1")
                xt = [k.sb(e4, [128, D], F32, "xt") for _ in range(2)]
                x1 = k.sb(e4, [128, D], F32, "x1")
                x2 = [k.sb(e4, [128, D], F32, "x2") for _ in range(2)]
                t1 = [k.sb(e4, [128, D], F32, "t1") for _ in range(2)]
                sgm = [k.sb(e4, [128, D], F32, "sgm") for _ in range(2)]
                oT = k.sb(e4, [128, 4, 128], BF16, "oT")
                mg = k.sb(e4, [128, D], BF16, "mg")
                mgT = k.sb(e4, [128, 8, 128], BF16, "mgT")
                xn2T = k.sb(e4, [128, 8, 128], BF16, "xn2T")
                qcT = k.sb(e4, [128, 8, 128], BF16, "qcT")
                pcT = k.sb(e4, [128, 8, 128], BF16, "pcT")
                ocb = k.sb(e4, [128, D], BF16, "ocb")
                ocT = k.sb(e4, [128, 8, 128], BF16, "ocT")
                rec4 = k.sb(e4, [128, 4], F32, "rec4")
                with ExitStack() as em:
                    wck = k.sb(em, [128, KC, D], BF16, "wck")
                    wcv = k.sb(em, [128, KC, D], BF16, "wcv")
                    load_w(wck, w_ca_k, KC, D, gain_col=16)
                    load_w(wcv, w_ca_v, KC, D, gain_col=16)
                    mnT = k.sb(em, [128, 8, NMEM], BF16, "mnT")
                    mtmp = k.sb(em, [128, 8, 128], BF16, "mtmp")
                    mo = [k.sb(em, [128, D], F32, "mo") for _ in range(2)]
                    k.op(POOL, lambda e: e.memset(mv1[:, :, :, 256:257], 1.0), w=[mv1])
                    for mb in range(2):
                        k.dma(SP, xt[mb][:, :], memp[mb * 128:(mb + 1) * 128, :], w=[xt[mb]])
                        normT(st, xt[mb], 128, mtmp)
                        k.op(POOL, lambda e, mb=mb: e.tensor_copy(out=mnT[:, :, mb * 128:(mb + 1) * 128], in_=mtmp[:, :, :]), r=[mtmp], w=[mnT])
                        for wi, (wt, dst) in enumerate(((wck, mk_p), (wcv, mv_p))):
                            mob = mo[wi]
                            for hh in range(2):
                                bk = bankf()
                                mm_tm(mtmp, 128, wt, KC, hh * 512, 512, bk)
                                k.op(ACT, lambda e, bk=bk, hh=hh, mob=mob: e.copy(out=mob[:, hh * 512:(hh + 1) * 512], in_=bk[:, :]), r=[bk], w=[mob])
                                if wi == 1:
                                    k.op(DVE, lambda e, bk=bk, hh=hh, mb=mb: e.tensor_copy(out=mv1[:, mb, 2 * hh:2 * hh + 2, 0:256],
                                                                                          in_=bk[:, :].rearrange("p (a b) -> p a b", b=256)), r=[bk], w=[mv1])
                            out_toks.append(k.dma(POOL, dst[mb * 128:(mb + 1) * 128, :], mob[:, :], r=[mob]))
                    for c8 in range(8):
                        bk = bankf()
                        for c in range(KC):
                            k.op(PE, lambda e, c=c, c8=c8, bk=bk: e.matmul(bk[:, 0:NMEM], lhsT=wck[:, c, c8 * 128:(c8 + 1) * 128], rhs=mnT[:, c, :],
                                                                           start=(c == 0), stop=(c == KC - 1)), r=[wck, mnT], w=[bk])
                        k.op(ACT, lambda e, c8=c8, bk=bk: e.copy(out=mkT[:, c8, :], in_=bk[:, 0:NMEM]), r=[bk], w=[mkT])
                k.barrier()
                ksf = [k.sb(e4, [128, 2, D], F32, "ksf") for _ in range(2)]
                ksb = k.sb(e4, [128, 2, D], BF16, "ksb")
                vsb = k.sb(e4, [128, 2, D], BF16, "vsb")
                ksT = k.sb(e4, [128, 2, 8, 128], BF16, "ksT")
                pcs = k.sb(e4, [128, 2, 4], BF16, "pcs")
                ones = k.sb(e4, [128, 1], BF16, "ones")
                hm = k.sb(e4, [4, D], F32, "hm")
                ovs = k.sb(e4, [4, D], F32, "ovs")
                ocs = k.sb(e4, [4, 256], F32, "ocs")
                ocsb = k.sb(e4, [4, 256], BF16, "ocsb")
                rs4 = k.sb(e4, [4, 1], F32, "rs4")
                k.op(POOL, lambda e: e.memset(ones[:], 1.0), w=[ones])
                k.dma(SP, hm[:], hmask[:, :], w=[hm])

                for t in range(NT):
                    smp = (t == T)
                    R = SPC if smp else 128
                    x_t = xt[t % 2]
                    if smp:
                        k.dma(SP, x_t[0:R, :], xsm[:, :], w=[x_t])
                    else:
                        k.dma(SP, x_t[:, :], xq[t, :, :], w=[x_t])
                    t1b, sgb = t1[t % 2], sgm[t % 2]
                    k.dma(SP, t1b[0:R, :], T1[t, 0:R, :], r=[T1], w=[t1b])
                    k.dma(SP, sgb[0:R, :], SG[t, 0:R, :], r=[SG], w=[sgb])
                    osrc = k.sb(e4, [128, 512], BF16, "osrc") if t == 0 else osrc
                    k.dma(SP, osrc[0:R, :], OA[t, 0:R, :], r=[OA], w=[osrc])
                    transposeT(osrc, R, 512, oT)
                    ymb = [bankf(), bankf()]
                    for hh in range(2):
                        mm_tm(oT, R, wmo, 4, hh * 512, 512, ymb[hh])
                    for hh in range(2):
                        sl = slice(hh * 512, (hh + 1) * 512)
                        k.op(DVE, lambda e, hh=hh, sl=sl, R=R, sgb=sgb: e.tensor_tensor(out=sgb[0:R, sl], in0=sgb[0:R, sl], in1=ymb[hh][0:R, :], op=ALU.mult),
                             r=[sgb, ymb[hh]], w=[sgb])
                        k.op(POOL, lambda e, sl=sl, R=R, sgb=sgb, t1b=t1b: e.tensor_tensor(out=mg[0:R, sl], in0=sgb[0:R, sl], in1=t1b[0:R, sl], op=ALU.add),
                             r=[sgb, t1b], w=[mg])
                    transposeT(mg, R, D, mgT)
                    mxb = [bankf(), bankf()]
                    for hh in range(2):
                        mm_tm(mgT, R, wmx, KC, hh * 512, 512, mxb[hh])
                    postnorm_res(st, mxb, R, 0, x_t, x1)
                    normT(st, x1, R, xn2T)
                    for half in range(2):
                        bk2 = [bankf(), bankf()]
                        for i in range(4):
                            c8 = half * 4 + i
                            bk = bk2[i // 2] if R == 128 else bk2[0]
                            oc = (i % 2) * 128 if R == 128 else i * R
                            for c in range(KC):
                                k.op(PE, lambda e, c=c, c8=c8, bk=bk, oc=oc, R=R: e.matmul(bk[:, oc:oc + R], lhsT=wcq[:, c, c8 * 128:(c8 + 1) * 128], rhs=xn2T[:, c, 0:R],
                                                                                          start=(c == 0), stop=(c == KC - 1)), r=[wcq, xn2T], w=[bk])
                        if R == 128:
                            for i2 in range(2):
                                k.op(ACT, lambda e, i2=i2, half=half, bk2=bk2: e.copy(out=qcT[:, half * 4 + 2 * i2:half * 4 + 2 * i2 + 2, :],
                                                                                      in_=bk2[i2][:, 0:256].rearrange("p (a b) -> p a b", b=128)), r=[bk2[i2]], w=[qcT])
                        else:
                            k.op(ACT, lambda e, half=half, bk2=bk2, R=R: e.copy(out=qcT[:, half * 4:half * 4 + 4, 0:R],
                                                                                in_=bk2[0][:, 0:4 * R].rearrange("p (a b) -> p a b", b=R)), r=[bk2[0]], w=[qcT])
                    if not smp:
                        for hp in range(2):
                            sb_ = bankf()
                            for i in range(4):
                                h, mb = hp * 2 + i // 2, i % 2
                                for ec in range(2):
                                    k.op(PE, lambda e, h=h, mb=mb, ec=ec, i=i, sb_=sb_: e.matmul(sb_[:, i * 128:(i + 1) * 128], lhsT=mkT[:, h * 2 + ec, mb * 128:(mb + 1) * 128],
                                                                                                 rhs=qcT[:, h * 2 + ec, :], start=(ec == 0), stop=(ec == 1)),
                                         r=[mkT, qcT], w=[sb_])
                            k.op(ACT, lambda e, hp=hp, sb_=sb_: e.activation(out=pcT[:, hp * 4:(hp + 1) * 4, :], in_=sb_[:, :].rearrange("p (a b) -> p a b", b=128),
                                                                             func=AF.Exp, scale=1.0 / 16.0), r=[sb_], w=[pcT])
                        for h in range(4):
                            ob = bankf()
                            for mb in range(2):
                                k.op(PE, lambda e, h=h, mb=mb, ob=ob: e.matmul(ob[:, 0:257], lhsT=pcT[:, h * 2 + mb, :], rhs=mv1[:, mb, h, :],
                                                                               start=(mb == 0), stop=(mb == 1)), r=[pcT, mv1], w=[ob])
                            k.op(DVE, lambda e, h=h, ob=ob: e.reciprocal(out=rec4[:, h:h + 1], in_=ob[:, 256:257]), r=[ob], w=[rec4])
                            k.op(DVE, lambda e, h=h, ob=ob: e.tensor_scalar(out=ocb[:, h * 256:(h + 1) * 256], in0=ob[:, 0:256], scalar1=rec4[:, h:h + 1], scalar2=None,
                                                                            op0=ALU.mult), r=[ob, rec4], w=[ocb])
                        transposeT(ocb, 128, D, ocT)
                    else:
                        for s in range(SPC):
                            kf, vf = ksf[0], ksf[1]
                            k.dma(SP, kf[:], cmk[s, :, :].rearrange("(a p) d -> p a d", p=128), w=[kf])
                            k.dma(SP, vf[:], cmv[s, :, :].rearrange("(a p) d -> p a d", p=128), w=[vf])
                            k.op(POOL, lambda e, kf=kf: e.tensor_copy(out=ksb[:], in_=kf[:]), r=[kf], w=[ksb])
                            k.op(DVE, lambda e, vf=vf: e.tensor_copy(out=vsb[:], in_=vf[:]), r=[vf], w=[vsb])
                            for mb in range(2):
                                pb = bankb()
                                pv = pb[:].rearrange("p (a b) -> p a b", b=128)
                                for c8 in range(8):
                                    k.op(PE, lambda e, c8=c8, mb=mb, pv=pv, pb=pb: e.transpose(pv[:, c8, :], ksb[:, mb, c8 * 128:(c8 + 1) * 128], ident[:, :]),
                                         r=[ksb, ident], w=[pb])
                                k.op(ACT, lambda e, mb=mb, pv=pv: e.copy(out=ksT[:, mb, :, :], in_=pv[:, :, :]), r=[pb], w=[ksT])
                            sb_ = bankf()
                            for mb in range(2):
                                for h in range(4):
                                    for ec in range(2):
                                        k.op(PE, lambda e, mb=mb, h=h, ec=ec, s=s, sb_=sb_: e.matmul(sb_[:, mb * 4 + h:mb * 4 + h + 1], lhsT=ksT[:, mb, h * 2 + ec, :],
                                                                                                     rhs=qcT[:, h * 2 + ec, s:s + 1], start=(ec == 0), stop=(ec == 1)),
                                             r=[ksT, qcT], w=[sb_])
                            k.op(ACT, lambda e, sb_=sb_: e.activation(out=pcs[:, :, :].rearrange("p a b -> p (a b)"), in_=sb_[:, 0:8], func=AF.Exp, scale=1.0 / 16.0),
                                 r=[sb_], w=[pcs])
                            ovb = [bankf(), bankf()]
                            for hh in range(2):
                                for mb in range(2):
                                    k.op(PE, lambda e, hh=hh, mb=mb, ovb=ovb: e.matmul(ovb[hh][0:4, :], lhsT=pcs[:, mb, :], rhs=vsb[:, mb, hh * 512:(hh + 1) * 512],
                                                                                       start=(mb == 0), stop=(mb == 1)), r=[pcs, vsb], w=[ovb[hh]])
                            rb_ = bankf()
                            for mb in range(2):
                                k.op(PE, lambda e, mb=mb, rb_=rb_: e.matmul(rb_[0:4, 0:1], lhsT=pcs[:, mb, :], rhs=ones[:, 0:1], start=(mb == 0), stop=(mb == 1)),
                                     r=[pcs, ones], w=[rb_])
                            for hh in range(2):
                                k.op(DVE, lambda e, hh=hh, ovb=ovb: e.tensor_tensor(out=ovs[:, hh * 512:(hh + 1) * 512], in0=ovb[hh][0:4, :], in1=hm[:, hh * 512:(hh + 1) * 512],
                                                                                   op=ALU.mult), r=[ovb[hh], hm], w=[ovs])
                            k.op(DVE, lambda e: e.tensor_reduce(out=ocs[:, :], in_=ovs[:, :].rearrange("p (h e) -> p e h", h=4), axis=AX.X, op=ALU.add), r=[ovs], w=[ocs])
                            k.op(DVE, lambda e, rb_=rb_: e.reciprocal(out=rs4[:, :], in_=rb_[0:4, 0:1]), r=[rb_], w=[rs4])
                            k.op(DVE, lambda e: e.tensor_scalar(out=ocsb[:, :], in0=ocs[:, :], scalar1=rs4[:, 0:1], scalar2=None, op0=ALU.mult), r=[ocs, rs4], w=[ocsb])
                            pb = bankb()
                            for ec in range(2):
                                k.op(PE, lambda e, ec=ec, pb=pb: e.transpose(pb[:, ec * 4:(ec + 1) * 4], ocsb[0:4, ec * 128:(ec + 1) * 128], ident[0:4, 0:4]),
                                     r=[ocsb, ident], w=[pb])
                            k.op(DVE, lambda e, pb=pb, s=s: e.tensor_copy(out=ocT[:, :, s].rearrange("p (h e) -> p e h", e=2),
                                                                          in_=pb[:, 0:8].rearrange("p (e h) -> p e h", h=4)), r=[pb], w=[ocT])
                    cab = [bankf(), bankf()]
                    for hh in range(2):
                        mm_tm(ocT, R, wcoo, KC, hh * 512, 512, cab[hh])
                    x2b = x2[t % 2]
                    postnorm_res(st, cab, R, D, x1, x2b)
                    if os.environ.get("K_DBG", "") == "x1":
                        k.dma(POOL, X2[t, 0:R, :], x1[0:R, :], r=[x1], w=[X2])
                    else:
                        k.dma(POOL, X2[t, 0:R, :], x2b[0:R, :], r=[x2b], w=[X2])
            k.barrier()

        with PStack("4b") as e5:
            e5.chk()
            st = mk_small(e5)
            mk_gpost(e5, st, 2 * D, 3 * D)
            wup = k.sb(e5, [128, KC, 4096], BF16, "wup")
            wdn = k.sb(e5, [128, 32, D], BF16, "wdn")
            load_w(wup, w_ff_up, KC, 4096, gain_col=24)
            load_w(wdn, w_ff_down, 32, D)
            x2 = [k.sb(e5, [128, D], F32, "x2") for _ in range(2)]
            yo = k.sb(e5, [128, D], F32, "yo")
            xn3T = k.sb(e5, [128, 8, 256], BF16, "xn3T")
            hr = [k.sb(e5, [128, 512], F32, "hr") for _ in range(2)]
            hT = k.sb(e5, [128, 32, 256], BF16, "hT")
            groups = [[(t_, 128) for t_ in range(g_, g_ + 2)] for g_ in range(0, T, 2)] + [[(T, SPC)]]
            hri = 0
            for grp in groups:
                W = sum(R_ for _, R_ in grp)
                offs = []
                o_ = 0
                for gi, (t, R) in enumerate(grp):
                    offs.append(o_)
                    k.dma(SP, x2[gi][0:R, :], X2[t, 0:R, :], r=[X2], w=[x2[gi]])
                    normT(st, x2[gi], R, xn3T, c0=o_)
                    o_ += R
                per = min(4, 512 // W)
                for f0 in range(0, 32, per):
                    bk = bankf()
                    for i in range(per):
                        fc = f0 + i
                        for c in range(KC):
                            k.op(PE, lambda e, c=c, fc=fc, i=i, bk=bk, W=W: e.matmul(bk[:, i * W:(i + 1) * W], lhsT=wup[:, c, fc * 128:(fc + 1) * 128], rhs=xn3T[:, c, 0:W],
                                                                                    start=(c == 0), stop=(c == KC - 1)), r=[wup, xn3T], w=[bk])
                    hri += 1
                    hrb = hr[hri % 2]
                    k.op(ACT, lambda e, bk=bk, hrb=hrb, W=W, per=per: e.activation(out=hrb[:, 0:per * W], in_=bk[:, 0:per * W], func=AF.Relu), r=[bk], w=[hrb])
                    k.op(POOL if hri % 2 else DVE, lambda e, hrb=hrb, f0=f0, W=W, per=per: e.tensor_tensor(
                        out=hT[:, f0:f0 + per, 0:W], in0=hrb[:, 0:per * W].rearrange("p (a b) -> p a b", b=W),
                        in1=hrb[:, 0:per * W].rearrange("p (a b) -> p a b", b=W), op=ALU.mult), r=[hrb], w=[hT])
                for gi, (t, R) in enumerate(grp):
                    o0 = offs[gi]
                    fb = [bankf(), bankf()]
                    for hh in range(2):
                        for c in range(32):
                            k.op(PE, lambda e, c=c, hh=hh, o0=o0, R=R, fb=fb: e.matmul(fb[hh][0:R, 0:512], lhsT=hT[:, c, o0:o0 + R], rhs=wdn[:, c, hh * 512:(hh + 1) * 512],
                                                                                      start=(c == 0), stop=(c == 31)), r=[hT, wdn], w=[fb[hh]])
                    postnorm_res(st, fb, R, 0, x2[gi], yo)
                    if t == T:
                        out_toks.append(k.dma(POOL, y_s[:, :], yo[0:R, :], r=[yo]))
                    else:
                        out_toks.append(k.dma(POOL, y_p[t, :, :], yo[:, :], r=[yo]))
        for tk in out_toks:
            k._wait(SP, tk)
        k.barrier()
    return nc


def _rope_tables(pos):
    inv = 1.0 / (10000.0 ** (np.arange(0, 32, 2, dtype=np.float32) / 32.0))
    ang = pos.astype(np.float32)[:, None] * inv[None, :].astype(np.float32)
    return np.cos(ang).astype(np.float32), np.sin(ang).astype(np.float32)


_CACHE = {}


def kernel(**inp):
    x_prompt = np.asarray(inp["x_prompt"]); x_sample = np.asarray(inp["x_sample"])
    Bp, SEQ, _ = x_prompt.shape
    DEC = x_sample.shape[0]
    NPOOL, PS = inp["cache_ckv"].shape[1], inp["cache_ckv"].shape[2]
    NPG = inp["page_table"].shape[1]
    assert Bp == 2 and NPG == 128 and DEC % 8 == 0
    NB = SEQ // 128
    T = NB // 4
    SPC = DEC // 8
    past_len = NPG * PS
    key = (T, NB, SPC, PS, NPOOL)
    if key not in _CACHE:
        _CACHE[key] = build(*key)
    nc = _CACHE[key]

    f32 = lambda a: np.ascontiguousarray(np.asarray(a, dtype=np.float32))
    cos_all, sin_all = _rope_tables(np.arange(SEQ))
    cos_s, sin_s = _rope_tables(np.array([past_len]))
    pool_ckv = f32(inp["cache_ckv"][0]).reshape(NPOOL * (PS // 16), 4096)
    pool_kr = f32(inp["cache_krope"][0]).reshape(NPOOL * (PS // 16), 512)
    gp = np.zeros((128, 40), np.float32)
    for i, nm in enumerate(("norm_mix_pre_g", "norm_ca_pre_g", "mem_norm_g", "norm_mlp_pre_g")):
        gp[:, i * 8:(i + 1) * 8] = f32(inp[nm][0]).reshape(8, 128).T
    gp[:, 32:35] = f32(inp["q_norm_g"][0]).reshape(3, 128).T
    gbc = np.concatenate([f32(inp["norm_mix_post_g"][0]), f32(inp["norm_ca_post_g"][0]), f32(inp["norm_mlp_post_g"][0]),
                          f32(inp["kv_norm_g"][0])])[None, :].repeat(128, 0)
    conv_wp = np.ascontiguousarray(f32(inp["conv_w"][0]).reshape(3, 4, 128).transpose(2, 1, 0).reshape(128, 12))
    bd = np.zeros((SPC, SPC, 8), np.float32)
    for s in range(SPC):
        bd[s, s, :] = 1.0
    hm = np.zeros((4, 4, 256), np.float32)
    for h in range(4):
        hm[h, h, :] = 1.0
    shared = {
        "pool_ckv": pool_ckv, "pool_kr": pool_kr,
        "ropek": np.ascontiguousarray(np.concatenate([cos_all, sin_all], 1).reshape(NB, 128, 32).transpose(1, 0, 2).reshape(128, NB * 32)),
        "bdmask": bd.reshape(SPC, SPC * 8), "hmask": hm.reshape(4, 1024),
        "w_in": f32(inp["w_in"][0]), "conv_wp": conv_wp, "w_conv_out": f32(inp["w_conv_out"][0]),
        "w_uq": f32(inp["w_uq"][0]).reshape(384, 768), "w_uk": f32(inp["w_uk"][0]).reshape(256, 512),
        "w_uv": f32(inp["w_uv"][0]).reshape(256, 512), "w_mla_out": f32(inp["w_mla_out"][0]),
        "w_mix_out": f32(inp["w_mix_out"][0]), "w_ca_q": f32(inp["w_ca_q"][0]).reshape(D, D),
        "w_ca_k": f32(inp["w_ca_k"][0]).reshape(D, D), "w_ca_v": f32(inp["w_ca_v"][0]).reshape(D, D),
        "w_ca_o": f32(inp["w_ca_o"][0]).reshape(D, D), "w_ff_up": f32(inp["w_ff_up"][0]),
        "w_ff_down": f32(inp["w_ff_down"][0]), "gpart": gp, "gbc": np.ascontiguousarray(gbc),
    }
    ropes = np.concatenate([np.tile(cos_s, (1, 8)), np.tile(sin_s, (1, 8)), cos_s, sin_s], 1).repeat(SPC, 0)
    in_maps = []
    kk = np.arange(128)[:, None]
    qq = np.arange(128)[None, :]
    tri = (kk <= qq).astype(np.float32)
    for c in range(8):
        b, j = c // 4, c % 4
        xb = f32(x_prompt[b]).reshape(NB, 128, D)
        blocks = [4 * t + j for t in range(T)]
        xq = np.ascontiguousarray(xb[blocks])
        xh = np.zeros((2 * T, D), np.float32)
        for t, g in enumerate(blocks):
            if g > 0:
                xh[2 * t:2 * t + 2] = xb[g - 1, 126:128]
        masks = np.zeros((16, 128, 512), np.float32)
        for d in range(16):
            for qi in range(4):
                lim = 4 * qi + j
                if d < lim:
                    masks[d, :, qi * 128:(qi + 1) * 128] = 1.0
                elif d == lim:
                    masks[d, :, qi * 128:(qi + 1) * 128] = tri
        cq = cos_all.reshape(NB, 128, 16)[blocks]
        sq = sin_all.reshape(NB, 128, 16)[blocks]
        ropeq = np.concatenate([np.tile(cq, (1, 1, 8)), np.tile(sq, (1, 1, 8))], 2)
        sl = slice(c * SPC, (c + 1) * SPC)
        m = dict(shared)
        m.update({
            "xq": xq, "xh": xh, "xs": np.ascontiguousarray(xb), "xsm": f32(x_sample[sl, 0, :]),
            "memp": f32(inp["mem_prompt"][b]), "stc": f32(inp["state_conv"][0, sl]),
            "cmk": f32(inp["cache_mem_k"][0, sl]).reshape(SPC, NMEM, D), "cmv": f32(inp["cache_mem_v"][0, sl]).reshape(SPC, NMEM, D),
            "ptT": np.ascontiguousarray(np.concatenate([np.asarray(inp["page_table"])[sl].T.astype(np.int32) * (PS // 16) + ch_ for ch_ in range(PS // 16)], 1)),
            "masks": masks, "ropeq": np.ascontiguousarray(ropeq.astype(np.float32)), "ropes": np.ascontiguousarray(ropes.astype(np.float32)),
        })
        in_maps.append(m)
    res = run_bass_kernel_spmd(nc, in_maps, core_ids=list(range(8))).results

    y_prompt = np.zeros((2, SEQ, D), np.float32)
    ckv_prompt = np.zeros((1, 2, SEQ, 256), np.float32)
    kr_prompt = np.zeros((1, 2, SEQ, 32), np.float32)
    for c in range(8):
        b, j = c // 4, c % 4
        for t in range(T):
            g = 4 * t + j
            y_prompt[b, g * 128:(g + 1) * 128] = res[c]["y_p"][t]
            ckv_prompt[0, b, g * 128:(g + 1) * 128] = res[c]["ckv_p"][t]
            kr_prompt[0, b, g * 128:(g + 1) * 128] = res[c]["kr_p"][t]
    y_sample = np.concatenate([res[c]["y_s"] for c in range(8)], 0).reshape(DEC, 1, D)
    conv_prompt = np.stack([res[3]["conv_p"], res[7]["conv_p"]], 0)[None]
    mem_k = np.stack([res[0]["mk_p"], res[4]["mk_p"]], 0).reshape(1, 2, NMEM, 4, 256)
    mem_v = np.stack([res[0]["mv_p"], res[4]["mv_p"]], 0).reshape(1, 2, NMEM, 4, 256)
    ckv_sample = np.concatenate([res[c]["ckv_s"] for c in range(8)], 0).reshape(1, DEC, 1, 256)
    kr_sample = np.concatenate([res[c]["kr_s"] for c in range(8)], 0).reshape(1, DEC, 1, 32)
    conv_sample = np.concatenate([res[c]["conv_s"] for c in range(8)], 0).reshape(1, DEC, 2, 512)
    return (y_prompt, y_sample, ckv_prompt, kr_prompt, conv_prompt.astype(np.float32), mem_k, mem_v,
            ckv_sample, kr_sample, conv_sample)
```

```python
import numpy as np
from contextlib import ExitStack
import concourse.bass as bass
import concourse.mybir as mybir
from concourse.bass_utils import run_bass_kernel_spmd

F32 = mybir.dt.float32
BF16 = mybir.dt.bfloat16
I32 = mybir.dt.int32
AF = mybir.ActivationFunctionType
ALU = mybir.AluOpType
AX = mybir.AxisListType

D = 1024
KC = 8
OFF_H, OFF_GB, OFF_GC, OFF_CQ, OFF_CKV, OFF_KR, OFF_GCONV, OFF_GMLA, IN_COLS = (
    0, 512, 1024, 1536, 1920, 2176, 2208, 3232, 4256)
EPS = 1e-6
NS = 8
SCALE = 96 ** -0.5
NMEM = 256


import os
_PH = os.environ.get("K_PHASES", "2,1,3,3s,4a,4b").split(",")


class _Skip(Exception):
    pass


class PStack(ExitStack):
    def __init__(self, name):
        super().__init__()
        self.pname = name

    def chk(self):
        if self.pname not in _PH:
            raise _Skip()

    def __exit__(self, et, ev, tb):
        r = super().__exit__(None if et is _Skip else et, None if et is _Skip else ev, None if et is _Skip else tb)
        return True if et is _Skip else r


class Tok:
    __slots__ = ("sem", "val", "eng")

    def __init__(self, sem, val, eng):
        self.sem, self.val, self.eng = sem, val, eng


class Buf:
    def __init__(self, t, psum=False):
        self.t = t
        self.lw = None
        self.rd = {}
        self.wd = {}
        self.psum = psum

    def __getitem__(self, k):
        return self.t[k]


class Eng:
    def __init__(self, name, e, sem):
        self.name, self.e, self.sem = name, e, sem
        self.count = 0
        self.waited = {}


class B:
    def __init__(self, nc, es):
        self.nc, self.es = nc, es

        def mk(n, e):
            return Eng(n, e, es.enter_context(nc.semaphore("sem_" + n)))
        self.PE = mk("pe", nc.tensor)
        self.ACT = mk("act", nc.scalar)
        self.DVE = mk("dve", nc.vector)
        self.POOL = mk("pool", nc.gpsimd)
        self.SP = mk("sp", nc.sync)
        self.engs = [self.PE, self.ACT, self.DVE, self.POOL, self.SP]
        self.dq = {}
        for q in (self.SP, self.POOL, self.ACT):
            self.dq[q.name] = ([es.enter_context(nc.semaphore("d_%s%d" % (q.name, i))) for i in range(NS)], [0])
        self.uid = 0

    def sb(self, stack, shape, dt, name=None):
        self.uid += 1
        return Buf(stack.enter_context(self.nc.sbuf_tensor("%s_%d" % (name or "t", self.uid), list(shape), dt)))

    def psum(self, stack, shape, dt, name=None):
        self.uid += 1
        return Buf(stack.enter_context(self.nc.psum_tensor("%s_%d" % (name or "p", self.uid), list(shape), dt)), psum=True)

    def _wait(self, E, tok):
        if tok is None:
            return
        if tok.eng is E and E is self.PE:
            return
        k = tok.sem.num
        if E.waited.get(k, -1) >= tok.val:
            return
        E.e.wait_ge(tok.sem, tok.val)
        E.waited[k] = tok.val

    def _deps(self, E, r, w, wd=()):
        for b in r:
            self._wait(E, b.lw)
            for t in list(b.wd.values()):
                self._wait(E, t)
        for b in w:
            self._wait(E, b.lw)
            for t in list(b.wd.values()):
                self._wait(E, t)
            for t in list(b.rd.values()):
                self._wait(E, t)
        for b in wd:
            self._wait(E, b.lw)
            for t in list(b.rd.values()):
                self._wait(E, t)

    def _upd(self, tok, r, w, wd=()):
        for b in r:
            b.rd[tok.sem.num] = tok
        for b in w:
            b.lw = tok
            b.rd = {}
            b.wd = {}
        for b in wd:
            b.wd[tok.sem.num] = tok

    def op(self, E, fn, r=(), w=(), wd=()):
        w = list(w) + [b for b in r if b.psum and b not in w]
        r = [b for b in r if b not in w]
        self._deps(E, r, w, wd)
        inst = fn(E.e)
        E.count += 1
        inst.then_inc(E.sem, 1)
        tok = Tok(E.sem, E.count, E)
        self._upd(tok, r, w, wd)
        return tok

    def dma(self, Q, out_ap, in_ap, r=(), w=(), idx=None, slow=False):
        sems, ctr = self.dq[Q.name]
        i = ctr[0]
        ctr[0] += 1
        sem = sems[i % NS]
        prev = 16 * (i // NS)
        if prev > 0 and Q.waited.get(sem.num, -1) < prev:
            Q.e.wait_ge(sem, prev)
            Q.waited[sem.num] = prev
        self._deps(Q, r, w)
        if idx is None:
            if slow:
                inst = Q.e.dma_start(out=out_ap, in_=in_ap, allow_slow_non_contiguous=True)
            else:
                inst = Q.e.dma_start(out=out_ap, in_=in_ap)
        else:
            inst = Q.e.indirect_dma_start(out=out_ap, out_offset=None, in_=in_ap,
                                          in_offset=bass.IndirectOffsetOnAxis(ap=idx, axis=0))
        inst.then_inc(sem, 16)
        tok = Tok(sem, prev + 16, None)
        self._upd(tok, r, w)
        return tok

    def barrier(self):
        toks = [Tok(E.sem, E.count, None) for E in self.engs if E.count > 0]
        for name, (sems, ctr) in self.dq.items():
            n = ctr[0]
            for k, sem in enumerate(sems):
                cnt = (n - k + NS - 1) // NS if n > k else 0
                if cnt > 0:
                    toks.append(Tok(sem, 16 * cnt, None))
        for E in self.engs:
            for t in toks:
                self._wait(E, t)


def build(T, NB, SPC, PS, NPOOL):
    assert T % 4 == 0 and PS % 16 == 0
    CH = PS // 16
    NG = T // 4
    S = NB * 128
    nc = bass.Bass("TRN2", target_bir_lowering=False)

    def din(name, shape, dt=F32):
        return nc.dram_tensor(name, list(shape), dt, kind="ExternalInput").ap()

    def dout(name, shape):
        return nc.dram_tensor(name, list(shape), F32, kind="ExternalOutput").ap()

    def dscr(name, shape, dt=F32):
        return Buf(nc.dram_tensor(name, list(shape), dt, kind="Internal").ap())

    xq = din("xq", [T, 128, D]); xh = din("xh", [2 * T, D]); xs = din("xs", [NB, 128, D])
    xsm = din("xsm", [SPC, D]); memp = din("memp", [NMEM, D])
    pool_ckv = din("pool_ckv", [NPOOL * CH, 4096]); pool_kr = din("pool_kr", [NPOOL * CH, 512])
    stc = din("stc", [SPC, 2, 512]); cmk = din("cmk", [SPC, NMEM, D]); cmv = din("cmv", [SPC, NMEM, D])
    ptT = din("ptT", [128, CH * SPC], I32)
    masks = din("masks", [16, 128, 512])
    ropeq = din("ropeq", [T, 128, 256])
    ropek = din("ropek", [128, NB * 32])
    ropes = din("ropes", [SPC, 288])
    bdmask = din("bdmask", [SPC, SPC * 8])
    hmask = din("hmask", [4, 1024])
    w_in = din("w_in", [D, IN_COLS]); conv_wp = din("conv_wp", [128, 12])
    w_conv_out = din("w_conv_out", [512, D]); w_uq = din("w_uq", [384, 768])
    w_uk = din("w_uk", [256, 512]); w_uv = din("w_uv", [256, 512])
    w_mla_out = din("w_mla_out", [512, D]); w_mix_out = din("w_mix_out", [D, D])
    w_ca_q = din("w_ca_q", [D, D]); w_ca_k = din("w_ca_k", [D, D]); w_ca_v = din("w_ca_v", [D, D])
    w_ca_o = din("w_ca_o", [D, D]); w_ff_up = din("w_ff_up", [D, 4096]); w_ff_down = din("w_ff_down", [4096, D])
    gpart = din("gpart", [128, 5 * 8])
    gbc = din("gbc", [128, 3 * D + 256])

    y_p = dout("y_p", [T, 128, D]); y_s = dout("y_s", [SPC, D])
    ckv_p = dout("ckv_p", [T, 128, 256]); kr_p = dout("kr_p", [T, 128, 32])
    conv_p = dout("conv_p", [2, 512]); mk_p = dout("mk_p", [NMEM, D]); mv_p = dout("mv_p", [NMEM, D])
    ckv_s = dout("ckv_s", [SPC, 256]); kr_s = dout("kr_s", [SPC, 32]); conv_s = dout("conv_s", [SPC, 2, 512])

    NT = T + 1
    T1 = dscr("scr_t1", [NT, 128, D]); SG = dscr("scr_sg", [NT, 128, D]); X2 = dscr("scr_x2", [NT, 128, D])
    OA = dscr("scr_oa", [NT, 128, 512], BF16)

    with ExitStack() as es:
        k = B(nc, es)
        PE, ACT, DVE, POOL, SP = k.PE, k.ACT, k.DVE, k.POOL, k.SP
        out_toks = []
        rr = [0]

        def alt():
            rr[0] += 1
            return DVE if rr[0] % 2 else POOL

        ident = k.sb(es, [128, 128], BF16, "ident")
        identf = k.sb(es, [128, 128], F32, "identf")
        gp = k.sb(es, [128, 40], F32, "gp")
        gkv = k.sb(es, [128, 256], F32, "gkv")
        cw = k.sb(es, [128, 12], F32, "cw")
        psf = [k.psum(es, [128, 512], F32, "psf") for _ in range(6)]
        psb = [k.psum(es, [128, 1024], BF16, "psb") for _ in range(2)]
        pfi = [0]
        pbi = [0]

        pinned = set()

        def bankf(pin=False):
            while True:
                pfi[0] += 1
                b = psf[pfi[0] % 6]
                if id(b) not in pinned:
                    break
            if pin:
                pinned.add(id(b))
            return b

        def unpin(b):
            pinned.discard(id(b))

        def bankb():
            pbi[0] += 1
            return psb[pbi[0] % 2]

        k.op(POOL, lambda e: e.memset(ident[:], 1.0), w=[ident])
        k.op(POOL, lambda e: e.affine_select(out=ident[:], in_=ident[:], pattern=[[-1, 128]], compare_op=ALU.is_equal,
                                             fill=0.0, base=0, channel_multiplier=1), r=[ident], w=[ident])
        k.op(POOL, lambda e: e.memset(identf[:], 1.0), w=[identf])
        k.op(POOL, lambda e: e.affine_select(out=identf[:], in_=identf[:], pattern=[[-1, 128]], compare_op=ALU.is_equal,
                                             fill=0.0, base=0, channel_multiplier=1), r=[identf], w=[identf])
        k.dma(SP, gp[:], gpart[:, :], w=[gp])
        k.dma(SP, gkv[:], gbc[:, 3 * D:3 * D + 256], w=[gkv])
        k.dma(SP, cw[:], conv_wp[:, :], w=[cw])

        stg = [k.sb(es, [128, 1024], F32, "stg") for _ in range(4)]
        stq = [(stg[i], 0) for i in range(4)]
        sti = [0]

        def load_w(dst, src, kcw, ncols, gain_col=None, col0=0, dcol0=0):
            for kc_ in range(kcw):
                for c0 in range(0, ncols, 1024):
                    n = min(1024, ncols - c0)
                    sti[0] += 1
                    i = sti[0]
                    st, so = stq[i % 4]
                    sv = st[:, so:so + n]
                    k.dma(SP, sv, src[kc_ * 128:(kc_ + 1) * 128, col0 + c0:col0 + c0 + n], w=[st])
                    o = dst[:, kc_, dcol0 + c0:dcol0 + c0 + n]
                    if gain_col is None:
                        E = (DVE, POOL, ACT)[i % 3]
                        if E is ACT:
                            k.op(ACT, lambda e, o=o, sv=sv: e.copy(out=o, in_=sv), r=[st], wd=[dst])
                        else:
                            k.op(E, lambda e, o=o, sv=sv: e.tensor_copy(out=o, in_=sv), r=[st], wd=[dst])
                    else:
                        g = gp[:, gain_col + kc_:gain_col + kc_ + 1]
                        if i % 2:
                            k.op(DVE, lambda e, o=o, sv=sv, g=g: e.tensor_scalar(out=o, in0=sv, scalar1=g, scalar2=None, op0=ALU.mult),
                                 r=[st, gp], wd=[dst])
                        else:
                            k.op(ACT, lambda e, o=o, sv=sv, g=g: e.activation(out=o, in_=sv, func=AF.Copy, scale=g), r=[st, gp], wd=[dst])

        def rstd_from_ss(ss_ap, out_ap, ssb, outb, n, R):
            k.op(ACT, lambda e: e.activation(out=out_ap, in_=ss_ap, func=AF.Sqrt, bias=EPS, scale=1.0 / n), r=[ssb], w=[outb])
            k.op(DVE, lambda e: e.reciprocal(out=out_ap, in_=out_ap), r=[outb], w=[outb])

        def normT(st, xt, R, dstT, c0=0):
            junk, ss, rs, xb = st["junk"], st["ss"], st["rs"], st["xb"]
            k.op(ACT, lambda e: e.activation(out=junk[0:R, :], in_=xt[0:R, :], func=AF.Square, accum_out=ss[0:R, 0:1]),
                 r=[xt], w=[junk, ss])
            rstd_from_ss(ss[0:R, 0:1], rs[0:R, 0:1], ss, rs, D, R)
            k.op(DVE, lambda e: e.tensor_scalar(out=xb[0:R, :], in0=xt[0:R, :], scalar1=rs[0:R, 0:1], scalar2=None, op0=ALU.mult),
                 r=[xt, rs], w=[xb])
            pb = bankb()
            pv = pb[:].rearrange("p (a b) -> p a b", b=128)
            for c in range(8):
                k.op(PE, lambda e, c=c: e.transpose(pv[:, c, 0:R], xb[0:R, c * 128:(c + 1) * 128], ident[0:R, 0:R]),
                     r=[xb, ident], w=[pb])
            k.op(ACT, lambda e: e.copy(out=dstT[:, :, c0:c0 + R], in_=pv[:, :, 0:R]), r=[pb], w=[dstT])

        def transposeT(src, R, ncol, dstT, E=None):
            nchunk = ncol // 128
            pb = bankb()
            pv = pb[:].rearrange("p (a b) -> p a b", b=128)
            for c in range(nchunk):
                k.op(PE, lambda e, c=c: e.transpose(pv[:, c, 0:R], src[0:R, c * 128:(c + 1) * 128], ident[0:R, 0:R]),
                     r=[src, ident], w=[pb])
            k.op(E or DVE, lambda e: e.tensor_copy(out=dstT[:, 0:nchunk, 0:R], in_=pv[:, 0:nchunk, 0:R]), r=[pb], w=[dstT])

        def mm_tm(xT, R, wt, kcw, col0, ncols, bank, bcol0=0):
            for c in range(kcw):
                k.op(PE, lambda e, c=c: e.matmul(bank[0:R, bcol0:bcol0 + ncols], lhsT=xT[:, c, 0:R],
                                                 rhs=wt[:, c, col0:col0 + ncols], start=(c == 0), stop=(c == kcw - 1)),
                     r=[xT, wt], w=[bank])

        def ckv_post(st, bank, R, cos_ap, sin_ap, ropeb, out_ck, out_kr, outb):
            junk, ss, rs = st["junk"], st["ss2"], st["rs2"]
            k.op(ACT, lambda e: e.activation(out=junk[0:R, 0:256], in_=bank[0:R, 0:256], func=AF.Square, accum_out=ss[0:R, 0:1]),
                 r=[bank], w=[junk, ss])
            rstd_from_ss(ss[0:R, 0:1], rs[0:R, 0:1], ss, rs, 256, R)
            k.op(DVE, lambda e: e.scalar_tensor_tensor(out=out_ck, in0=bank[0:R, 0:256], scalar=rs[0:R, 0:1],
                                                       in1=gkv[0:R, 0:256], op0=ALU.mult, op1=ALU.mult),
                 r=[bank, rs, gkv], w=[outb])
            kr = st["kr"]
            tm = st["tm"]
            k.op(ACT, lambda e: e.copy(out=kr[0:R, 0:32], in_=bank[0:R, 256:288]), r=[bank], w=[kr])
            k.op(DVE, lambda e: e.tensor_tensor(out=tm[0:R, 0:16], in0=kr[0:R, 0:16], in1=cos_ap, op=ALU.mult), r=[kr, ropeb], w=[tm])
            k.op(DVE, lambda e: e.tensor_tensor(out=tm[0:R, 16:32], in0=kr[0:R, 16:32], in1=sin_ap, op=ALU.mult), r=[kr, ropeb], w=[tm])
            k.op(DVE, lambda e: e.tensor_tensor(out=out_kr[:, 0:16], in0=tm[0:R, 0:16], in1=tm[0:R, 16:32], op=ALU.subtract),
                 r=[tm], w=[outb])
            k.op(DVE, lambda e: e.tensor_tensor(out=tm[0:R, 32:48], in0=kr[0:R, 0:16], in1=sin_ap, op=ALU.mult), r=[kr, ropeb], w=[tm])
            k.op(DVE, lambda e: e.tensor_tensor(out=tm[0:R, 48:64], in0=kr[0:R, 16:32], in1=cos_ap, op=ALU.mult), r=[kr, ropeb], w=[tm])
            k.op(DVE, lambda e: e.tensor_tensor(out=out_kr[:, 16:32], in0=tm[0:R, 32:48], in1=tm[0:R, 48:64], op=ALU.add),
                 r=[tm], w=[outb])

        def postnorm_res(st, banks, R, gcol, xres, xout):
            gb = st["gpost"]
            junk, ss, rs = st["junk"], st["ss3"], st["rs3"]
            for hh in range(2):
                k.op(ACT, lambda e, hh=hh: e.activation(out=junk[0:R, hh * 512:(hh + 1) * 512], in_=banks[hh][0:R, :], func=AF.Square,
                                                       accum_out=ss[0:R, hh:hh + 1]), r=[banks[hh]], w=[junk, ss])
            k.op(DVE, lambda e: e.tensor_tensor(out=ss[0:R, 2:3], in0=ss[0:R, 0:1], in1=ss[0:R, 1:2], op=ALU.add), r=[ss], w=[ss])
            rstd_from_ss(ss[0:R, 2:3], rs[0:R, 0:1], ss, rs, D, R)
            for hh in range(2):
                sl = slice(hh * 512, (hh + 1) * 512)
                k.op(DVE, lambda e, hh=hh, sl=sl: e.scalar_tensor_tensor(out=junk[0:R, sl], in0=banks[hh][0:R, :], scalar=rs[0:R, 0:1],
                                                                        in1=gb[0:R, gcol + hh * 512:gcol + (hh + 1) * 512],
                                                                        op0=ALU.mult, op1=ALU.mult), r=[banks[hh], rs, gb], w=[junk])
                k.op(POOL, lambda e, sl=sl: e.tensor_tensor(out=xout[0:R, sl], in0=junk[0:R, sl], in1=xres[0:R, sl], op=ALU.add),
                     r=[junk, xres], w=[xout])

        def mk_small(stack):
            return {
                "junk": k.sb(stack, [128, D], F32, "junk"), "xb": k.sb(stack, [128, D], BF16, "xb"),
                "ss": k.sb(stack, [128, 1], F32, "ss"), "rs": k.sb(stack, [128, 1], F32, "rs"),
                "ss2": k.sb(stack, [128, 1], F32, "ss2"), "rs2": k.sb(stack, [128, 1], F32, "rs2"),
                "ss3": k.sb(stack, [128, 4], F32, "ss3"), "rs3": k.sb(stack, [128, 1], F32, "rs3"),
                "kr": k.sb(stack, [128, 32], F32, "kr"), "tm": k.sb(stack, [128, 64], F32, "tm"),
            }

        def mk_gpost(stack, st, lo=0, hi=3 * D):
            g = k.sb(stack, [128, hi - lo], F32, "gpost")
            k.dma(SP, g[:], gbc[:, lo:hi], w=[g])
            st["gpost"] = g

        with ExitStack() as e13:
            qT = k.sb(e13, [128, 8, T * 128], BF16, "qT")
            qlT = k.sb(e13, [128, 2, SPC, 8], BF16, "qlT")
            qrT = k.sb(e13, [32, SPC, 8], BF16, "qrT")
            cknb = k.sb(e13, [SPC, 257], BF16, "cknb")
            cknT = k.sb(e13, [128, 2, SPC], BF16, "cknT")
            krnT = k.sb(e13, [32, SPC], BF16, "krnT")
            wuk = k.sb(e13, [128, 2, 512], BF16, "wuk")
            wuv = k.sb(e13, [128, 2, 512], BF16, "wuv")
            load_w(wuk, w_uk, 2, 512)
            load_w(wuv, w_uv, 2, 512)

            with PStack("2") as e2:
                e2.chk()
                st = mk_small(e2)
                win = k.sb(e2, [128, KC, IN_COLS], BF16, "win")
                wuq = k.sb(e2, [128, 3, 768], BF16, "wuq")
                wco = k.sb(e2, [128, 4, D], BF16, "wco")
                load_w(win, w_in, KC, IN_COLS, gain_col=0)
                load_w(wuq, w_uq, 3, 768, gain_col=32)
                load_w(wco, w_conv_out, 4, D)
                xt = [k.sb(e2, [128, D], F32, "xt") for _ in range(2)]
                xnT = [k.sb(e2, [128, 8, 128], BF16, "xnT") for _ in range(2)]
                uTh = k.sb(e2, [128, 4, 2 * T], F32, "uTh")
                uT = k.sb(e2, [128, 4, 130], F32, "uT")
                hs = k.sb(e2, [128, 128], F32, "hs")
                acc = k.sb(e2, [128, 128], F32, "acc")
                ycT = k.sb(e2, [128, 4, 128], BF16, "ycT")
                sgc = k.sb(e2, [128, 512], F32, "sgc")
                t1 = [k.sb(e2, [128, D], F32, "t1")] * 2
                sgm = [k.sb(e2, [128, D], F32, "sgm")] * 2
                cko = [k.sb(e2, [128, 288], F32, "cko") for _ in range(2)]
                cqn = k.sb(e2, [128, 384], BF16, "cqn")
                cqnT = k.sb(e2, [128, 3, 128], BF16, "cqnT")
                qf = k.sb(e2, [128, 8, 96], F32, "qf")
                qtm = k.sb(e2, [128, 4, 128], F32, "qtm")
                qb = k.sb(e2, [128, 8, 96], BF16, "qb")
                rq = [k.sb(e2, [128, 256], F32, "rq") for _ in range(2)]
                stt = k.sb(e2, [SPC, 2, 512], F32, "stt")
                stT = k.sb(e2, [128, 2, 4, SPC], F32, "stT")
                utm = st["junk"]
                rsm = k.sb(e2, [SPC, 288], F32, "rsm")
                wukT = k.sb(e2, [64, 8, 256], BF16, "wukT")

                k.dma(SP, xt[0][0:2 * T, :], xh[:, :], w=[xt[0]])
                normT(st, xt[0], 2 * T, xnT[0])
                for fc in range(4):
                    bk = bankf()
                    for part, off in ((0, OFF_H), (1, OFF_GC)):
                        for c in range(KC):
                            k.op(PE, lambda e, c=c, off=off, part=part, fc=fc, bk=bk: e.matmul(
                                bk[:, part * 128:part * 128 + 2 * T], lhsT=win[:, c, off + fc * 128:off + (fc + 1) * 128],
                                rhs=xnT[0][:, c, 0:2 * T], start=(c == 0), stop=(c == KC - 1)), r=[win, xnT[0]], w=[bk])
                    k.op(ACT, lambda e, bk=bk: e.copy(out=hs[:, 0:2 * T], in_=bk[:, 0:2 * T]), r=[bk], w=[hs])
                    k.op(DVE, lambda e, bk=bk, fc=fc: e.tensor_tensor(out=uTh[:, fc, :], in0=hs[:, 0:2 * T], in1=bk[:, 128:128 + 2 * T],
                                                                       op=ALU.mult), r=[hs, bk], w=[uTh])

                k.dma(SP, stt[:], stc[:, :, :], w=[stt])
                for kk in range(2):
                    bk = bankf()
                    for fc in range(4):
                        k.op(PE, lambda e, kk=kk, fc=fc, bk=bk: e.transpose(bk[:, fc * SPC:(fc + 1) * SPC], stt[0:SPC, kk, fc * 128:(fc + 1) * 128],
                                                                           identf[0:SPC, 0:SPC]), r=[stt, identf], w=[bk])
                    k.op(DVE, lambda e, kk=kk, bk=bk: e.tensor_copy(out=stT[:, kk, :, :], in_=bk[:, 0:4 * SPC].rearrange("p (a b) -> p a b", b=SPC)),
                         r=[bk], w=[stT])
                for h in range(8):
                    pb = bankb()
                    for lc in range(2):
                        k.op(PE, lambda e, h=h, lc=lc, pb=pb: e.transpose(pb[0:64, lc * 128:(lc + 1) * 128], wuk[:, lc, h * 64:(h + 1) * 64],
                                                                         ident[:, :]), r=[wuk, ident], w=[pb])
                    k.op(DVE, lambda e, h=h, pb=pb: e.tensor_copy(out=wukT[:, h, :], in_=pb[0:64, 0:256]), r=[pb], w=[wukT])

                for t in range(NT):
                    smp = (t == T)
                    R = SPC if smp else 128
                    x_t = xt[t % 2]
                    xn = xnT[t % 2]
                    if smp:
                        k.dma(SP, x_t[0:R, :], xsm[:, :], w=[x_t])
                        k.dma(SP, rsm[:], ropes[:, :], w=[rsm])
                    else:
                        k.dma(SP, x_t[:, :], xq[t, :, :], w=[x_t])
                        k.dma(SP, rq[t % 2][:], ropeq[t, :, :], w=[rq[t % 2]])
                    normT(st, x_t, R, xn)
                    if not smp:
                        k.op(POOL, lambda e, t=t: e.tensor_copy(out=uT[:, :, 0:2], in_=uTh[:, :, 2 * t:2 * t + 2]), r=[uTh], w=[uT])
                    for fc in range(4):
                        bk = bankf()
                        for part, off in ((0, OFF_H), (1, OFF_GB), (2, OFF_GC)):
                            for c in range(KC):
                                k.op(PE, lambda e, c=c, off=off, part=part, fc=fc, bk=bk, xn=xn, R=R: e.matmul(
                                    bk[:, part * 128:part * 128 + R], lhsT=win[:, c, off + fc * 128:off + (fc + 1) * 128],
                                    rhs=xn[:, c, 0:R], start=(c == 0), stop=(c == KC - 1)), r=[win, xn], w=[bk])
                        k.op(ACT, lambda e, bk=bk, R=R: e.copy(out=hs[:, 0:R], in_=bk[:, 0:R]), r=[bk], w=[hs])
                        k.op(DVE, lambda e, bk=bk, fc=fc, R=R: e.tensor_tensor(out=uT[:, fc, 2:2 + R], in0=hs[:, 0:R], in1=bk[:, 256:256 + R],
                                                                                op=ALU.mult), r=[hs, bk], w=[uT])
                        if smp:
                            a0, a1, a2 = stT[:, 0, fc, :], stT[:, 1, fc, :], uT[:, fc, 2:2 + R]
                            rd = [stT, uT, cw]
                        else:
                            a0, a1, a2 = uT[:, fc, 0:R], uT[:, fc, 1:1 + R], uT[:, fc, 2:2 + R]
                            rd = [uT, cw]
                        k.op(DVE, lambda e, a0=a0, fc=fc, R=R: e.tensor_scalar(out=acc[:, 0:R], in0=a0, scalar1=cw[:, fc * 3:fc * 3 + 1], scalar2=None,
                                                                               op0=ALU.mult), r=rd, w=[acc])
                        k.op(DVE, lambda e, a1=a1, fc=fc, R=R: e.scalar_tensor_tensor(out=acc[:, 0:R], in0=a1, scalar=cw[:, fc * 3 + 1:fc * 3 + 2],
                                                                                      in1=acc[:, 0:R], op0=ALU.mult, op1=ALU.add), r=rd + [acc], w=[acc])
                        k.op(DVE, lambda e, a2=a2, fc=fc, R=R: e.scalar_tensor_tensor(out=acc[:, 0:R], in0=a2, scalar=cw[:, fc * 3 + 2:fc * 3 + 3],
                                                                                      in1=acc[:, 0:R], op0=ALU.mult, op1=ALU.add), r=rd + [acc], w=[acc])
                        k.op(DVE, lambda e, bk=bk, fc=fc, R=R: e.tensor_tensor(out=ycT[:, fc, 0:R], in0=acc[:, 0:R], in1=bk[:, 128:128 + R], op=ALU.mult),
                             r=[acc, bk], w=[ycT])
                    if smp or t == T - 1:
                        bk = bankf()
                        ncol = R if smp else 2
                        c0 = 2 if smp else 128
                        for fc in range(4):
                            k.op(PE, lambda e, fc=fc, bk=bk, ncol=ncol, c0=c0: e.transpose(bk[0:ncol, fc * 128:(fc + 1) * 128], uT[:, fc, c0:c0 + ncol],
                                                                                          identf[:, :]), r=[uT, identf], w=[bk])
                        k.op(DVE, lambda e, bk=bk, ncol=ncol: e.tensor_copy(out=utm[0:ncol, 0:512], in_=bk[0:ncol, :]), r=[bk], w=[utm])
                        if smp:
                            out_toks.append(k.dma(POOL, conv_s[:, 1, :], utm[0:R, 0:512], r=[utm]))
                            out_toks.append(k.dma(POOL, conv_s[:, 0, :], stt[0:R, 1, :], r=[stt]))
                        else:
                            out_toks.append(k.dma(POOL, conv_p[:, :], utm[0:2, 0:512], r=[utm]))
                    ycb = [bankf(), bankf()]
                    for hh in range(2):
                        mm_tm(ycT, R, wco, 4, hh * 512, 512, ycb[hh])
                    t1b = t1[t % 2]
                    for hh in range(2):
                        bk = bankf()
                        mm_tm(xn, R, win, KC, OFF_GCONV + hh * 512, 512, bk)
                        k.op(ACT, lambda e, bk=bk, R=R: e.activation(out=sgc[0:R, :], in_=bk[0:R, :], func=AF.Sigmoid), r=[bk], w=[sgc])
                        k.op(DVE, lambda e, hh=hh, R=R, t1b=t1b: e.tensor_tensor(out=t1b[0:R, hh * 512:(hh + 1) * 512], in0=sgc[0:R, :],
                                                                                  in1=ycb[hh][0:R, :], op=ALU.mult), r=[sgc, ycb[hh]], w=[t1b])
                    k.dma(POOL, T1[t, 0:R, :], t1b[0:R, :], r=[t1b], w=[T1])
                    sgb = sgm[t % 2]
                    for hh in range(2):
                        bk = bankf()
                        mm_tm(xn, R, win, KC, OFF_GMLA + hh * 512, 512, bk)
                        k.op(ACT, lambda e, bk=bk, hh=hh, R=R, sgb=sgb: e.activation(out=sgb[0:R, hh * 512:(hh + 1) * 512], in_=bk[0:R, :],
                                                                                      func=AF.Sigmoid), r=[bk], w=[sgb])
                    k.dma(POOL, SG[t, 0:R, :], sgb[0:R, :], r=[sgb], w=[SG])
                    bk = bankf()
                    mm_tm(xn, R, win, KC, OFF_CKV, 288, bk)
                    ckb = cko[t % 2]
                    if smp:
                        cos_ap, sin_ap, ropeb = rsm[0:R, 256:272], rsm[0:R, 272:288], rsm
                    else:
                        cos_ap, sin_ap, ropeb = rq[t % 2][:, 0:16], rq[t % 2][:, 128:144], rq[t % 2]
                    ckv_post(st, bk, R, cos_ap, sin_ap, ropeb, ckb[0:R, 0:256], ckb[0:R, 256:288], ckb)
                    if smp:
                        out_toks.append(k.dma(POOL, ckv_s[:, :], ckb[0:R, 0:256], r=[ckb]))
                        out_toks.append(k.dma(POOL, kr_s[:, :], ckb[0:R, 256:288], r=[ckb]))
                        k.op(DVE, lambda e: e.tensor_copy(out=cknb[:, 0:256], in_=ckb[0:R, 0:256]), r=[ckb], w=[cknb])
                        k.op(POOL, lambda e: e.memset(cknb[:, 256:257], 1.0), w=[cknb])
                        k.op(DVE, lambda e: e.tensor_copy(out=st["xb"][0:R, 0:32], in_=ckb[0:R, 256:288]), r=[ckb], w=[st["xb"]])
                        pb = bankb()
                        for lc in range(2):
                            k.op(PE, lambda e, lc=lc, pb=pb: e.transpose(pb[:, lc * 128:lc * 128 + R], cknb[0:R, lc * 128:(lc + 1) * 128], ident[0:R, 0:R]),
                                 r=[cknb, ident], w=[pb])
                        k.op(PE, lambda e, pb=pb: e.transpose(pb[0:32, 256:256 + R], st["xb"][0:R, 0:32], ident[0:R, 0:R]), r=[st["xb"], ident], w=[pb])
                        k.op(DVE, lambda e, pb=pb: e.tensor_copy(out=cknT[:, :, :], in_=pb[:, 0:256].rearrange("p (a b) -> p a b", b=128)[:, :, 0:R]),
                             r=[pb], w=[cknT])
                        k.op(DVE, lambda e, pb=pb: e.tensor_copy(out=krnT[:, :], in_=pb[0:32, 256:256 + R]), r=[pb], w=[krnT])
                    else:
                        out_toks.append(k.dma(POOL, ckv_p[t, :, :], ckb[:, 0:256], r=[ckb]))
                        out_toks.append(k.dma(POOL, kr_p[t, :, :], ckb[:, 256:288], r=[ckb]))
                    bk = bankf()
                    mm_tm(xn, R, win, KC, OFF_CQ, 384, bk)
                    k.op(ACT, lambda e, bk=bk, R=R: e.activation(out=st["junk"][0:R, 0:384], in_=bk[0:R, 0:384], func=AF.Square,
                                                                 accum_out=st["ss"][0:R, 0:1]), r=[bk], w=[st["junk"], st["ss"]])
                    rstd_from_ss(st["ss"][0:R, 0:1], st["rs"][0:R, 0:1], st["ss"], st["rs"], 384, R)
                    k.op(DVE, lambda e, bk=bk, R=R: e.tensor_scalar(out=cqn[0:R, :], in0=bk[0:R, 0:384], scalar1=st["rs"][0:R, 0:1], scalar2=None,
                                                                    op0=ALU.mult), r=[bk, st["rs"]], w=[cqn])
                    transposeT(cqn, R, 384, cqnT)
                    qb0, qb1 = bankf(), bankf()
                    mm_tm(cqnT, R, wuq, 3, 0, 512, qb0)
                    mm_tm(cqnT, R, wuq, 3, 512, 256, qb1)
                    qfl = qf[:].rearrange("p a b -> p (a b)")
                    k.op(ACT, lambda e, R=R: e.copy(out=qfl[0:R, 0:512], in_=qb0[0:R, :]), r=[qb0], w=[qf])
                    k.op(ACT, lambda e, R=R: e.copy(out=qfl[0:R, 512:768], in_=qb1[0:R, 0:256]), r=[qb1], w=[qf])
                    if smp:
                        cq8 = rsm[0:R, 0:128].rearrange("p (a b) -> p a b", b=16)
                        sq8 = rsm[0:R, 128:256].rearrange("p (a b) -> p a b", b=16)
                        rb = rsm
                    else:
                        cq8 = rq[t % 2][:, 0:128].rearrange("p (a b) -> p a b", b=16)
                        sq8 = rq[t % 2][:, 128:256].rearrange("p (a b) -> p a b", b=16)
                        rb = rq[t % 2]
                    x1, x2 = qf[0:R, :, 64:80], qf[0:R, :, 80:96]
                    tq = [qtm[0:R, i, :].rearrange("p (a b) -> p a b", b=16) for i in range(4)]
                    k.op(POOL, lambda e: e.tensor_tensor(out=tq[0], in0=x1, in1=cq8, op=ALU.mult), r=[qf, rb], w=[qtm])
                    k.op(POOL, lambda e: e.tensor_tensor(out=tq[1], in0=x2, in1=sq8, op=ALU.mult), r=[qf, rb], w=[qtm])
                    k.op(POOL, lambda e: e.tensor_tensor(out=tq[2], in0=x1, in1=sq8, op=ALU.mult), r=[qf, rb], w=[qtm])
                    k.op(POOL, lambda e: e.tensor_tensor(out=tq[3], in0=x2, in1=cq8, op=ALU.mult), r=[qf, rb], w=[qtm])
                    k.op(DVE, lambda e, R=R: e.tensor_copy(out=qb[0:R, :, 32:96], in_=qf[0:R, :, 0:64]), r=[qf], w=[qb])
                    k.op(DVE, lambda e, R=R: e.tensor_tensor(out=qb[0:R, :, 0:16], in0=tq[0], in1=tq[1], op=ALU.subtract), r=[qtm], w=[qb])
                    k.op(DVE, lambda e, R=R: e.tensor_tensor(out=qb[0:R, :, 16:32], in0=tq[2], in1=tq[3], op=ALU.add), r=[qtm], w=[qb])
                    pb = bankb()
                    pv = pb[:].rearrange("p (a b) -> p a b", b=128)
                    if not smp:
                        for h in range(8):
                            k.op(PE, lambda e, h=h, pb=pb, pv=pv, R=R: e.transpose(pv[0:96, h, 0:R], qb[0:R, h, :], ident[0:R, 0:R]), r=[qb, ident], w=[pb])
                        k.op(DVE, lambda e, pv=pv, t=t: e.tensor_copy(out=qT[0:96, :, t * 128:(t + 1) * 128], in_=pv[0:96, :, :]), r=[pb], w=[qT])
                    else:
                        for h in range(8):
                            k.op(PE, lambda e, h=h, pb=pb, pv=pv, R=R: e.transpose(pv[0:64, h, 0:R], qb[0:R, h, 32:96], ident[0:R, 0:R]), r=[qb, ident], w=[pb])
                        qsT = k.sb(e2, [64, 8, SPC], BF16, "qsT")
                        k.op(DVE, lambda e, pv=pv, R=R: e.tensor_copy(out=qsT[0:64, :, :], in_=pv[0:64, :, 0:R]), r=[pb], w=[qsT])
                        pb2 = bankb()
                        pv2 = pb2[:].rearrange("p (a b) -> p a b", b=128)
                        for h in range(8):
                            k.op(PE, lambda e, h=h, pv2=pv2, pb2=pb2, R=R: e.transpose(pv2[0:32, h, 0:R], qb[0:R, h, 0:32], ident[0:R, 0:R]),
                                 r=[qb, ident], w=[pb2])
                        k.op(DVE, lambda e, pv2=pv2, R=R: e.tensor_copy(out=qrT[:, :, :].rearrange("p s h -> p h s"), in_=pv2[0:32, :, 0:R]),
                             r=[pb2], w=[qrT])
                        for lc in range(2):
                            bk = bankf()
                            for h in range(8):
                                k.op(PE, lambda e, h=h, lc=lc, bk=bk, R=R: e.matmul(bk[:, h * SPC:(h + 1) * SPC], lhsT=wukT[:, h, lc * 128:(lc + 1) * 128],
                                                                                   rhs=qsT[0:64, h, 0:R], start=True, stop=True), r=[wukT, qsT], w=[bk])
                            k.op(DVE, lambda e, lc=lc, bk=bk: e.tensor_copy(out=qlT[:, lc, :, :].rearrange("p s h -> p h s"),
                                                                             in_=bk[:, 0:8 * SPC].rearrange("p (h s) -> p h s", s=SPC)), r=[bk], w=[qlT])
            k.barrier()
            o_all = k.sb(e13, [128, NT, 512], BF16, "o_all")
            eK = ExitStack()
            eK.__enter__()
            ckvnT = k.sb(eK, [128, 2, S], BF16, "ckvnT")
            KT = k.sb(eK, [128, S], BF16, "KT")
            KRB = k.sb(eK, [128, S], BF16, "KRB")
            wukp = k.sb(eK, [128, 2, 8, 96], BF16, "wukp")
            k.op(DVE, lambda e: e.memset(wukp[:].rearrange("p a h d -> p (a h d)"), 0.0), w=[wukp])
            for lc_ in range(2):
                k.op(DVE, lambda e, lc_=lc_: e.tensor_copy(out=wukp[:, lc_, :, 32:96], in_=wuk[:, lc_, :].rearrange("p (h d) -> p h d", d=64)), r=[wuk], w=[wukp])

            with PStack("1") as e1:
                e1.chk()
                stX = mk_small(e1)
                stY = mk_small(e1)
                wkv = k.sb(e1, [128, KC, 288], BF16, "wkv")
                load_w(wkv, w_in, KC, 288, gain_col=0, col0=OFF_CKV)
                xt = [k.sb(e1, [128, D], F32, "xt") for _ in range(3)]
                xnT = [k.sb(e1, [128, 8, 128], BF16, "xnT") for _ in range(2)]
                rk = k.sb(e1, [128, NB, 32], F32, "rk")
                ckb = [k.sb(e1, [128, 352], BF16, "ckb") for _ in range(2)]
                for cb_ in ckb:
                    k.op(DVE, lambda e, cb_=cb_: e.memset(cb_[:, 288:352], 0.0), w=[cb_])
                k.dma(SP, rk[:].rearrange("p n c -> p (n c)"), ropek[:, :], w=[rk])

                def stage_x(nb):
                    x_t = xt[nb % 3]
                    k.dma(SP if nb % 2 else ACT, x_t[:, :], xs[nb, :, :], w=[x_t])
                    normT(stX, x_t, 128, xnT[nb % 2])

                def stage_y(nb):
                    xn = xnT[nb % 2]
                    bk = bankf()
                    mm_tm(xn, 128, wkv, KC, 0, 288, bk)
                    cb = ckb[nb % 2]
                    ckv_post(stY, bk, 128, rk[:, nb, 0:16], rk[:, nb, 16:32], rk, cb[:, 0:256], cb[:, 256:288], cb)
                    pb = bankb()
                    pv = pb[:].rearrange("p (a b) -> p a b", b=128)
                    k.op(PE, lambda e: e.transpose(pv[:, 0, :], cb[:, 0:128], ident[:, :]), r=[cb, ident], w=[pb])
                    k.op(PE, lambda e: e.transpose(pv[:, 1, :], cb[:, 128:256], ident[:, :]), r=[cb, ident], w=[pb])
                    k.op(PE, lambda e: e.transpose(pv[0:96, 2, :], cb[:, 256:352], ident[:, :]), r=[cb, ident], w=[pb])
                    k.op(ACT, lambda e: e.copy(out=ckvnT[:, :, nb * 128:(nb + 1) * 128], in_=pv[:, 0:2, :]), r=[pb], w=[ckvnT])
                    k.op(ACT, lambda e: e.copy(out=KRB[0:96, nb * 128:(nb + 1) * 128], in_=pv[0:96, 2, :]), r=[pb], w=[KRB])

                stage_x(0)
                for nb in range(NB):
                    if nb + 1 < NB:
                        stage_x(nb + 1)
                    stage_y(nb)
            k.barrier()

            with PStack("3") as e3:
                e3.chk()
                mk16 = k.sb(e3, [128, 16, 512], BF16, "mk16")
                for d in range(16):
                    sti[0] += 1
                    s_ = stg[sti[0] % 4]
                    k.dma(SP, s_[:, 0:512], masks[d, :, :], w=[s_])
                    k.op(alt(), lambda e, d=d, s_=s_: e.tensor_copy(out=mk16[:, d, :], in_=s_[:, 0:512]), r=[s_], w=[mk16])
                V1 = k.sb(e3, [128, NB, 65], BF16, "V1")
                k.op(POOL, lambda e: e.memset(V1[:, :, 64:65], 1.0), w=[V1])
                pT = [k.sb(e3, [128, 512], BF16, "pT") for _ in range(3)]
                pM = [k.sb(e3, [128, 512], BF16, "pM") for _ in range(2)]
                rec = k.sb(e3, [128, 4], F32, "rec")
                oTs = k.sb(e3, [128, 512], F32, "oTs")
                pti = 0
                for h in range(8):
                    for c0 in range(0, S, 512):
                        bk = bankf()
                        for lc in range(2):
                            k.op(PE, lambda e, lc=lc, bk=bk, c0=c0, h=h: e.matmul(bk[0:96, :], lhsT=wukp[:, lc, h, :],
                                                                                 rhs=ckvnT[:, lc, c0:c0 + 512], start=(lc == 0), stop=(lc == 1)),
                                 r=[wukp, ckvnT], w=[bk])
                        k.op(DVE, lambda e, bk=bk, c0=c0: e.tensor_tensor(out=KT[0:96, c0:c0 + 512], in0=bk[0:96, :], in1=KRB[0:96, c0:c0 + 512], op=ALU.add),
                             r=[bk, KRB], w=[KT])
                    for nb0 in range(0, NB, 8):
                        bk = bankf()
                        nn = min(8, NB - nb0)
                        for i in range(nn):
                            nb = nb0 + i
                            for lc in range(2):
                                k.op(PE, lambda e, lc=lc, bk=bk, nb=nb, i=i, h=h: e.matmul(bk[:, i * 64:(i + 1) * 64], lhsT=ckvnT[:, lc, nb * 128:(nb + 1) * 128],
                                                                                          rhs=wuv[:, lc, h * 64:(h + 1) * 64], start=(lc == 0), stop=(lc == 1)),
                                     r=[wuv, ckvnT], w=[bk])
                        k.op(DVE, lambda e, bk=bk, nb0=nb0, nn=nn: e.tensor_copy(out=V1[:, nb0:nb0 + nn, 0:64],
                                                                                 in_=bk[:, 0:nn * 64].rearrange("p (a b) -> p a b", b=64)), r=[bk], w=[V1])
                    for u in range(NG):
                        nkb = 16 * u + 16
                        oT_ = bankf(pin=True)

                        def qstart(kb, u=u):
                            return 0 if kb < 16 * u else ((kb - 16 * u) // 4) * 128

                        def emit_st(kb, h=h, u=u):
                            sb_ = bankf()
                            q0 = qstart(kb)
                            k.op(PE, lambda e: e.matmul(sb_[:, q0:512], lhsT=KT[0:96, kb * 128:(kb + 1) * 128],
                                                        rhs=qT[0:96, h, u * 512 + q0:(u + 1) * 512], start=True, stop=True),
                                 r=[KT, qT], w=[sb_])
                            return sb_
                        pend = [emit_st(0)]
                        if nkb > 1:
                            pend.append(emit_st(1))
                        for kb in range(nkb):
                            sb_ = pend.pop(0)
                            if kb + 2 < nkb:
                                pend.append(emit_st(kb + 2))
                            pti += 1
                            p_ = pT[pti % 3]
                            q0 = qstart(kb)
                            k.op(ACT, lambda e, sb_=sb_, p_=p_, q0=q0: e.activation(out=p_[:, q0:512], in_=sb_[:, q0:512], func=AF.Exp, scale=SCALE), r=[sb_], w=[p_])
                            if kb >= 16 * u:
                                d = kb - 16 * u
                                pm_ = pM[pti % 2]
                                k.op(DVE if pti % 2 else POOL, lambda e, p_=p_, d=d, pm_=pm_, q0=q0: e.tensor_tensor(out=pm_[:, q0:512], in0=p_[:, q0:512], in1=mk16[:, d, q0:512], op=ALU.mult),
                                     r=[p_, mk16], w=[pm_])
                                p_ = pm_
                            k.op(PE, lambda e, p_=p_, kb=kb, nkb=nkb, q0=q0: e.matmul(oT_[0:65, q0:512], lhsT=V1[:, kb, :], rhs=p_[:, q0:512],
                                                                                     start=(kb == 0), stop=(kb == nkb - 1)), r=[p_, V1], w=[oT_])
                        k.op(ACT, lambda e: e.copy(out=oTs[0:65, :], in_=oT_[0:65, :]), r=[oT_], w=[oTs])
                        unpin(oT_)
                        tb = bankf()
                        for qi in range(4):
                            k.op(PE, lambda e, qi=qi, tb=tb: e.transpose(tb[:, qi * 65:(qi + 1) * 65], oTs[0:65, qi * 128:(qi + 1) * 128], identf[0:65, 0:65]),
                                 r=[oTs, identf], w=[tb])
                        tv = tb[:, 0:260].rearrange("p (a b) -> p a b", b=65)
                        k.op(DVE, lambda e, tv=tv: e.reciprocal(out=rec[:, :], in_=tv[:, :, 64]), r=[tb], w=[rec])
                        for qi in range(4):
                            k.op(DVE, lambda e, qi=qi, tv=tv, u=u, h=h: e.tensor_scalar(out=o_all[:, 4 * u + qi, h * 64:(h + 1) * 64], in0=tv[:, qi, 0:64],
                                                                                       scalar1=rec[:, qi:qi + 1], scalar2=None, op0=ALU.mult),
                                 r=[tb, rec], w=[o_all])
            k.barrier()
            eK.close()

            with PStack("3s") as e3s:
                e3s.chk()
                pti_ = k.sb(e3s, [128, CH * SPC], I32, "pti")
                k.dma(SP, pti_[:], ptT[:, :], w=[pti_])
                NBUF = 3
                G = [k.sb(e3s, [128, 16, 256], F32, "G") for _ in range(NBUF)]
                Gk = [k.sb(e3s, [128, 16, 32], F32, "Gk") for _ in range(NBUF)]
                Gb = [k.sb(e3s, [128, 16, 256], BF16, "Gb") for _ in range(NBUF)]
                Gkb = [k.sb(e3s, [128, 16, 32], BF16, "Gkb") for _ in range(NBUF)]
                cTc = [k.sb(e3s, [128, 4, 2, 512], BF16, "cTc") for _ in range(2)]
                cTk = [k.sb(e3s, [32, 2, 1024], BF16, "cTk") for _ in range(2)]
                pch = [k.sb(e3s, [8, 2048], BF16, "pch") for _ in range(2)]
                ptp = [k.sb(e3s, [128, 16, 8], BF16, "ptp") for _ in range(2)]
                rsx = k.sb(e3s, [8, SPC * CH * 4], F32, "rsx")
                den = k.sb(e3s, [8, 2], F32, "den")
                olb = k.sb(e3s, [8, 256], BF16, "olb")
                OT = k.sb(e3s, [128, 2, 8, SPC], BF16, "OT")
                bdm = k.sb(e3s, [SPC, SPC * 8], F32, "bdm")
                pnw = k.sb(e3s, [SPC, SPC * 8], F32, "pnw")
                PnT = k.sb(e3s, [SPC, SPC, 8], BF16, "PnT")
                k.dma(SP, bdm[:], bdmask[:, :], w=[bdm])
                bk = bankf()
                qlf = [qlT[:, lc, :, :].rearrange("p s h -> p (s h)") for lc in range(2)]
                for lc in range(2):
                    k.op(PE, lambda e, lc=lc, bk=bk: e.matmul(bk[0:SPC, 0:SPC * 8], lhsT=cknT[:, lc, :], rhs=qlf[lc], start=(lc == 0), stop=False),
                         r=[cknT, qlT], w=[bk])
                k.op(PE, lambda e, bk=bk: e.matmul(bk[0:SPC, 0:SPC * 8], lhsT=krnT[:, :], rhs=qrT[:, :, :].rearrange("p s h -> p (s h)"),
                                                   start=False, stop=True), r=[krnT, qrT], w=[bk])
                k.op(ACT, lambda e, bk=bk: e.activation(out=pnw[:, :], in_=bk[0:SPC, 0:SPC * 8], func=AF.Exp, scale=SCALE), r=[bk], w=[pnw])
                k.op(DVE, lambda e: e.tensor_tensor(out=PnT[:, :, :].rearrange("p s h -> p (s h)"), in0=pnw[:, :], in1=bdm[:, :], op=ALU.mult),
                     r=[pnw, bdm], w=[PnT])

                chunks = [(s_, ch_) for s_ in range(SPC) for ch_ in range(CH)]
                NCK = len(chunks)

                def load(c):
                    s_, ch_ = chunks[c]
                    g_, gk_, gb_, gkb_ = G[c % NBUF], Gk[c % NBUF], Gb[c % NBUF], Gkb[c % NBUF]
                    k.dma(POOL, g_[:].rearrange("p a b -> p (a b)"), pool_ckv[:, :], r=[pti_], w=[g_], idx=pti_[:, ch_ * SPC + s_:ch_ * SPC + s_ + 1])
                    k.dma(POOL, gk_[:].rearrange("p a b -> p (a b)"), pool_kr[:, :], r=[pti_], w=[gk_], idx=pti_[:, ch_ * SPC + s_:ch_ * SPC + s_ + 1])
                    Ec = (DVE, ACT, DVE)[c % 3]
                    if Ec is ACT:
                        k.op(ACT, lambda e: e.copy(out=gb_[:], in_=g_[:]), r=[g_], w=[gb_])
                    else:
                        k.op(Ec, lambda e: e.tensor_copy(out=gb_[:], in_=g_[:]), r=[g_], w=[gb_])
                    k.op(DVE, lambda e: e.tensor_copy(out=gkb_[:], in_=gk_[:]), r=[gk_], w=[gkb_])

                tbs = [psb[0], psb[1], psf[4], psf[5]]
                pinned.add(id(psf[4]))
                pinned.add(id(psf[5]))
                tbi = [0]

                def bankb4():
                    tbi[0] += 1
                    b_ = tbs[tbi[0] % 4]
                    full = b_[:] if (b_ is psb[0] or b_ is psb[1]) else b_.t.bitcast(BF16)[:]
                    return b_, full

                def stageA(c):
                    gb_, gkb_, cc, ck = Gb[c % NBUF], Gkb[c % NBUF], cTc[c % 2], cTk[c % 2]
                    for grp in range(4):
                        pb, pf = bankb4()
                        pv = pf.rearrange("p (a b) -> p a b", b=128)
                        for lc in range(2):
                            for i in range(4):
                                tt = grp * 4 + i
                                k.op(PE, lambda e, lc=lc, i=i, tt=tt, pv=pv: e.transpose(pv[:, lc * 4 + i, :], gb_[:, tt, lc * 128:(lc + 1) * 128], ident[:, :]),
                                     r=[gb_, ident], w=[pb])
                        if grp % 2:
                            k.op(ACT, lambda e, pf=pf, grp=grp: e.copy(out=cc[:, grp, :, :].rearrange("p a b -> p (a b)"), in_=pf[:, :]), r=[pb], w=[cc])
                        else:
                            k.op(DVE, lambda e, pf=pf, grp=grp: e.tensor_copy(out=cc[:, grp, :, :].rearrange("p a b -> p (a b)"), in_=pf[:, :]), r=[pb], w=[cc])
                    for half in range(2):
                        pb, pf = bankb4()
                        pv = pf.rearrange("p (a b) -> p a b", b=128)
                        for i in range(8):
                            tt = half * 8 + i
                            k.op(PE, lambda e, i=i, tt=tt, pv=pv: e.transpose(pv[0:32, i, :], gkb_[:, tt, :], ident[:, :]), r=[gkb_, ident], w=[pb])
                        if half:
                            k.op(ACT, lambda e, pf=pf, half=half: e.copy(out=ck[:, half, :], in_=pf[0:32, :]), r=[pb], w=[ck])
                        else:
                            k.op(DVE, lambda e, pf=pf, half=half: e.tensor_copy(out=ck[:, half, :], in_=pf[0:32, :]), r=[pb], w=[ck])

                def stageB(c):
                    s_, ch_ = chunks[c]
                    cc, ck, pc = cTc[c % 2], cTk[c % 2], pch[c % 2]
                    for grp in range(4):
                        sb_ = bankf()
                        for lc in range(2):
                            k.op(PE, lambda e, lc=lc, sb_=sb_, grp=grp: e.matmul(sb_[0:8, :], lhsT=qlT[:, lc, s_, :], rhs=cc[:, grp, lc, :],
                                                                                start=(lc == 0), stop=False), r=[qlT, cc], w=[sb_])
                        k.op(PE, lambda e, sb_=sb_, grp=grp: e.matmul(sb_[0:8, :], lhsT=qrT[:, s_, :], rhs=ck[:, grp // 2, (grp % 2) * 512:(grp % 2 + 1) * 512],
                                                                     start=False, stop=True), r=[qrT, ck], w=[sb_])
                        col = (s_ * CH + ch_) * 4 + grp
                        k.op(ACT, lambda e, sb_=sb_, grp=grp, col=col: e.activation(out=pc[0:8, grp * 512:(grp + 1) * 512], in_=sb_[0:8, :], func=AF.Exp,
                                                                                   scale=SCALE, accum_out=rsx[0:8, col:col + 1]), r=[sb_], w=[pc, rsx])

                def stageC(c):
                    pc, pp = pch[c % 2], ptp[c % 2]
                    pb, pf = bankb4()
                    pv8 = pf[:, 0:128].rearrange("p (a b) -> p a b", b=8)
                    for tt in range(16):
                        k.op(PE, lambda e, tt=tt, pv8=pv8: e.transpose(pv8[:, tt, :], pc[0:8, tt * 128:(tt + 1) * 128], ident[0:8, 0:8]),
                             r=[pc, ident], w=[pb])
                    k.op(DVE, lambda e, pv8=pv8: e.tensor_copy(out=pp[:, :, :], in_=pv8[:, :, :]), r=[pb], w=[pp])

                def stageD(c, oacc):
                    s_, ch_ = chunks[c]
                    pp, gb_ = ptp[c % 2], Gb[c % NBUF]
                    for tt in range(16):
                        last = (ch_ == CH - 1 and tt == 15)
                        k.op(PE, lambda e, tt=tt, last=last: e.matmul(oacc[0:8, 0:256], lhsT=pp[:, tt, :], rhs=gb_[:, tt, :],
                                                                     start=False, stop=last), r=[pp, gb_], w=[oacc])

                load(0)
                if NCK > 1:
                    load(1)
                stageA(0)
                oacc = None
                for c in range(NCK):
                    s_, ch_ = chunks[c]
                    if ch_ == 0:
                        oacc = bankf(pin=True)
                        k.op(PE, lambda e, s_=s_, oacc=oacc: e.matmul(oacc[0:8, 0:257], lhsT=PnT[:, s_, :], rhs=cknb[:, 0:257], start=True, stop=False),
                             r=[PnT, cknb], w=[oacc])
                    if c + 2 < NCK:
                        load(c + 2)
                    stageB(c)
                    if c + 1 < NCK:
                        stageA(c + 1)
                    stageC(c)
                    stageD(c, oacc)
                    if ch_ == CH - 1:
                        s = s_
                        k.op(DVE, lambda e, s=s: e.reduce_sum(out=den[0:8, 1:2], in_=rsx[0:8, s * CH * 4:(s + 1) * CH * 4], axis=AX.X), r=[rsx], w=[den])
                        k.op(DVE, lambda e, oacc=oacc: e.tensor_tensor(out=den[0:8, 1:2], in0=den[0:8, 1:2], in1=oacc[0:8, 256:257], op=ALU.add), r=[den, oacc], w=[den])
                        k.op(DVE, lambda e: e.reciprocal(out=den[0:8, 1:2], in_=den[0:8, 1:2]), r=[den], w=[den])
                        k.op(DVE, lambda e, oacc=oacc: e.tensor_scalar(out=olb[:, :], in0=oacc[0:8, 0:256], scalar1=den[0:8, 1:2], scalar2=None, op0=ALU.mult),
                             r=[oacc, den], w=[olb])
                        pb, pf = bankb4()
                        for lc in range(2):
                            k.op(PE, lambda e, lc=lc, pf=pf: e.transpose(pf[:, lc * 8:(lc + 1) * 8], olb[0:8, lc * 128:(lc + 1) * 128], ident[0:8, 0:8]),
                                 r=[olb, ident], w=[pb])
                        k.op(DVE, lambda e, pf=pf, s=s: e.tensor_copy(out=OT[:, :, :, s], in_=pf[:, 0:16].rearrange("p (a b) -> p a b", b=8)), r=[pb], w=[OT])
                        unpin(oacc)
                bk = bankf()
                for h in range(8):
                    for lc in range(2):
                        k.op(PE, lambda e, h=h, lc=lc, bk=bk: e.matmul(bk[0:SPC, h * 64:(h + 1) * 64], lhsT=OT[:, lc, h, :], rhs=wuv[:, lc, h * 64:(h + 1) * 64],
                                                                       start=(lc == 0), stop=(lc == 1)), r=[OT, wuv], w=[bk])
                k.op(DVE, lambda e, bk=bk: e.tensor_copy(out=o_all[0:SPC, T, :], in_=bk[0:SPC, :]), r=[bk], w=[o_all])
                unpin(psf[4])
                unpin(psf[5])
            k.dma(POOL, OA[:, :, :].rearrange("t p c -> p t c"), o_all[:, :, :], r=[o_all], w=[OA])
            k.barrier()

            e13.close()
            with PStack("4a") as e4:
                e4.chk()
                st = mk_small(e4)
                mk_gpost(e4, st)
                wmo = k.sb(e4, [128, 4, D], BF16, "wmo")
                wmx = k.sb(e4, [128, KC, D], BF16, "wmx")
                wcq = k.sb(e4, [128, KC, D], BF16, "wcq")
                wcoo = k.sb(e4, [128, KC, D], BF16, "wcoo")
                load_w(wmo, w_mla_out, 4, D)
                load_w(wmx, w_mix_out, KC, D)
                load_w(wcq, w_ca_q, KC, D, gain_col=8)
                load_w(wcoo, w_ca_o, KC, D)
                mkT = k.sb(e4, [128, 8, NMEM], BF16, "mkT")
                mv1 = k.sb(e4, [128, 2, 4, 257], BF16, "mv1")
                xt = [k.sb(e4, [128, D], F32, "xt") for _ in range(2)]
                x1 = k.sb(e4, [128, D], F32, "x1")
                x2 = [k.sb(e4, [128, D], F32, "x2") for _ in range(2)]
                t1 = [k.sb(e4, [128, D], F32, "t1") for _ in range(2)]
                sgm = [k.sb(e4, [128, D], F32, "sgm") for _ in range(2)]
                oT = k.sb(e4, [128, 4, 128], BF16, "oT")
                mg = k.sb(e4, [128, D], BF16, "mg")
                mgT = k.sb(e4, [128, 8, 128], BF16, "mgT")
                xn2T = k.sb(e4, [128, 8, 128], BF16, "xn2T")
                qcT = k.sb(e4, [128, 8, 128], BF16, "qcT")
                pcT = k.sb(e4, [128, 8, 128], BF16, "pcT")
                ocb = k.sb(e4, [128, D], BF16, "ocb")
                ocT = k.sb(e4, [128, 8, 128], BF16, "ocT")
                rec4 = k.sb(e4, [128, 4], F32, "rec4")
                with ExitStack() as em:
                    wck = k.sb(em, [128, KC, D], BF16, "wck")
                    wcv = k.sb(em, [128, KC, D], BF16, "wcv")
                    load_w(wck, w_ca_k, KC, D, gain_col=16)
                    load_w(wcv, w_ca_v, KC, D, gain_col=16)
                    mnT = k.sb(em, [128, 8, NMEM], BF16, "mnT")
                    mtmp = k.sb(em, [128, 8, 128], BF16, "mtmp")
                    mo = [k.sb(em, [128, D], F32, "mo") for _ in range(2)]
                    k.op(POOL, lambda e: e.memset(mv1[:, :, :, 256:257], 1.0), w=[mv1])
                    for mb in range(2):
                        k.dma(SP, xt[mb][:, :], memp[mb * 128:(mb + 1) * 128, :], w=[xt[mb]])
                        normT(st, xt[mb], 128, mtmp)
                        k.op(POOL, lambda e, mb=mb: e.tensor_copy(out=mnT[:, :, mb * 128:(mb + 1) * 128], in_=mtmp[:, :, :]), r=[mtmp], w=[mnT])
                        for wi, (wt, dst) in enumerate(((wck, mk_p), (wcv, mv_p))):
                            mob = mo[wi]
                            for hh in range(2):
                                bk = bankf()
                                mm_tm(mtmp, 128, wt, KC, hh * 512, 512, bk)
                                k.op(ACT, lambda e, bk=bk, hh=hh, mob=mob: e.copy(out=mob[:, hh * 512:(hh + 1) * 512], in_=bk[:, :]), r=[bk], w=[mob])
                                if wi == 1:
                                    k.op(DVE, lambda e, bk=bk, hh=hh, mb=mb: e.tensor_copy(out=mv1[:, mb, 2 * hh:2 * hh + 2, 0:256],
                                                                                          in_=bk[:, :].rearrange("p (a b) -> p a b", b=256)), r=[bk], w=[mv1])
                            out_toks.append(k.dma(POOL, dst[mb * 128:(mb + 1) * 128, :], mob[:, :], r=[mob]))
                    for c8 in range(8):
                        bk = bankf()
                        for c in range(KC):
                            k.op(PE, lambda e, c=c, c8=c8, bk=bk: e.matmul(bk[:, 0:NMEM], lhsT=wck[:, c, c8 * 128:(c8 + 1) * 128], rhs=mnT[:, c, :],
                                                                           start=(c == 0), stop=(c == KC - 1)), r=[wck, mnT], w=[bk])
                        k.op(ACT, lambda e, c8=c8, bk=bk: e.copy(out=mkT[:, c8, :], in_=bk[:, 0:NMEM]), r=[bk], w=[mkT])
                k.barrier()
                ksf = [k.sb(e4, [128, 2, D], F32, "ksf") for _ in range(2)]
                ksb = k.sb(e4, [128, 2, D], BF16, "ksb")
                vsb = k.sb(e4, [128, 2, D], BF16, "vsb")
                ksT = k.sb(e4, [128, 2, 8, 128], BF16, "ksT")
                pcs = k.sb(e4, [128, 2, 4], BF16, "pcs")
                ones = k.sb(e4, [128, 1], BF16, "ones")
                hm = k.sb(e4, [4, D], F32, "hm")
                ovs = k.sb(e4, [4, D], F32, "ovs")
                ocs = k.sb(e4, [4, 256], F32, "ocs")
                ocsb = k.sb(e4, [4, 256], BF16, "ocsb")
                rs4 = k.sb(e4, [4, 1], F32, "rs4")
                k.op(POOL, lambda e: e.memset(ones[:], 1.0), w=[ones])
                k.dma(SP, hm[:], hmask[:, :], w=[hm])

                for t in range(NT):
                    smp = (t == T)
                    R = SPC if smp else 128
                    x_t = xt[t % 2]
                    if smp:
                        k.dma(SP, x_t[0:R, :], xsm[:, :], w=[x_t])
                    else:
                        k.dma(SP, x_t[:, :], xq[t, :, :], w=[x_t])
                    t1b, sgb = t1[t % 2], sgm[t % 2]
                    k.dma(SP, t1b[0:R, :], T1[t, 0:R, :], r=[T1], w=[t1b])
                    k.dma(SP, sgb[0:R, :], SG[t, 0:R, :], r=[SG], w=[sgb])
                    osrc = k.sb(e4, [128, 512], BF16, "osrc") if t == 0 else osrc
                    k.dma(SP, osrc[0:R, :], OA[t, 0:R, :], r=[OA], w=[osrc])
                    transposeT(osrc, R, 512, oT)
                    ymb = [bankf(), bankf()]
                    for hh in range(2):
                        mm_tm(oT, R, wmo, 4, hh * 512, 512, ymb[hh])
                    for hh in range(2):
                        sl = slice(hh * 512, (hh + 1) * 512)
                        k.op(DVE, lambda e, hh=hh, sl=sl, R=R, sgb=sgb: e.tensor_tensor(out=sgb[0:R, sl], in0=sgb[0:R, sl], in1=ymb[hh][0:R, :], op=ALU.mult),
                             r=[sgb, ymb[hh]], w=[sgb])
                        k.op(POOL, lambda e, sl=sl, R=R, sgb=sgb, t1b=t1b: e.tensor_tensor(out=mg[0:R, sl], in0=sgb[0:R, sl], in1=t1b[0:R, sl], op=ALU.add),
                             r=[sgb, t1b], w=[mg])
                    transposeT(mg, R, D, mgT)
                    mxb = [bankf(), bankf()]
                    for hh in range(2):
                        mm_tm(mgT, R, wmx, KC, hh * 512, 512, mxb[hh])
                    postnorm_res(st, mxb, R, 0, x_t, x1)
                    normT(st, x1, R, xn2T)
                    for half in range(2):
                        bk2 = [bankf(), bankf()]
                        for i in range(4):
                            c8 = half * 4 + i
                            bk = bk2[i // 2] if R == 128 else bk2[0]
                            oc = (i % 2) * 128 if R == 128 else i * R
                            for c in range(KC):
                                k.op(PE, lambda e, c=c, c8=c8, bk=bk, oc=oc, R=R: e.matmul(bk[:, oc:oc + R], lhsT=wcq[:, c, c8 * 128:(c8 + 1) * 128], rhs=xn2T[:, c, 0:R],
                                                                                          start=(c == 0), stop=(c == KC - 1)), r=[wcq, xn2T], w=[bk])
                        if R == 128:
                            for i2 in range(2):
                                k.op(ACT, lambda e, i2=i2, half=half, bk2=bk2: e.copy(out=qcT[:, half * 4 + 2 * i2:half * 4 + 2 * i2 + 2, :],
                                                                                      in_=bk2[i2][:, 0:256].rearrange("p (a b) -> p a b", b=128)), r=[bk2[i2]], w=[qcT])
                        else:
                            k.op(ACT, lambda e, half=half, bk2=bk2, R=R: e.copy(out=qcT[:, half * 4:half * 4 + 4, 0:R],
                                                                                in_=bk2[0][:, 0:4 * R].rearrange("p (a b) -> p a b", b=R)), r=[bk2[0]], w=[qcT])
                    if not smp:
                        for hp in range(2):
                            sb_ = bankf()
                            for i in range(4):
                                h, mb = hp * 2 + i // 2, i % 2
                                for ec in range(2):
                                    k.op(PE, lambda e, h=h, mb=mb, ec=ec, i=i, sb_=sb_: e.matmul(sb_[:, i * 128:(i + 1) * 128], lhsT=mkT[:, h * 2 + ec, mb * 128:(mb + 1) * 128],
                                                                                                 rhs=qcT[:, h * 2 + ec, :], start=(ec == 0), stop=(ec == 1)),
                                         r=[mkT, qcT], w=[sb_])
                            k.op(ACT, lambda e, hp=hp, sb_=sb_: e.activation(out=pcT[:, hp * 4:(hp + 1) * 4, :], in_=sb_[:, :].rearrange("p (a b) -> p a b", b=128),
                                                                             func=AF.Exp, scale=1.0 / 16.0), r=[sb_], w=[pcT])
                        for h in range(4):
                            ob = bankf()
                            for mb in range(2):
                                k.op(PE, lambda e, h=h, mb=mb, ob=ob: e.matmul(ob[:, 0:257], lhsT=pcT[:, h * 2 + mb, :], rhs=mv1[:, mb, h, :],
                                                                               start=(mb == 0), stop=(mb == 1)), r=[pcT, mv1], w=[ob])
                            k.op(DVE, lambda e, h=h, ob=ob: e.reciprocal(out=rec4[:, h:h + 1], in_=ob[:, 256:257]), r=[ob], w=[rec4])
                            k.op(DVE, lambda e, h=h, ob=ob: e.tensor_scalar(out=ocb[:, h * 256:(h + 1) * 256], in0=ob[:, 0:256], scalar1=rec4[:, h:h + 1], scalar2=None,
                                                                            op0=ALU.mult), r=[ob, rec4], w=[ocb])
                        transposeT(ocb, 128, D, ocT)
                    else:
                        for s in range(SPC):
                            kf, vf = ksf[0], ksf[1]
                            k.dma(SP, kf[:], cmk[s, :, :].rearrange("(a p) d -> p a d", p=128), w=[kf])
                            k.dma(SP, vf[:], cmv[s, :, :].rearrange("(a p) d -> p a d", p=128), w=[vf])
                            k.op(POOL, lambda e, kf=kf: e.tensor_copy(out=ksb[:], in_=kf[:]), r=[kf], w=[ksb])
                            k.op(DVE, lambda e, vf=vf: e.tensor_copy(out=vsb[:], in_=vf[:]), r=[vf], w=[vsb])
                            for mb in range(2):
                                pb = bankb()
                                pv = pb[:].rearrange("p (a b) -> p a b", b=128)
                                for c8 in range(8):
                                    k.op(PE, lambda e, c8=c8, mb=mb, pv=pv, pb=pb: e.transpose(pv[:, c8, :], ksb[:, mb, c8 * 128:(c8 + 1) * 128], ident[:, :]),
                                         r=[ksb, ident], w=[pb])
                                k.op(ACT, lambda e, mb=mb, pv=pv: e.copy(out=ksT[:, mb, :, :], in_=pv[:, :, :]), r=[pb], w=[ksT])
                            sb_ = bankf()
                            for mb in range(2):
                                for h in range(4):
                                    for ec in range(2):
                                        k.op(PE, lambda e, mb=mb, h=h, ec=ec, s=s, sb_=sb_: e.matmul(sb_[:, mb * 4 + h:mb * 4 + h + 1], lhsT=ksT[:, mb, h * 2 + ec, :],
                                                                                                     rhs=qcT[:, h * 2 + ec, s:s + 1], start=(ec == 0), stop=(ec == 1)),
                                             r=[ksT, qcT], w=[sb_])
                            k.op(ACT, lambda e, sb_=sb_: e.activation(out=pcs[:, :, :].rearrange("p a b -> p (a b)"), in_=sb_[:, 0:8], func=AF.Exp, scale=1.0 / 16.0),
                                 r=[sb_], w=[pcs])
                            ovb = [bankf(), bankf()]
                            for hh in range(2):
                                for mb in range(2):
                                    k.op(PE, lambda e, hh=hh, mb=mb, ovb=ovb: e.matmul(ovb[hh][0:4, :], lhsT=pcs[:, mb, :], rhs=vsb[:, mb, hh * 512:(hh + 1) * 512],
                                                                                       start=(mb == 0), stop=(mb == 1)), r=[pcs, vsb], w=[ovb[hh]])
                            rb_ = bankf()
                            for mb in range(2):
                                k.op(PE, lambda e, mb=mb, rb_=rb_: e.matmul(rb_[0:4, 0:1], lhsT=pcs[:, mb, :], rhs=ones[:, 0:1], start=(mb == 0), stop=(mb == 1)),
                                     r=[pcs, ones], w=[rb_])
                            for hh in range(2):
                                k.op(DVE, lambda e, hh=hh, ovb=ovb: e.tensor_tensor(out=ovs[:, hh * 512:(hh + 1) * 512], in0=ovb[hh][0:4, :], in1=hm[:, hh * 512:(hh + 1) * 512],
                                                                                   op=ALU.mult), r=[ovb[hh], hm], w=[ovs])
                            k.op(DVE, lambda e: e.tensor_reduce(out=ocs[:, :], in_=ovs[:, :].rearrange("p (h e) -> p e h", h=4), axis=AX.X, op=ALU.add), r=[ovs], w=[ocs])
                            k.op(DVE, lambda e, rb_=rb_: e.reciprocal(out=rs4[:, :], in_=rb_[0:4, 0:1]), r=[rb_], w=[rs4])
                            k.op(DVE, lambda e: e.tensor_scalar(out=ocsb[:, :], in0=ocs[:, :], scalar1=rs4[:, 0:1], scalar2=None, op0=ALU.mult), r=[ocs, rs4], w=[ocsb])
                            pb = bankb()
                            for ec in range(2):
                                k.op(PE, lambda e, ec=ec, pb=pb: e.transpose(pb[:, ec * 4:(ec + 1) * 4], ocsb[0:4, ec * 128:(ec + 1) * 128], ident[0:4, 0:4]),
                                     r=[ocsb, ident], w=[pb])
                            k.op(DVE, lambda e, pb=pb, s=s: e.tensor_copy(out=ocT[:, :, s].rearrange("p (h e) -> p e h", e=2),
                                                                          in_=pb[:, 0:8].rearrange("p (e h) -> p e h", h=4)), r=[pb], w=[ocT])
                    cab = [bankf(), bankf()]
                    for hh in range(2):
                        mm_tm(ocT, R, wcoo, KC, hh * 512, 512, cab[hh])
                    x2b = x2[t % 2]
                    postnorm_res(st, cab, R, D, x1, x2b)
                    if os.environ.get("K_DBG", "") == "x1":
                        k.dma(POOL, X2[t, 0:R, :], x1[0:R, :], r=[x1], w=[X2])
                    else:
                        k.dma(POOL, X2[t, 0:R, :], x2b[0:R, :], r=[x2b], w=[X2])
            k.barrier()

        with PStack("4b") as e5:
            e5.chk()
            st = mk_small(e5)
            mk_gpost(e5, st, 2 * D, 3 * D)
            wup = k.sb(e5, [128, KC, 4096], BF16, "wup")
            wdn = k.sb(e5, [128, 32, D], BF16, "wdn")
            load_w(wup, w_ff_up, KC, 4096, gain_col=24)
            load_w(wdn, w_ff_down, 32, D)
            x2 = [k.sb(e5, [128, D], F32, "x2") for _ in range(2)]
            yo = k.sb(e5, [128, D], F32, "yo")
            xn3T = k.sb(e5, [128, 8, 256], BF16, "xn3T")
            hr = [k.sb(e5, [128, 512], F32, "hr") for _ in range(2)]
            hT = k.sb(e5, [128, 32, 256], BF16, "hT")
            groups = [[(t_, 128) for t_ in range(g_, g_ + 2)] for g_ in range(0, T, 2)] + [[(T, SPC)]]
            hri = 0
            for grp in groups:
                W = sum(R_ for _, R_ in grp)
                offs = []
                o_ = 0
                for gi, (t, R) in enumerate(grp):
                    offs.append(o_)
                    k.dma(SP, x2[gi][0:R, :], X2[t, 0:R, :], r=[X2], w=[x2[gi]])
                    normT(st, x2[gi], R, xn3T, c0=o_)
                    o_ += R
                per = min(4, 512 // W)
                for f0 in range(0, 32, per):
                    bk = bankf()
                    for i in range(per):
                        fc = f0 + i
                        for c in range(KC):
                            k.op(PE, lambda e, c=c, fc=fc, i=i, bk=bk, W=W: e.matmul(bk[:, i * W:(i + 1) * W], lhsT=wup[:, c, fc * 128:(fc + 1) * 128], rhs=xn3T[:, c, 0:W],
                                                                                    start=(c == 0), stop=(c == KC - 1)), r=[wup, xn3T], w=[bk])
                    hri += 1
                    hrb = hr[hri % 2]
                    k.op(ACT, lambda e, bk=bk, hrb=hrb, W=W, per=per: e.activation(out=hrb[:, 0:per * W], in_=bk[:, 0:per * W], func=AF.Relu), r=[bk], w=[hrb])
                    k.op(POOL if hri % 2 else DVE, lambda e, hrb=hrb, f0=f0, W=W, per=per: e.tensor_tensor(
                        out=hT[:, f0:f0 + per, 0:W], in0=hrb[:, 0:per * W].rearrange("p (a b) -> p a b", b=W),
                        in1=hrb[:, 0:per * W].rearrange("p (a b) -> p a b", b=W), op=ALU.mult), r=[hrb], w=[hT])
                for gi, (t, R) in enumerate(grp):
                    o0 = offs[gi]
                    fb = [bankf(), bankf()]
                    for hh in range(2):
                        for c in range(32):
                            k.op(PE, lambda e, c=c, hh=hh, o0=o0, R=R, fb=fb: e.matmul(fb[hh][0:R, 0:512], lhsT=hT[:, c, o0:o0 + R], rhs=wdn[:, c, hh * 512:(hh + 1) * 512],
                                                                                      start=(c == 0), stop=(c == 31)), r=[hT, wdn], w=[fb[hh]])
                    postnorm_res(st, fb, R, 0, x2[gi], yo)
                    if t == T:
                        out_toks.append(k.dma(POOL, y_s[:, :], yo[0:R, :], r=[yo]))
                    else:
                        out_toks.append(k.dma(POOL, y_p[t, :, :], yo[:, :], r=[yo]))
        for tk in out_toks:
            k._wait(SP, tk)
        k.barrier()
    return nc


def _rope_tables(pos):
    inv = 1.0 / (10000.0 ** (np.arange(0, 32, 2, dtype=np.float32) / 32.0))
    ang = pos.astype(np.float32)[:, None] * inv[None, :].astype(np.float32)
    return np.cos(ang).astype(np.float32), np.sin(ang).astype(np.float32)


_CACHE = {}


def kernel(**inp):
    x_prompt = np.asarray(inp["x_prompt"]); x_sample = np.asarray(inp["x_sample"])
    Bp, SEQ, _ = x_prompt.shape
    DEC = x_sample.shape[0]
    NPOOL, PS = inp["cache_ckv"].shape[1], inp["cache_ckv"].shape[2]
    NPG = inp["page_table"].shape[1]
    assert Bp == 2 and NPG == 128 and DEC % 8 == 0
    NB = SEQ // 128
    T = NB // 4
    SPC = DEC // 8
    past_len = NPG * PS
    key = (T, NB, SPC, PS, NPOOL)
    if key not in _CACHE:
        _CACHE[key] = build(*key)
    nc = _CACHE[key]

    f32 = lambda a: np.ascontiguousarray(np.asarray(a, dtype=np.float32))
    cos_all, sin_all = _rope_tables(np.arange(SEQ))
    cos_s, sin_s = _rope_tables(np.array([past_len]))
    pool_ckv = f32(inp["cache_ckv"][0]).reshape(NPOOL * (PS // 16), 4096)
    pool_kr = f32(inp["cache_krope"][0]).reshape(NPOOL * (PS // 16), 512)
    gp = np.zeros((128, 40), np.float32)
    for i, nm in enumerate(("norm_mix_pre_g", "norm_ca_pre_g", "mem_norm_g", "norm_mlp_pre_g")):
        gp[:, i * 8:(i + 1) * 8] = f32(inp[nm][0]).reshape(8, 128).T
    gp[:, 32:35] = f32(inp["q_norm_g"][0]).reshape(3, 128).T
    gbc = np.concatenate([f32(inp["norm_mix_post_g"][0]), f32(inp["norm_ca_post_g"][0]), f32(inp["norm_mlp_post_g"][0]),
                          f32(inp["kv_norm_g"][0])])[None, :].repeat(128, 0)
    conv_wp = np.ascontiguousarray(f32(inp["conv_w"][0]).reshape(3, 4, 128).transpose(2, 1, 0).reshape(128, 12))
    bd = np.zeros((SPC, SPC, 8), np.float32)
    for s in range(SPC):
        bd[s, s, :] = 1.0
    hm = np.zeros((4, 4, 256), np.float32)
    for h in range(4):
        hm[h, h, :] = 1.0
    shared = {
        "pool_ckv": pool_ckv, "pool_kr": pool_kr,
        "ropek": np.ascontiguousarray(np.concatenate([cos_all, sin_all], 1).reshape(NB, 128, 32).transpose(1, 0, 2).reshape(128, NB * 32)),
        "bdmask": bd.reshape(SPC, SPC * 8), "hmask": hm.reshape(4, 1024),
        "w_in": f32(inp["w_in"][0]), "conv_wp": conv_wp, "w_conv_out": f32(inp["w_conv_out"][0]),
        "w_uq": f32(inp["w_uq"][0]).reshape(384, 768), "w_uk": f32(inp["w_uk"][0]).reshape(256, 512),
        "w_uv": f32(inp["w_uv"][0]).reshape(256, 512), "w_mla_out": f32(inp["w_mla_out"][0]),
        "w_mix_out": f32(inp["w_mix_out"][0]), "w_ca_q": f32(inp["w_ca_q"][0]).reshape(D, D),
        "w_ca_k": f32(inp["w_ca_k"][0]).reshape(D, D), "w_ca_v": f32(inp["w_ca_v"][0]).reshape(D, D),
        "w_ca_o": f32(inp["w_ca_o"][0]).reshape(D, D), "w_ff_up": f32(inp["w_ff_up"][0]),
        "w_ff_down": f32(inp["w_ff_down"][0]), "gpart": gp, "gbc": np.ascontiguousarray(gbc),
    }
    ropes = np.concatenate([np.tile(cos_s, (1, 8)), np.tile(sin_s, (1, 8)), cos_s, sin_s], 1).repeat(SPC, 0)
    in_maps = []
    kk = np.arange(128)[:, None]
    qq = np.arange(128)[None, :]
    tri = (kk <= qq).astype(np.float32)
    for c in range(8):
        b, j = c // 4, c % 4
        xb = f32(x_prompt[b]).reshape(NB, 128, D)
        blocks = [4 * t + j for t in range(T)]
        xq = np.ascontiguousarray(xb[blocks])
        xh = np.zeros((2 * T, D), np.float32)
        for t, g in enumerate(blocks):
            if g > 0:
                xh[2 * t:2 * t + 2] = xb[g - 1, 126:128]
        masks = np.zeros((16, 128, 512), np.float32)
        for d in range(16):
            for qi in range(4):
                lim = 4 * qi + j
                if d < lim:
                    masks[d, :, qi * 128:(qi + 1) * 128] = 1.0
                elif d == lim:
                    masks[d, :, qi * 128:(qi + 1) * 128] = tri
        cq = cos_all.reshape(NB, 128, 16)[blocks]
        sq = sin_all.reshape(NB, 128, 16)[blocks]
        ropeq = np.concatenate([np.tile(cq, (1, 1, 8)), np.tile(sq, (1, 1, 8))], 2)
        sl = slice(c * SPC, (c + 1) * SPC)
        m = dict(shared)
        m.update({
            "xq": xq, "xh": xh, "xs": np.ascontiguousarray(xb), "xsm": f32(x_sample[sl, 0, :]),
            "memp": f32(inp["mem_prompt"][b]), "stc": f32(inp["state_conv"][0, sl]),
            "cmk": f32(inp["cache_mem_k"][0, sl]).reshape(SPC, NMEM, D), "cmv": f32(inp["cache_mem_v"][0, sl]).reshape(SPC, NMEM, D),
            "ptT": np.ascontiguousarray(np.concatenate([np.asarray(inp["page_table"])[sl].T.astype(np.int32) * (PS // 16) + ch_ for ch_ in range(PS // 16)], 1)),
            "masks": masks, "ropeq": np.ascontiguousarray(ropeq.astype(np.float32)), "ropes": np.ascontiguousarray(ropes.astype(np.float32)),
        })
        in_maps.append(m)
    res = run_bass_kernel_spmd(nc, in_maps, core_ids=list(range(8))).results

    y_prompt = np.zeros((2, SEQ, D), np.float32)
    ckv_prompt = np.zeros((1, 2, SEQ, 256), np.float32)
    kr_prompt = np.zeros((1, 2, SEQ, 32), np.float32)
    for c in range(8):
        b, j = c // 4, c % 4
        for t in range(T):
            g = 4 * t + j
            y_prompt[b, g * 128:(g + 1) * 128] = res[c]["y_p"][t]
            ckv_prompt[0, b, g * 128:(g + 1) * 128] = res[c]["ckv_p"][t]
            kr_prompt[0, b, g * 128:(g + 1) * 128] = res[c]["kr_p"][t]
    y_sample = np.concatenate([res[c]["y_s"] for c in range(8)], 0).reshape(DEC, 1, D)
    conv_prompt = np.stack([res[3]["conv_p"], res[7]["conv_p"]], 0)[None]
    mem_k = np.stack([res[0]["mk_p"], res[4]["mk_p"]], 0).reshape(1, 2, NMEM, 4, 256)
    mem_v = np.stack([res[0]["mv_p"], res[4]["mv_p"]], 0).reshape(1, 2, NMEM, 4, 256)
    ckv_sample = np.concatenate([res[c]["ckv_s"] for c in range(8)], 0).reshape(1, DEC, 1, 256)
    kr_sample = np.concatenate([res[c]["kr_s"] for c in range(8)], 0).reshape(1, DEC, 1, 32)
    conv_sample = np.concatenate([res[c]["conv_s"] for c in range(8)], 0).reshape(1, DEC, 2, 512)
    return (y_prompt, y_sample, ckv_prompt, kr_prompt, conv_prompt.astype(np.float32), mem_k, mem_v,
            ckv_sample, kr_sample, conv_sample)
```
